# Optimizing a Trainium2 kernel written in Bass

```python
import math
import jax, jax.numpy as jnp
from jax import lax
import numpy as np

D_MODEL = 1024
BATCH = 8
SEQ = 4096
DEPTH = 4
DEC_BATCH = 16
DEC_SEQ = 4096
PAST_LEN = 128

D_ATT = D_MODEL // 2
N_ATT_HEADS = 4
ATT_HEAD_DIM = D_ATT // N_ATT_HEADS
QK_DIM = ATT_HEAD_DIM // 2
D_GMLP = D_MODEL - D_ATT
N_GMLP_GROUPS = 4
GMLP_GROUP_DIM = D_GMLP // N_GMLP_GROUPS
CHUNK = 128
Q_BLOCK = 128
EPS = 1e-6
D_IN = 4 * D_ATT + 3 * D_GMLP
SPLITS = (D_ATT, 2 * D_ATT, 3 * D_ATT, 4 * D_ATT,
          4 * D_ATT + D_GMLP, 4 * D_ATT + 2 * D_GMLP)

kernel_name = "hymba_diffattn_gmlp_encoder"


def rmsnorm(x, g):
    xf = x.astype(jnp.float32)
    ms = jnp.mean(xf * xf, axis=-1, keepdims=True)
    return (xf * lax.rsqrt(ms + EPS) * g.astype(jnp.float32)).astype(x.dtype)


def alibi_slopes(n):
    return jnp.asarray(np.array([2.0 ** (-8.0 * (i + 1) / n) for i in range(n)], dtype=np.float32))


def lambda_init_fn(layer_idx):
    return 0.8 - 0.6 * math.exp(-0.3 * layer_idx)


def diff_attention(q, k, v, lam, lam_init, sub_g):
    B, S = q.shape[0], q.shape[1]
    nblk = S // Q_BLOCK
    scale = QK_DIM ** -0.5
    slopes = alibi_slopes(N_ATT_HEADS)
    pos = jnp.arange(S)
    qb = q.reshape(B, nblk, Q_BLOCK, N_ATT_HEADS, 2, QK_DIM).transpose(1, 0, 2, 3, 4, 5)

    def one_block(args):
        q_blk, blk = args
        t = blk * Q_BLOCK + jnp.arange(Q_BLOCK)
        dist = jnp.abs(t[:, None] - pos[None, :]).astype(jnp.float32)
        bias = -slopes[:, None, None] * dist
        s = jnp.einsum('bqhcd,bshcd->bhcqs', q_blk, k).astype(jnp.float32) * scale
        p = jax.nn.softmax(s + bias[None, :, None], axis=-1)
        a = p[:, :, 0] - lam * p[:, :, 1]
        return jnp.einsum('bhqs,bshe->bqhe', a.astype(v.dtype), v)

    o = lax.map(one_block, (qb, jnp.arange(nblk)))
    o = o.transpose(1, 0, 2, 3, 4).reshape(B, S, N_ATT_HEADS, ATT_HEAD_DIM)
    o = rmsnorm(o, sub_g) * (1.0 - lam_init)
    return o.reshape(B, S, D_ATT)


def spatial_gating(u, vg, vnorm_g, w_s, b_s):
    B, S = u.shape[0], u.shape[1]
    nc = S // CHUNK
    vn = rmsnorm(vg, vnorm_g).reshape(B, nc, CHUNK, N_GMLP_GROUPS, GMLP_GROUP_DIM)
    sv = jnp.einsum('gts,bcsgd->bctgd', w_s, vn) + b_s.T[None, None, :, :, None]
    return u * sv.reshape(B, S, D_GMLP)


def hybrid_layer(x, l, norm_g, w_in, lambda_qk, subln_g, vnorm_g, w_s, b_s, w_out):
    B, S = x.shape[0], x.shape[1]
    h = rmsnorm(x, norm_g[l])
    z = jnp.einsum('bsd,de->bse', h, w_in[l])
    zq, zk, zv, g_att, u, vg, g_gm = jnp.split(z, SPLITS, axis=-1)
    q = zq.reshape(B, S, N_ATT_HEADS, 2, QK_DIM)
    k = zk.reshape(B, S, N_ATT_HEADS, 2, QK_DIM)
    v = zv.reshape(B, S, N_ATT_HEADS, ATT_HEAD_DIM)
    lam_init = lambda_init_fn(l)
    lq = lambda_qk[l].astype(jnp.float32)
    lam = jnp.exp(jnp.sum(lq[0] * lq[1])) - jnp.exp(jnp.sum(lq[2] * lq[3])) + lam_init
    att = diff_attention(q, k, v, lam, lam_init, subln_g[l]) * jax.nn.silu(g_att)
    sgu = spatial_gating(u, vg, vnorm_g[l], w_s[l], b_s[l]) * jax.nn.silu(g_gm)
    y = jnp.einsum('bse,ed->bsd', jnp.concatenate([att, sgu], axis=-1), w_out[l])
    return x + y


def trunk(x, norm_g, w_in, lambda_qk, subln_g, vnorm_g, w_s, b_s, w_out, final_g):
    for l in range(DEPTH):
        x = hybrid_layer(x, l, norm_g, w_in, lambda_qk, subln_g, vnorm_g, w_s, b_s, w_out)
    return rmsnorm(x, final_g)


def setup_inputs(seed: int = 0) -> dict:
    key = jax.random.key(seed)
    ks = jax.random.split(key, 12)
    f32 = jnp.float32
    x_prompt = jax.random.normal(ks[0], (BATCH, SEQ, D_MODEL), f32)
    x_sample = jax.random.normal(ks[1], (DEC_BATCH, DEC_SEQ, D_MODEL), f32)
    norm_g = 1.0 + 0.02 * jax.random.normal(ks[2], (DEPTH, D_MODEL), f32)
    w_in = jax.random.normal(ks[3], (DEPTH, D_MODEL, D_IN), f32) * D_MODEL ** -0.5
    lambda_qk = 0.1 * jax.random.normal(ks[4], (DEPTH, 4, QK_DIM), f32)
    subln_g = 1.0 + 0.02 * jax.random.normal(ks[5], (DEPTH, ATT_HEAD_DIM), f32)
    vnorm_g = 1.0 + 0.02 * jax.random.normal(ks[6], (DEPTH, D_GMLP), f32)
    w_s = jax.random.normal(ks[7], (DEPTH, N_GMLP_GROUPS, CHUNK, CHUNK), f32) * CHUNK ** -0.5
    b_s = 1.0 + 0.1 * jax.random.normal(ks[8], (DEPTH, N_GMLP_GROUPS, CHUNK), f32)
    w_out = jax.random.normal(ks[9], (DEPTH, D_MODEL, D_MODEL), f32) * D_MODEL ** -0.5
    final_g = 1.0 + 0.02 * jax.random.normal(ks[10], (D_MODEL,), f32)
    return {"x_prompt": x_prompt, "x_sample": x_sample, "norm_g": norm_g, "w_in": w_in,
            "lambda_qk": lambda_qk, "subln_g": subln_g, "vnorm_g": vnorm_g, "w_s": w_s,
            "b_s": b_s, "w_out": w_out, "final_g": final_g}


def reference(x_prompt, x_sample, norm_g, w_in, lambda_qk, subln_g, vnorm_g, w_s, b_s, w_out, final_g):
    y_prompt = trunk(x_prompt, norm_g, w_in, lambda_qk, subln_g, vnorm_g, w_s, b_s, w_out, final_g)
    y_sample = trunk(x_sample, norm_g, w_in, lambda_qk, subln_g, vnorm_g, w_s, b_s, w_out, final_g)
    return (y_prompt, y_sample)
```

```python
import math
from contextlib import ExitStack

import numpy as np
import concourse.bass as bass
import concourse.mybir as mybir
from concourse.bass_utils import run_bass_kernel_spmd

F32 = mybir.dt.float32
BF16 = mybir.dt.bfloat16
ALU = mybir.AluOpType
AF = mybir.ActivationFunctionType
AX = mybir.AxisListType

D = 1024
DIN = 3584
NH = 4
EPS = 1e-6
QB = 512
SLOPES = [2.0 ** (-2.0 * (i + 1)) for i in range(NH)]
THRESH = 60.0
KEEP = [int(math.floor((THRESH / m - 1.0) / 128.0 - 1e-9)) + 1 for m in SLOPES]
HEAD_ORDER = [3, 2, 1, 0]


def lam_init_fn(l):
    return 0.8 - 0.6 * math.exp(-0.3 * l)


class Prog:
    EPOCH = 12000

    def __init__(self):
        self.ins = []

    def op(self, eng, fn, reads=(), writes=(), dma=None):
        self.ins.append([eng, fn, tuple(reads), tuple(writes), dma, None, False])

    def build(self, nc, stack, final_wait_prefix=("st_",)):
        ins = self.ins
        n = len(ins)
        last_w = {}
        rd_eng = {}
        rd_dma = {}
        deps_all = [None] * n
        for i in range(n):
            eng, fn, reads, writes, dma, _, _ = ins[i]
            deps = {}
            for k in reads:
                j = last_w.get(k)
                if j is not None:
                    deps[j] = True
            for k in writes:
                j = last_w.get(k)
                if j is not None and j not in deps:
                    deps[j] = True
                for j in rd_eng.get(k, {}).values():
                    if j != i and j not in deps:
                        deps[j] = False
                for j in rd_dma.get(k, ()):
                    if j not in deps:
                        deps[j] = False
            for k in reads:
                if dma is not None:
                    rd_dma.setdefault(k, []).append(i)
                else:
                    rd_eng.setdefault(k, {})[eng] = i
            for k in writes:
                last_w[k] = i
                rd_eng[k] = {}
                rd_dma[k] = []
            need = []
            for j, raw in deps.items():
                ej, dj = ins[j][0], ins[j][4]
                if dj is not None:
                    need.append(j)
                elif ej == eng and dma is None:
                    if raw and eng != "pe":
                        need.append(j)
                elif ej == eng and dma is not None:
                    need.append(j)
                else:
                    need.append(j)
            deps_all[i] = need
            for j in need:
                ins[j][6] = True
        eng_cnt = {}
        dma_cnt = {}
        ticket = [None] * n
        dma_before = [None] * n
        sem_names = {}
        for i in range(n):
            eng, fn, reads, writes, dma, _, needs = ins[i]
            dma_before[i] = dict(dma_cnt) if False else None
            if dma is not None:
                dma_cnt[dma] = dma_cnt.get(dma, 0) + 16
                ticket[i] = (("d", dma), dma_cnt[dma])
                sem_names[("d", dma)] = True
            elif needs:
                c = eng_cnt.get(eng, 0) + 1
                eng_cnt[eng] = c
                ep = (c - 1) // self.EPOCH
                ticket[i] = (("e", eng, ep), c - ep * self.EPOCH)
                sem_names[("e", eng, ep)] = True
        sems = {}
        for k in sem_names:
            nm = "s_" + "_".join(str(x) for x in k)
            sems[k] = stack.enter_context(nc.semaphore(nm))
        dma_run = {}
        dma_seen_at = [None] * n
        for i in range(n):
            snap = {}
            for j in deps_all[i]:
                dj = ins[j][4]
                if dj is not None:
                    snap[dj] = dma_run.get(dj, 0)
            dma_seen_at[i] = snap
            if ins[i][4] is not None:
                dma_run[ins[i][4]] = dma_run.get(ins[i][4], 0) + 16
        final = {k: v for k, v in dma_run.items() if k.startswith(final_wait_prefix)}
        per_eng = {}
        for i in range(n):
            per_eng.setdefault(ins[i][0], []).append(i)

        def run_engine(ename, e):
            waited = {}
            for i in per_eng.get(ename, ()):
                _, fn, _, _, dma, _, needs = ins[i]
                wl = {}
                for j in deps_all[i]:
                    sk, val = ticket[j]
                    if sk[0] == "d":
                        val = dma_seen_at[i][sk[1]]
                    if val > wl.get(sk, 0):
                        wl[sk] = val
                for sk, val in wl.items():
                    if waited.get(sk, 0) < val:
                        e.wait_ge(sems[sk], val)
                        waited[sk] = val
                r = fn(e)
                if dma is not None:
                    r.then_inc(sems[("d", dma)], 16)
                elif needs:
                    r.then_inc(sems[ticket[i][0]], 1)
            if ename == "sp":
                for k, v in final.items():
                    e.wait_ge(sems[("d", k)], v)

        with nc.Block() as block:
            @block.tensor
            def _(e):
                run_engine("pe", e)

            @block.scalar
            def _(e):
                run_engine("act", e)

            @block.vector
            def _(e):
                run_engine("dve", e)

            @block.gpsimd
            def _(e):
                run_engine("pool", e)

            @block.sync
            def _(e):
                run_engine("sp", e)
        return {e: len(v) for e, v in per_eng.items()}


def build_nc(NSEQ, S, L, last_is_final=True):
    NT = S // 128
    NCH = S // QB
    NKB = NT
    NBA = max(NKB - 4, 1)
    NBC = 2 * NBA + 1
    nc = bass.Bass("TRN2", target_bir_lowering=False)
    dt_in = lambda name, shape: nc.dram_tensor(name, list(shape), F32, kind="ExternalInput").ap()
    x_d = dt_in("x", (NSEQ, S, D))
    win_d = dt_in("w_in", (L, D, DIN))
    wout_d = dt_in("w_out", (L, D, D))
    ng_d = dt_in("ng", (L, 128, 8))
    lq_d = dt_in("lq", (L, 128, 256))
    subg_d = dt_in("subg", (L, 128, 512))
    vng_d = dt_in("vng", (L, 128, 512))
    wsT_d = dt_in("wsT", (L, 128, 512))
    bs_d = dt_in("bs", (L, 128, 4))
    fg_d = dt_in("fg", (128, D))
    ident_d = dt_in("identf", (128, 128))
    mident_d = dt_in("midentf", (128, 512))
    bhi_d = dt_in("bhif", (128, 896))
    blo_d = dt_in("blof", (128, 896))
    bcol_d = dt_in("bcol", (128, NH * NBC))
    ftab_d = dt_in("ftab", (128, NH * 16))
    y_d = nc.dram_tensor("y", [NSEQ, S, D], F32, kind="ExternalOutput").ap()
    R_d = nc.dram_tensor("Rres", [NSEQ, S, D], F32).ap()
    hT_d = nc.dram_tensor("hTs", [NSEQ, NCH, 128, 8 * QB], BF16).ap()

    P = Prog()
    with ExitStack() as st:
        sb = lambda name, shape, dt: st.enter_context(nc.sbuf_tensor(name, list(shape), dt))
        KT = sb("KT", (128, NH, S), BF16)
        VP = sb("VP", (128, NKB, NH, 129), BF16)
        Wkv = sb("Wkv", (128, 8, 1024), BF16)
        Wq = sb("Wq", (128, 8, 512), BF16)
        Wg = sb("Wg", (128, 8, 2048), BF16)
        Wo = sb("Wo", (128, 8, 1024), BF16)
        xt = [sb("xt%d" % i, (128, D), F32) for i in range(2)]
        et = [sb("et%d" % i, (128, 1024), BF16) for i in range(2)]
        hT = sb("hT", (128, 8, QB), BF16)
        QT = sb("QT", (128, NH, QB), BF16)
        Oacc = sb("Oacc", (128, 4, 2, 129), F32)
        G = sb("G", (128, 4, 512), F32)
        cat = sb("cat", (128, 4, 1024), BF16)
        tA = sb("tA", (128, 512), F32)
        tB = sb("tB", (128, 512), F32)
        vn = sb("vn", (128, 512), BF16)
        otmp = sb("otmp", (128, 4, 128), F32)
        scr = sb("scr", (128, 1032), F32)
        ident = sb("ident", (128, 128), BF16)
        mident = sb("mident", (128, NH, 128), BF16)
        Bhi = sb("Bhi", (128, 896), BF16)
        Blo = sb("Blo", (128, 896), BF16)
        bcol = sb("bcol_s", (128, NH * NBC), F32)
        ftab = sb("ftab_s", (128, NH * 16), F32)
        ng = sb("ng_s", (128, 8), F32)
        sublnvec = sb("sublnvec", (128, 512), F32)
        vnormg = sb("vnormg", (128, 512), F32)
        wsT = sb("wsT_s", (128, 512), BF16)
        bsb = sb("bs_s", (128, 4), F32)
        fgs = sb("fg_s", (128, D), F32)
        lqs = tA[:, 0:256]
        lprod = tA[:, 256:384]
        lsum = sb("lsum", (128, 2), F32)
        lexp = sb("lexp", (128, 2), F32)
        neglam = sb("neglam", (128, 1), F32)
        mhalf = sb("mhalf", (128, 8), F32)
        epsc = sb("epsc", (128, 8), F32)
        stat = sb("stat", (128, 64), F32)
        ps = [st.enter_context(nc.psum_tensor("ps%d" % i, [128, 2, 512], F32)) for i in range(4)]

        def bank(b):
            return ps[b // 2][:, b % 2, :]

        def bkey(b):
            return "bank%d" % b

        cnt = [0]

        def load_const(dst_ap, src_ap, key, via=None):
            if via is None:
                P.op("sp", lambda e, d=dst_ap, s=src_ap: e.dma_start(out=d, in_=s), reads=(), writes=(key,), dma="ld_c")
            else:
                stg, stg_key, width = via
                P.op("sp", lambda e, d=stg[:, 0:width], s=src_ap: e.dma_start(out=d, in_=s), reads=(), writes=(stg_key,), dma="ld_" + stg_key)
                P.op("dve", lambda e, d=dst_ap, s=stg[:, 0:width]: e.tensor_copy(out=d, in_=s), reads=(stg_key,), writes=(key,))

        load_const(ident[:], ident_d, "ident", via=(xt[0], "xt0", 128))
        load_const(mident[:].rearrange("p h k -> p (h k)"), mident_d, "mident", via=(xt[1], "xt1", 512))
        load_const(Bhi[:], bhi_d, "Bhi", via=(xt[0], "xt0", 896))
        load_const(Blo[:], blo_d, "Blo", via=(xt[1], "xt1", 896))
        load_const(bcol[:], bcol_d, "bcol")
        load_const(ftab[:], ftab_d, "ftab")
        load_const(fgs[:], fg_d, "fgs")
        P.op("dve", lambda e: e.memset(mhalf[:], -0.5), writes=("mhalf",))
        P.op("dve", lambda e: e.memset(epsc[:], float(EPS)), writes=("epsc",))
        P.op("dve", lambda e: e.memset(VP[:].rearrange("p a b c -> p (a b) c")[:, :, 128:129], 1.0), writes=("VPones",))

        rr = [0]

        def next_bank(pool):
            b = pool[rr[0] % len(pool)]
            rr[0] += 1
            return b

        conv_rr = [0]

        def convert(dst_ap, src_ap, scal_ap, rkeys, wkey):
            eng = ("dve", "act")[conv_rr[0] % 2]
            conv_rr[0] += 1
            if scal_ap is None:
                if eng == "act":
                    P.op("act", lambda e: e.copy(out=dst_ap, in_=src_ap), reads=rkeys, writes=(wkey,))
                else:
                    P.op(eng, lambda e: e.tensor_copy(out=dst_ap, in_=src_ap), reads=rkeys, writes=(wkey,))
            else:
                if eng == "act":
                    P.op("act", lambda e: e.activation(out=dst_ap, in_=src_ap, func=AF.Copy, scale=scal_ap), reads=rkeys + ("ng",), writes=(wkey,))
                else:
                    P.op(eng, lambda e: e.tensor_scalar(out=dst_ap, in0=src_ap, scalar1=scal_ap, scalar2=None, op0=ALU.mult), reads=rkeys + ("ng",), writes=(wkey,))

        stage_rr = [0]

        def layer_setup(l):
            li = lam_init_fn(l)
            P.op("sp", lambda e: e.dma_start(out=ng[:], in_=ng_d[l]), writes=("ng",), dma="ld_c")
            P.op("sp", lambda e: e.dma_start(out=sublnvec[:], in_=subg_d[l]), writes=("sublnvec",), dma="ld_c")
            P.op("sp", lambda e: e.dma_start(out=vnormg[:], in_=vng_d[l]), writes=("vnormg",), dma="ld_c")
            P.op("sp", lambda e: e.dma_start(out=bsb[:], in_=bs_d[l]), writes=("bs",), dma="ld_c")
            P.op("sp", lambda e: e.dma_start(out=tA[:], in_=wsT_d[l]), writes=("tA",), dma="ld_c")
            P.op("dve", lambda e: e.tensor_copy(out=wsT[:], in_=tA[:]), reads=("tA",), writes=("wsT",))
            P.op("dve", lambda e: e.tensor_scalar(out=sublnvec[:], in0=sublnvec[:], scalar1=float((1.0 - li) * 0.5), scalar2=None, op0=ALU.mult),
                 reads=("sublnvec",), writes=("sublnvec",))
            P.op("sp", lambda e: e.dma_start(out=lqs, in_=lq_d[l]), writes=("tA",), dma="ld_c")
            lq4 = lqs.rearrange("p (a b d) -> p a b d", a=2, b=2)
            P.op("dve", lambda e: e.tensor_tensor(out=lprod.rearrange("p (a d) -> p a d", a=2), in0=lq4[:, :, 0, :], in1=lq4[:, :, 1, :], op=ALU.mult),
                 reads=("tA",), writes=("tA",))
            P.op("dve", lambda e: e.reduce_sum(out=lsum[:], in_=lprod.rearrange("p (a d) -> p a d", a=2), axis=AX.X), reads=("tA",), writes=("lsum",))
            P.op("act", lambda e: e.activation(out=lexp[:], in_=lsum[:], func=AF.Exp), reads=("lsum",), writes=("lexp",))
            P.op("dve", lambda e: e.tensor_tensor(out=neglam[:], in0=lexp[:, 1:2], in1=lexp[:, 0:1], op=ALU.subtract), reads=("lexp",), writes=("neglam",))
            P.op("dve", lambda e: e.tensor_scalar(out=neglam[:], in0=neglam[:], scalar1=float(-li), scalar2=None, op0=ALU.add), reads=("neglam",), writes=("neglam",))
            KTs = KT[:].rearrange("p h s -> p (h s)").bitcast(F32)
            NSL = min(8, (NH * S * 2) // 4096)
            skeys = ["KTs%d" % j for j in range(NSL)]
            P.op("dve", lambda e: e.memset(stat[:, 60:61], 0.0), writes=("KT", "fence") + tuple(skeys))
            pieces = [(Wq, 0, "Wq"), (Wkv, 0, "Wkv"), (Wkv, 512, "Wkv"), (Wg, 0, "Wg"), (Wg, 512, "Wg"), (Wg, 1024, "Wg"), (Wg, 1536, "Wg")]
            nq = [0]

            def stage_load(src_ap, wdt):
                j = stage_rr[0] % NSL
                stage_rr[0] += 1
                q = ("sp", "act")[nq[0] % 2]
                nq[0] += 1
                dst = KTs[:, j * 1024:j * 1024 + wdt]
                P.op(q, lambda e: e.dma_start(out=dst, in_=src_ap), writes=(skeys[j],), dma="ld_" + skeys[j])
                return KTs[:, j * 1024:(j + 1) * 1024], skeys[j]

            def conv(dst_ap, src_ap, scal_ap, skey, wkey):
                if scal_ap is None:
                    P.op("dve", lambda e: e.tensor_copy(out=dst_ap, in_=src_ap), reads=(skey,), writes=(wkey,))
                else:
                    P.op("dve", lambda e: e.tensor_scalar(out=dst_ap, in0=src_ap, scalar1=scal_ap, scalar2=None, op0=ALU.mult), reads=(skey, "ng"), writes=(wkey,))

            for k in range(8):
                for b in range(4):
                    c0 = b * 1024
                    wdt = min(1024, DIN - c0)
                    stg, skey = stage_load(win_d[l, k * 128:(k + 1) * 128, c0:c0 + wdt], wdt)
                    for hh in range(wdt // 512):
                        dstT, dcol, dkey = pieces[(c0 // 512) + hh]
                        conv(dstT[:, k, dcol:dcol + 512], stg[:, hh * 512:(hh + 1) * 512], ng[:, k:k + 1], skey, dkey)
                stg, skey = stage_load(wout_d[l, k * 128:(k + 1) * 128, :], 1024)
                conv(Wo[:, k, :], stg, None, skey, "Wo")
            P.op("dve", lambda e: e.memset(stat[:, 61:62], 0.0), reads=tuple(skeys), writes=("KT", "fence2"))

        ALLB = list(range(8))
        xslot = [0]

        def rstd_from_ss(ss_ap, n, inv_n, skey, okey, out_ap):
            P.op("dve", lambda e: e.tensor_scalar(out=out_ap, in0=ss_ap, scalar1=float(inv_n), scalar2=float(EPS), op0=ALU.mult, op1=ALU.add),
                 reads=(skey,), writes=(okey,))
            P.op("pool", lambda e: e.tensor_tensor(out=out_ap, in0=out_ap, in1=mhalf[:, 0:n], op=ALU.pow), reads=(okey, "mhalf"), writes=(okey,))

        def xload(s, t, src):
            sl = t % 2
            xk = "xt%d" % sl
            P.op("sp", lambda e: e.dma_start(out=xt[sl][:], in_=src[s, t * 128:(t + 1) * 128, :]),
                 reads=("R%d_%d" % (s, t),), writes=(xk,), dma="ld_" + xk)

        hT2 = G[:].rearrange("p a b -> p (a b)").bitcast(BF16).rearrange("p (k t) -> p k t", k=8)
        hTb = [hT[:], hT2]
        hTkeys = [("hT",), ("G0", "G1", "G2", "G3")]

        def pass1(s, l):
            src = x_d if l == 0 else R_d

            def build_tile(c, i):
                hbuf = hTb[c % 2]
                hkeys = hTkeys[c % 2]
                t = 4 * c + i
                sl = t % 2
                xk = "xt%d" % sl
                if t < 2:
                    xload(s, t, src)
                ssk = "ss%d" % (i % 2)
                ssa = stat[:, (i % 2):(i % 2) + 1]
                rsa = stat[:, 2 + (i % 2):3 + (i % 2)]
                rsk = "rs%d" % (i % 2)
                P.op("act", lambda e: e.activation(out=scr[:, 0:512], in_=xt[sl][:, 0:512], func=AF.Square, accum_out=ssa),
                     reads=(xk,), writes=("scr0", "scr1", ssk))
                ssa2 = stat[:, 4 + (i % 2):5 + (i % 2)]
                ssk2 = "ssb%d" % (i % 2)
                P.op("act", lambda e: e.activation(out=scr[:, 0:512], in_=xt[sl][:, 512:1024], func=AF.Square, accum_out=ssa2),
                     reads=(xk,), writes=("scr0", "scr1", ssk2))
                P.op("dve", lambda e: e.tensor_tensor(out=ssa, in0=ssa, in1=ssa2, op=ALU.add), reads=(ssk, ssk2), writes=(ssk,))
                rstd_from_ss(ssa, 1, 1.0 / D, ssk, rsk, rsa)
                hb = et[i % 2]
                hk = "et%d" % (i % 2)
                P.op("dve", lambda e: e.tensor_scalar(out=hb[:], in0=xt[sl][:], scalar1=rsa, scalar2=None, op0=ALU.mult),
                     reads=(xk, rsk), writes=(hk,))
                if t + 2 < NT:
                    xload(s, t + 2, src)

            def trans_tile(c, i):
                hbuf = hTb[c % 2]
                hkeys = hTkeys[c % 2]
                hb = et[i % 2]
                hk = "et%d" % (i % 2)
                b = next_bank(ALLB)
                pT = bank(b).bitcast(BF16)
                for k in range(8):
                    P.op("pe", lambda e, k=k: e.transpose(out=pT[:, k * 128:(k + 1) * 128], in_=hb[:, k * 128:(k + 1) * 128], identity=ident[:]),
                         reads=(hk, "ident"), writes=(bkey(b),))
                P.op("act", lambda e: e.copy(out=hbuf[:, :, i * 128:(i + 1) * 128], in_=pT.rearrange("p (k t) -> p k t", k=8)),
                     reads=(bkey(b),), writes=hkeys)

            def proj_group(c, g):
                hbuf = hTb[c % 2]
                hkeys = hTkeys[c % 2]
                b = next_bank(ALLB)
                if g < 4:
                    h = g
                    for k in range(8):
                        P.op("pe", lambda e, k=k: e.matmul(out=bank(b), lhsT=Wkv[:, k, h * 128:(h + 1) * 128], rhs=hbuf[:, k, :], start=(k == 0), stop=(k == 7)),
                             reads=("Wkv",) + hkeys, writes=(bkey(b),))
                    P.op("act", lambda e: e.copy(out=KT[:, h, c * QB:(c + 1) * QB], in_=bank(b)), reads=(bkey(b),), writes=("KT",))
                else:
                    i = g - 4
                    for k in range(8):
                        P.op("pe", lambda e, k=k: e.matmul(out=bank(b), lhsT=hbuf[:, k, i * 128:(i + 1) * 128], rhs=Wkv[:, k, 512:1024], start=(k == 0), stop=(k == 7)),
                             reads=("Wkv",) + hkeys, writes=(bkey(b),))
                    P.op("dve", lambda e: e.tensor_copy(out=VP[:, 4 * c + i, :, 0:128], in_=bank(b).rearrange("p (h d) -> p h d", h=NH)),
                         reads=(bkey(b),), writes=("VP",))

            def tile_ci(t):
                return (t // 4, t % 4)

            build_tile(0, 0)
            build_tile(0, 1)
            for t in range(4):
                trans_tile(0, t)
                if t + 2 < NT:
                    build_tile(*tile_ci(t + 2))
            for c in range(NCH):
                for i in range(4):
                    proj_group(c, 2 * i)
                    proj_group(c, 2 * i + 1)
                    if c + 1 < NCH:
                        t = 4 * (c + 1) + i
                        trans_tile(c + 1, i)
                        if t + 2 < NT:
                            build_tile(*tile_ci(t + 2))
                hbuf = hTb[c % 2]
                P.op("pool", lambda e, c=c, hbuf=hbuf: e.dma_start(out=hT_d[s, c], in_=hbuf.rearrange("p k t -> p (k t)")), reads=hTkeys[c % 2], writes=("hTd%d_%d" % (s, c),), dma="st_hT")

        MISC = [7, 0, 1, 2, 3]
        sbuf_tog = [0]
        et_tog = [0]
        et3_tog = [0]

        def pass2(s, l, is_last):
            src = x_d if l == 0 else R_d
            xload(s, 0, src)
            xload(s, 1, src)
            P.op("sp", lambda e: e.dma_start(out=hT[:].rearrange("p k t -> p (k t)"), in_=hT_d[s, 0]), reads=("hTd%d_%d" % (s, 0),), writes=("hT",), dma="ld_hT")
            def qproj():
                for h in range(NH):
                    b = next_bank(MISC)
                    for k in range(8):
                        P.op("pe", lambda e, b=b, k=k, h=h: e.matmul(out=bank(b), lhsT=Wq[:, k, h * 128:(h + 1) * 128], rhs=hT[:, k, :], start=(k == 0), stop=(k == 7)),
                             reads=("Wq", "hT"), writes=(bkey(b),))
                    P.op("act", lambda e, b=b, h=h: e.activation(out=QT[:, h, :], in_=bank(b), func=AF.Copy, scale=0.125), reads=(bkey(b),), writes=("QT",))

            qproj()
            for c in range(NCH):
                side_tasks = []

                def mm_actions(i, c0):
                    tok = slice(i * 128, (i + 1) * 128)
                    acts = []
                    for k in range(8):
                        acts.append(lambda k=k, tok=tok, c0=c0: P.op("pe", lambda e: e.matmul(out=bank(7), lhsT=hT[:, k, tok], rhs=Wg[:, k, c0:c0 + 512], start=(k == 0), stop=(k == 7)),
                                                                     reads=("Wg", "hT"), writes=(bkey(7),)))
                    return acts

                def task_gate(i):
                    a0 = lambda: P.op("act", lambda e: e.copy(out=tA[:], in_=bank(7)), reads=(bkey(7),), writes=("tA",))
                    a1 = lambda: P.op("act", lambda e: e.activation(out=G[:, i, :], in_=tA[:], func=AF.Tanh, scale=0.5), reads=("tA",), writes=("G%d" % i,))
                    d1 = lambda: P.op("dve", lambda e: e.scalar_tensor_tensor(out=G[:, i, :], in0=G[:, i, :], scalar=1.0, in1=tA[:], op0=ALU.add, op1=ALU.mult),
                                      reads=("tA", "G%d" % i), writes=("G%d" % i,))
                    d2 = lambda: P.op("pool", lambda e: e.tensor_tensor(out=G[:, i, :], in0=G[:, i, :], in1=sublnvec[:], op=ALU.mult), reads=("G%d" % i, "sublnvec"), writes=("G%d" % i,))
                    return (mm_actions(i, 0), [[a0, a1], [d1, d2]])

                def task_gm(i):
                    a0 = lambda: P.op("act", lambda e: e.copy(out=tB[:], in_=bank(7)), reads=(bkey(7),), writes=("tB",))
                    a1 = lambda: P.op("act", lambda e: e.activation(out=tA[:], in_=tB[:], func=AF.Tanh, scale=0.5), reads=("tB",), writes=("tA",))
                    d1 = lambda: P.op("dve", lambda e: e.scalar_tensor_tensor(out=tB[:], in0=tA[:], scalar=1.0, in1=tB[:], op0=ALU.add, op1=ALU.mult),
                                      reads=("tA", "tB"), writes=("tB",))
                    return (mm_actions(i, 1536), [[a0, a1], [d1]])

                def task_u(i):
                    a0 = lambda: P.op("act", lambda e: e.activation(out=tA[:], in_=bank(7), func=AF.Copy, scale=0.5), reads=(bkey(7),), writes=("tA",))
                    d1 = lambda: P.op("dve", lambda e: e.tensor_tensor(out=tB[:], in0=tB[:], in1=tA[:], op=ALU.mult), reads=("tA", "tB"), writes=("tB",))
                    return (mm_actions(i, 512), [[a0], [d1]])

                def task_vg(i):
                    ssv = stat[:, 8:9]
                    rsv = stat[:, 9:10]
                    a0 = lambda: P.op("act", lambda e: e.copy(out=tA[:], in_=bank(7)), reads=(bkey(7),), writes=("tA",))
                    a1 = lambda: P.op("act", lambda e: e.activation(out=vn[:], in_=tA[:], func=AF.Square, scale=float(512.0 ** -0.5), accum_out=ssv), reads=("tA",), writes=("vn", "ssv"))

                    def p1():
                        P.op("pool", lambda e: e.tensor_tensor(out=rsv, in0=ssv, in1=epsc[:, 0:1], op=ALU.add), reads=("ssv", "epsc"), writes=("rsv",))
                        P.op("pool", lambda e: e.tensor_tensor(out=rsv, in0=rsv, in1=mhalf[:, 0:1], op=ALU.pow), reads=("rsv", "mhalf"), writes=("rsv",))
                        P.op("pool", lambda e: e.tensor_tensor(out=tA[:], in0=tA[:], in1=rsv.to_broadcast([128, 512]), op=ALU.mult), reads=("tA", "rsv"), writes=("tA",))
                        P.op("pool", lambda e: e.tensor_tensor(out=vn[:], in0=tA[:], in1=vnormg[:], op=ALU.mult), reads=("tA", "vnormg"), writes=("vn",))
                    return (mm_actions(i, 1024), [[a0, a1], [p1], [], []])

                def task_sv(i):
                    mm = []
                    for g in range(4):
                        mm.append(lambda g=g: P.op("pe", lambda e: e.matmul(out=bank(7)[:, g * 128:(g + 1) * 128], lhsT=wsT[:, g * 128:(g + 1) * 128], rhs=vn[:, g * 128:(g + 1) * 128],
                                                                            start=True, stop=True, skip_group_check=True),
                                                   reads=("wsT", "vn"), writes=(bkey(7),)))
                    a0 = lambda: P.op("act", lambda e: e.copy(out=tA[:], in_=bank(7)), reads=(bkey(7),), writes=("tA",))
                    d1 = lambda: P.op("dve", lambda e: e.tensor_tensor(out=tA[:].rearrange("p (g d) -> p g d", g=4), in0=tA[:].rearrange("p (g d) -> p g d", g=4),
                                                                        in1=bsb[:, 0:4].unsqueeze(2).to_broadcast([128, 4, 128]), op=ALU.add),
                                      reads=("tA", "bs"), writes=("tA",))
                    d2 = lambda: P.op("dve", lambda e: e.tensor_tensor(out=cat[:, i, 512:1024], in0=tA[:], in1=tB[:], op=ALU.mult), reads=("tA", "tB"), writes=("cat%d" % i,))
                    return (mm, [[a0], [d1, d2]])

                def pgroup(i, c0):
                    b_ = next_bank(MISC)
                    tok = slice(i * 128, (i + 1) * 128)
                    for k in range(8):
                        P.op("pe", lambda e, k=k: e.matmul(out=bank(b_), lhsT=hT[:, k, tok], rhs=Wg[:, k, c0:c0 + 512], start=(k == 0), stop=(k == 7)),
                             reads=("Wg", "hT"), writes=(bkey(b_),))
                    return b_

                for i in range(4):
                    ssv = stat[:, 8 + 2 * (i % 2):9 + 2 * (i % 2)]
                    rsv = stat[:, 9 + 2 * (i % 2):10 + 2 * (i % 2)]
                    ssvk = "ssv%d" % (i % 2)
                    rsvk = "rsv%d" % (i % 2)
                    tG = tA if i % 2 == 0 else otmp[:].rearrange("p a b -> p (a b)")
                    tGk = "tA" if i % 2 == 0 else "otmp"
                    tGa = tA[:] if i % 2 == 0 else otmp[:].rearrange("p a b -> p (a b)")
                    bv = pgroup(i, 1024)
                    P.op("act", lambda e, bv=bv, ssv=ssv: e.activation(out=scr[:, 512:1024], in_=bank(bv), func=AF.Square, scale=float(512.0 ** -0.5), accum_out=ssv),
                         reads=(bkey(bv),), writes=("scr1", "scr2", ssvk))
                    P.op("pool", lambda e, ssv=ssv, rsv=rsv: e.tensor_tensor(out=rsv, in0=ssv, in1=epsc[:, 0:1], op=ALU.add), reads=(ssvk, "epsc"), writes=(rsvk,))
                    P.op("pool", lambda e, rsv=rsv: e.tensor_tensor(out=rsv, in0=rsv, in1=mhalf[:, 0:1], op=ALU.pow), reads=(rsvk, "mhalf"), writes=(rsvk,))
                    bg = pgroup(i, 0)
                    P.op("act", lambda e, bg=bg, tGa=tGa: e.activation(out=tGa, in_=bank(bg), func=AF.Tanh, scale=0.5), reads=(bkey(bg),), writes=(tGk,))
                    P.op("dve", lambda e, bv=bv, rsv=rsv: e.scalar_tensor_tensor(out=vn[:], in0=bank(bv), scalar=rsv, in1=vnormg[:], op0=ALU.mult, op1=ALU.mult),
                         reads=(bkey(bv), rsvk, "vnormg"), writes=("vn",))
                    P.op("dve", lambda e, bg=bg, i=i, tGa=tGa: e.scalar_tensor_tensor(out=G[:, i, :], in0=tGa, scalar=1.0, in1=bank(bg), op0=ALU.add, op1=ALU.mult),
                         reads=(tGk, bkey(bg)), writes=("G%d" % i,))
                    P.op("pool", lambda e, i=i: e.tensor_tensor(out=G[:, i, :], in0=G[:, i, :], in1=sublnvec[:], op=ALU.mult), reads=("G%d" % i, "sublnvec"), writes=("G%d" % i,))
                    bgm = pgroup(i, 1536)
                    P.op("act", lambda e, bgm=bgm: e.activation(out=tB[:], in_=bank(bgm), func=AF.Tanh, scale=0.5), reads=(bkey(bgm),), writes=("tB",))
                    P.op("dve", lambda e, bgm=bgm: e.scalar_tensor_tensor(out=tB[:], in0=tB[:], scalar=1.0, in1=bank(bgm), op0=ALU.add, op1=ALU.mult),
                         reads=("tB", bkey(bgm)), writes=("tB",))
                    bu = pgroup(i, 512)
                    P.op("dve", lambda e, bu=bu: e.scalar_tensor_tensor(out=tB[:], in0=tB[:], scalar=0.5, in1=bank(bu), op0=ALU.mult, op1=ALU.mult),
                         reads=("tB", bkey(bu)), writes=("tB",))
                    bsv = next_bank(MISC)
                    for g in range(4):
                        P.op("pe", lambda e, bsv=bsv, g=g: e.matmul(out=bank(bsv)[:, g * 128:(g + 1) * 128], lhsT=wsT[:, g * 128:(g + 1) * 128], rhs=vn[:, g * 128:(g + 1) * 128],
                                                                    start=True, stop=True, skip_group_check=True),
                             reads=("wsT", "vn"), writes=(bkey(bsv),))
                    for g in range(4):
                        P.op("dve", lambda e, bsv=bsv, g=g, i=i: e.scalar_tensor_tensor(out=cat[:, i, 512 + g * 128:512 + (g + 1) * 128], in0=bank(bsv)[:, g * 128:(g + 1) * 128],
                                                                                     scalar=bsb[:, g:g + 1], in1=tB[:, g * 128:(g + 1) * 128], op0=ALU.add, op1=ALU.mult),
                             reads=(bkey(bsv), "bs", "tB"), writes=("cat%d" % i,))
                if c + 1 < NCH:
                    P.op("sp", lambda e, c=c: e.dma_start(out=hT[:].rearrange("p k t -> p (k t)"), in_=hT_d[s, c + 1]), reads=("hTd%d_%d" % (s, c + 1),), writes=("hT",), dma="ld_hT")

                def acc_ap(idx):
                    bnk = 4 + idx // 3
                    col = (idx % 3) * 129
                    return bank(bnk)[:, col:col + 129], bnk

                steps = []
                for h in HEAD_ORDER:
                    lst = []
                    for kb in range(NKB):
                        if kb < 4 * c:
                            if 4 * c - kb <= KEEP[h]:
                                lst.append((kb, "b"))
                        elif kb < 4 * c + 4:
                            lst.append((kb, "i"))
                        else:
                            if kb - 4 * c - 3 <= KEEP[h]:
                                lst.append((kb, "a"))
                    lst = [x for x in lst if x[1] == "a"] + [x for x in lst if x[1] == "b"] + [x for x in lst if x[1] == "i"]
                    for n_, (kb, k) in enumerate(lst):
                        first = (n_ == 0) or (lst[n_ - 1][1] != k)
                        last = (n_ == len(lst) - 1) or (lst[n_ + 1][1] != k)
                        steps.append((h, kb, k, first, last, n_ == len(lst) - 1, first and n_ == 0))

                def emit_qk(h, kb, k):
                    X = sbuf_tog[0] % 2
                    sbuf_tog[0] += 1
                    inchunk = k == "i"
                    for m in range(2):
                        pr = slice(64 * m, 64 * m + 64)
                        P.op("pe", lambda e, X=X, m=m, pr=pr, h=h, kb=kb, inchunk=inchunk: e.matmul(out=ps[X][:, m, :], lhsT=KT[pr, h, kb * 128:(kb + 1) * 128], rhs=QT[pr, h, :],
                                                                                                  start=True, stop=(not inchunk), skip_group_check=True),
                             reads=("KT", "QT"), writes=(bkey(2 * X + m),))
                    if inchunk:
                        j = kb - 4 * c
                        off = 384 - 128 * j
                        c0, c1 = [(256, 512), (384, 512), (0, 128), (0, 256)][j]
                        for m in range(2):
                            P.op("pe", lambda e, X=X, m=m, h=h, off=off: e.matmul(out=ps[X][:, m, :], lhsT=mident[:, h, :], rhs=Bhi[:, off:off + 512], start=False, stop=False, skip_group_check=True),
                                 reads=("mident", "Bhi"), writes=(bkey(2 * X + m),))
                            P.op("pe", lambda e, X=X, m=m, h=h, off=off, c0=c0, c1=c1: e.matmul(out=ps[X][:, m, c0:c1], lhsT=mident[:, h, :], rhs=Blo[:, off + c0:off + c1], start=False, stop=True, skip_group_check=True),
                                 reads=("mident", "Blo"), writes=(bkey(2 * X + m),))
                    return X

                ETA = [et[0][:], et[1][:], tB[:].bitcast(BF16)]
                ETK = ["et0", "et1", "tB"]

                def emit_act(h, kb, k, X):
                    E = et3_tog[0] % 3
                    et3_tog[0] += 1
                    if k == "b":
                        idx = (4 * c - kb) - 1
                    elif k == "a":
                        idx = NBA + (kb - 4 * c - 4)
                    else:
                        idx = 2 * NBA
                    col = h * NBC + idx
                    P.op("act", lambda e, X=X, E=E, col=col: e.activation(out=ETA[E], in_=ps[X][:].rearrange("p a b -> p (a b)"), func=AF.Exp, bias=bcol[:, col:col + 1], scale=1.0),
                         reads=(bkey(2 * X), bkey(2 * X + 1), "bcol"), writes=(ETK[E],))
                    return E

                def emit_pv(h, kb, E, first):
                    for idx in range(8):
                        i, m = idx // 2, idx % 2
                        ap, bnk = acc_ap(idx)
                        stflag = first and (idx % 3 == 0)
                        P.op("pe", lambda e, ap=ap, E=E, m=m, i=i, kb=kb, h=h, stflag=stflag: e.matmul(out=ap, lhsT=ETA[E][:, m * 512 + i * 128:m * 512 + (i + 1) * 128], rhs=VP[:, kb, h, :],
                                                                                                     start=stflag, stop=False, skip_group_check=True),
                             reads=(ETK[E], "VP", "VPones"), writes=(bkey(bnk),))

                Oav = Oacc[:].rearrange("p i m e -> p (i m) e")
                scrv = scr[:, 0:1032].rearrange("p (a e) -> p a e", e=129)

                def emit_phase_evac(h, k, firstphase):
                    for g, (a0, a1) in enumerate([(0, 3), (3, 6), (6, 8)]):
                        bnk = 4 + g
                        n_ = a1 - a0
                        src_ = bank(bnk)[:, 0:n_ * 129].rearrange("p (a e) -> p a e", e=129)
                        dst = Oav[:, a0:a1, :]
                        if k == "i":
                            if firstphase:
                                P.op("dve", lambda e, src_=src_, dst=dst: e.tensor_copy(out=dst, in_=src_), reads=(bkey(bnk),), writes=("Oacc%d" % g,))
                            else:
                                P.op("dve", lambda e, src_=src_, dst=dst: e.tensor_tensor(out=dst, in0=src_, in1=dst, op=ALU.add), reads=(bkey(bnk), "Oacc%d" % g), writes=("Oacc%d" % g,))
                        else:
                            f0 = (h * 2 + (0 if k == "b" else 1)) * 8 + a0
                            fb = ftab[:, f0:f0 + n_].unsqueeze(2).to_broadcast([128, n_, 129])
                            if firstphase:
                                P.op("dve", lambda e, src_=src_, dst=dst, fb=fb: e.tensor_tensor(out=dst, in0=src_, in1=fb, op=ALU.mult),
                                     reads=(bkey(bnk), "ftab"), writes=("Oacc%d" % g,))
                            else:
                                P.op("dve", lambda e, src_=src_, a0=a0, a1=a1, fb=fb: e.tensor_tensor(out=scrv[:, a0:a1, :], in0=src_, in1=fb, op=ALU.mult),
                                     reads=(bkey(bnk), "ftab"), writes=("scr%d" % g,))
                    if k != "i" and not firstphase:
                        for g, (a0, a1) in enumerate([(0, 3), (3, 6), (6, 8)]):
                            P.op("dve", lambda e, a0=a0, a1=a1: e.tensor_tensor(out=Oav[:, a0:a1, :], in0=Oav[:, a0:a1, :], in1=scrv[:, a0:a1, :], op=ALU.add),
                                 reads=("Oacc%d" % g, "scr%d" % g), writes=("Oacc%d" % g,))

                rr_ = stat[:, 16:24].rearrange("p (i m) -> p i m", i=4)
                c1 = stat[:, 24:28]
                ssq = stat[:, 28:32]
                rso = stat[:, 32:36]
                scrA = scr[:, 0:512].rearrange("p (a e) -> p a e", e=128)
                scrB = scr[:, 512:1024].rearrange("p (a e) -> p a e", e=128)

                def combine_stages(h):
                    def s0():
                        P.op("dve", lambda e: e.reciprocal(out=rr_, in_=Oacc[:, :, :, 128]), reads=("Oacc0", "Oacc1", "Oacc2",), writes=("rr",))
                        P.op("dve", lambda e: e.tensor_scalar(out=c1, in0=rr_[:, :, 1], scalar1=neglam[:, 0:1], scalar2=None, op0=ALU.mult), reads=("rr", "neglam"), writes=("c1",))
                        P.op("dve", lambda e: e.tensor_tensor(out=otmp[:], in0=Oacc[:, :, 0, 0:128], in1=rr_[:, :, 0:1].to_broadcast([128, 4, 128]), op=ALU.mult),
                             reads=("Oacc0", "Oacc1", "Oacc2", "rr"), writes=("otmp",))
                        P.op("dve", lambda e: e.tensor_tensor(out=scrB, in0=Oacc[:, :, 1, 0:128], in1=c1.unsqueeze(2).to_broadcast([128, 4, 128]), op=ALU.mult),
                             reads=("Oacc0", "Oacc1", "Oacc2", "c1"), writes=("scr1", "scr2"))

                    def s1():
                        P.op("dve", lambda e: e.tensor_tensor(out=otmp[:], in0=otmp[:], in1=scrB, op=ALU.add), reads=("otmp", "scr1", "scr2"), writes=("otmp",))
                        P.op("pool", lambda e: e.tensor_tensor(out=scrA, in0=otmp[:], in1=otmp[:], op=ALU.mult), reads=("otmp",), writes=("scr0", "scr1"))

                    def s2():
                        P.op("dve", lambda e: e.reduce_sum(out=ssq, in_=scrA, axis=AX.X), reads=("scr0", "scr1"), writes=("ssq",))
                        rstd_from_ss(ssq, 4, 1.0 / 128, "ssq", "rso", rso)

                    def s3():
                        P.op("dve", lambda e: e.tensor_tensor(out=otmp[:], in0=otmp[:], in1=rso.unsqueeze(2).to_broadcast([128, 4, 128]), op=ALU.mult),
                             reads=("otmp", "rso"), writes=("otmp",))
                        P.op("dve", lambda e: e.tensor_tensor(out=cat[:, :, h * 128:(h + 1) * 128], in0=otmp[:], in1=G[:, :, h * 128:(h + 1) * 128], op=ALU.mult),
                             reads=("otmp", "G0", "G1", "G2", "G3"), writes=("cat0", "cat1", "cat2", "cat3"))
                    return [s0, s1, None, s2, None, s3]

                nst = len(steps)

                def make_sched(mps):
                    sched = {}
                    pos = 0
                    prev_last = -1
                    for (mm, posts) in side_tasks:
                        if len(mm) <= 4:
                            pos = max(pos, prev_last + 1)
                        nmm = (len(mm) + mps - 1) // mps if len(mm) > 4 else 1
                        for q in range(nmm):
                            chunk_ = mm[q * mps:(q + 1) * mps] if len(mm) > 4 else mm
                            sched.setdefault(pos + q, []).extend(chunk_)
                        pstart = max(pos + nmm, prev_last)
                        for q, pa in enumerate(posts):
                            sched.setdefault(pstart + q, []).extend(pa)
                        prev_last = pstart + len(posts) - 1
                        pos = max(pos + nmm + 1, pstart + 1)
                    return sched, prev_last + 2

                for mps in (2, 3, 4, 8):
                    sched, need = make_sched(mps)
                    if need <= nst - 1:
                        break
                cq = []
                Xs = {}
                for q_ in range(min(2, nst)):
                    Xs[q_] = emit_qk(steps[q_][0], steps[q_][1], steps[q_][2])
                nphase_done = {}
                for si, (h, kb, k, first, last, lasthead, firsthead) in enumerate(steps):
                    X = Xs.pop(si)
                    E = emit_act(h, kb, k, X)
                    if si + 2 < nst:
                        Xs[si + 2] = emit_qk(steps[si + 2][0], steps[si + 2][1], steps[si + 2][2])
                    emit_pv(h, kb, E, first)
                    if last:
                        firstphase = h not in nphase_done
                        nphase_done[h] = True
                        while cq:
                            f_ = cq.pop(0)
                            if f_ is not None:
                                f_()
                        emit_phase_evac(h, k, firstphase)
                        if lasthead:
                            st_ = combine_stages(h)
                            st_[0]()
                            cq = st_[1:]
                    elif cq:
                        f_ = cq.pop(0)
                        if f_ is not None:
                            f_()
                    for fn_ in sched.pop(si - 1, ()):
                        fn_()
                for key_ in sorted(sched.keys()):
                    for fn_ in sched[key_]:
                        fn_()
                while cq:
                    f_ = cq.pop(0)
                    if f_ is not None:
                        f_()
                if c + 1 < NCH:
                    qproj()
                tinfo = {}

                def tail_T(i):
                    b = next_bank(MISC)
                    pT = bank(b).bitcast(BF16)
                    E = et_tog[0] % 2
                    et_tog[0] += 1
                    ek = "et%d" % E
                    for k in range(8):
                        P.op("pe", lambda e, pT=pT, i=i, k=k: e.transpose(out=pT[:, k * 128:(k + 1) * 128], in_=cat[:, i, k * 128:(k + 1) * 128], identity=ident[:]),
                             reads=("cat%d" % i, "ident"), writes=(bkey(b),))
                    P.op("dve", lambda e, pT=pT, E=E: e.tensor_copy(out=et[E][:], in_=pT), reads=(bkey(b),), writes=(ek,))
                    tinfo[i] = (E, ek)

                def tail_M(i):
                    E, ek = tinfo[i]
                    t = 4 * c + i
                    sl = t % 2
                    xk = "xt%d" % sl
                    for half in range(2):
                        bo = next_bank(MISC)
                        for k in range(8):
                            P.op("pe", lambda e, bo=bo, E=E, k=k, half=half: e.matmul(out=bank(bo), lhsT=et[E][:, k * 128:(k + 1) * 128], rhs=Wo[:, k, half * 512:(half + 1) * 512],
                                                                                    start=(k == 0), stop=(k == 7)),
                                 reads=(ek, "Wo"), writes=(bkey(bo),))
                        P.op("dve", lambda e, bo=bo, sl=sl, half=half: e.tensor_tensor(out=xt[sl][:, half * 512:(half + 1) * 512], in0=bank(bo), in1=xt[sl][:, half * 512:(half + 1) * 512], op=ALU.add),
                             reads=(bkey(bo), xk), writes=(xk,))
                    if is_last:
                        ssf = stat[:, 40:41]
                        ssf2 = stat[:, 41:42]
                        rsf = stat[:, 42:43]
                        P.op("act", lambda e, sl=sl: e.activation(out=tA[:], in_=xt[sl][:, 0:512], func=AF.Square, accum_out=ssf), reads=(xk,), writes=("tA", "ssf"))
                        P.op("act", lambda e, sl=sl: e.activation(out=tA[:], in_=xt[sl][:, 512:1024], func=AF.Square, accum_out=ssf2), reads=(xk,), writes=("tA", "ssf2"))
                        P.op("dve", lambda e: e.tensor_tensor(out=ssf, in0=ssf, in1=ssf2, op=ALU.add), reads=("ssf", "ssf2"), writes=("ssf",))
                        rstd_from_ss(ssf, 1, 1.0 / D, "ssf", "rsf", rsf)
                        P.op("dve", lambda e, sl=sl: e.scalar_tensor_tensor(out=xt[sl][:], in0=xt[sl][:], scalar=rsf, in1=fgs[:], op0=ALU.mult, op1=ALU.mult),
                             reads=(xk, "rsf", "fgs"), writes=(xk,))
                        P.op("pool", lambda e, sl=sl, t=t: e.dma_start(out=y_d[s, t * 128:(t + 1) * 128, :], in_=xt[sl][:]), reads=(xk,), writes=("y%d_%d" % (s, t),), dma="st_" + xk)
                    else:
                        P.op("pool", lambda e, sl=sl, t=t: e.dma_start(out=R_d[s, t * 128:(t + 1) * 128, :], in_=xt[sl][:]), reads=(xk,), writes=("R%d_%d" % (s, t),), dma="st_" + xk)
                    if t + 2 < NT:
                        xload(s, t + 2, src)

                tail_T(0)
                tail_T(1)
                tail_M(0)
                tail_T(2)
                tail_M(1)
                tail_T(3)
                tail_M(2)
                tail_M(3)

        for l in range(L):
            layer_setup(l)
            for s in range(NSEQ):
                pass1(s, l)
                pass2(s, l, (l == L - 1) and last_is_final)
        counts = P.build(nc, st)
    return nc, counts


def make_consts(S):
    NKB = S // 128
    NBA = max(NKB - 4, 1)
    NBC = 2 * NBA + 1
    p = np.arange(128, dtype=np.float64)
    ident = np.eye(128, dtype=np.float32)
    mident = np.zeros((128, NH, 128), np.float32)
    for h in range(NH):
        mident[:, h, :] = np.eye(128) * SLOPES[h]
    y = np.arange(896, dtype=np.float64)
    d = np.abs(y[None, :] - 384.0 - p[:, None])
    bhi = (-np.minimum(d, 256.0)).astype(np.float32)
    blo = (-np.maximum(d - 256.0, 0.0)).astype(np.float32)
    bcol = np.zeros((128, NH, NBC), np.float64)
    for h in range(NH):
        m = SLOPES[h]
        for dl in range(1, NBA + 1):
            bcol[:, h, dl - 1] = -m * (128.0 * dl - p)
        for dl in range(0, NBA):
            bcol[:, h, NBA + dl] = -m * (128.0 * dl + p + 1.0)
    ftab = np.zeros((128, NH, 2, 8), np.float64)
    for h in range(NH):
        m = SLOPES[h]
        for i in range(4):
            for mm in range(2):
                ftab[:, h, 0, 2 * i + mm] = np.exp(-m * (128.0 * i + p))
                ftab[:, h, 1, 2 * i + mm] = np.exp(-m * (511.0 - 128.0 * i - p))
    return dict(identf=ident, midentf=mident.reshape(128, NH * 128), bhif=bhi, blof=blo,
                bcol=bcol.reshape(128, NH * NBC).astype(np.float32), ftab=ftab.reshape(128, NH * 16).astype(np.float32))


def make_params(norm_g, w_in, lambda_qk, subln_g, vnorm_g, w_s, b_s, w_out, final_g):
    L = w_in.shape[0]
    f = np.float32
    c = np.ascontiguousarray
    d = {}
    d["w_in"] = c(w_in, dtype=f)
    d["w_out"] = c(w_out, dtype=f)
    d["ng"] = c(np.asarray(norm_g, f).reshape(L, 8, 128).transpose(0, 2, 1))
    d["lq"] = c(np.broadcast_to(np.asarray(lambda_qk, f).reshape(L, 1, 256), (L, 128, 256)))
    d["subg"] = c(np.broadcast_to(np.tile(np.asarray(subln_g, f), (1, 4)).reshape(L, 1, 512), (L, 128, 512)))
    d["vng"] = c(np.broadcast_to(np.asarray(vnorm_g, f).reshape(L, 1, 512), (L, 128, 512)))
    d["wsT"] = c(np.asarray(w_s, f).transpose(0, 3, 1, 2).reshape(L, 128, 512))
    d["bs"] = c(np.asarray(b_s, f).transpose(0, 2, 1))
    d["fg"] = c(np.broadcast_to(np.asarray(final_g, f).reshape(1, D), (128, D)))
    return d


_CACHE = {}


def run(xs_per_core, params, S, L, n_cores, trace=False):
    NSEQ = xs_per_core[0].shape[0]
    key = (NSEQ, S, L)
    if key not in _CACHE:
        _CACHE[key] = build_nc(NSEQ, S, L)
    nc, counts = _CACHE[key]
    consts = make_consts(S)
    in_maps = []
    for ci in range(n_cores):
        m = {"x": np.ascontiguousarray(xs_per_core[ci], dtype=np.float32)}
        m.update(params)
        m.update(consts)
        in_maps.append(m)
    res = run_bass_kernel_spmd(nc, in_maps, core_ids=list(range(n_cores)))
    return [r["y"] for r in res.results]


def kernel(x_prompt, x_sample, norm_g, w_in, lambda_qk, subln_g, vnorm_g, w_s, b_s, w_out, final_g):
    x_prompt = np.asarray(x_prompt, np.float32)
    x_sample = np.asarray(x_sample, np.float32)
    S = x_prompt.shape[1]
    L = np.asarray(w_in).shape[0]
    allx = np.concatenate([x_prompt, x_sample], axis=0)
    n_cores = 8
    per = allx.shape[0] // n_cores
    xs = [allx[i * per:(i + 1) * per] for i in range(n_cores)]
    params = make_params(np.asarray(norm_g), np.asarray(w_in), np.asarray(lambda_qk), np.asarray(subln_g), np.asarray(vnorm_g),
                         np.asarray(w_s), np.asarray(b_s), np.asarray(w_out), np.asarray(final_g))
    ys = run(xs, params, S, L, n_cores)
    ally = np.concatenate(ys, axis=0)
    nb = x_prompt.shape[0]
    return (np.ascontiguousarray(ally[:nb]), np.ascontiguousarray(ally[nb:]))
```

```python
import math
from contextlib import ExitStack

import numpy as np
import concourse.bass as bass
import concourse.mybir as mybir
from concourse.bass_utils import run_bass_kernel_spmd

F32 = mybir.dt.float32
BF16 = mybir.dt.bfloat16
ALU = mybir.AluOpType
AF = mybir.ActivationFunctionType
AX = mybir.AxisListType

D = 1024
DIN = 3584
NH = 4
EPS = 1e-6
QB = 512
SLOPES = [2.0 ** (-2.0 * (i + 1)) for i in range(NH)]
THRESH = 60.0
KEEP = [int(math.floor((THRESH / m - 1.0) / 128.0 - 1e-9)) + 1 for m in SLOPES]
HEAD_ORDER = [3, 2, 1, 0]


def lam_init_fn(l):
    return 0.8 - 0.6 * math.exp(-0.3 * l)


class Prog:
    EPOCH = 12000

    def __init__(self):
        self.ins = []

    def op(self, eng, fn, reads=(), writes=(), dma=None):
        self.ins.append([eng, fn, tuple(reads), tuple(writes), dma, None, False])

    def build(self, nc, stack, final_wait_prefix=("st_",)):
        ins = self.ins
        n = len(ins)
        last_w = {}
        rd_eng = {}
        rd_dma = {}
        deps_all = [None] * n
        for i in range(n):
            eng, fn, reads, writes, dma, _, _ = ins[i]
            deps = {}
            for k in reads:
                j = last_w.get(k)
                if j is not None:
                    deps[j] = True
            for k in writes:
                j = last_w.get(k)
                if j is not None and j not in deps:
                    deps[j] = True
                for j in rd_eng.get(k, {}).values():
                    if j != i and j not in deps:
                        deps[j] = False
                for j in rd_dma.get(k, ()):
                    if j not in deps:
                        deps[j] = False
            for k in reads:
                if dma is not None:
                    rd_dma.setdefault(k, []).append(i)
                else:
                    rd_eng.setdefault(k, {})[eng] = i
            for k in writes:
                last_w[k] = i
                rd_eng[k] = {}
                rd_dma[k] = []
            need = []
            for j, raw in deps.items():
                ej, dj = ins[j][0], ins[j][4]
                if dj is not None:
                    need.append(j)
                elif ej == eng and dma is None:
                    if raw and eng != "pe":
                        need.append(j)
                elif ej == eng and dma is not None:
                    need.append(j)
                else:
                    need.append(j)
            deps_all[i] = need
            for j in need:
                ins[j][6] = True
        eng_cnt = {}
        dma_cnt = {}
        ticket = [None] * n
        dma_before = [None] * n
        sem_names = {}
        for i in range(n):
            eng, fn, reads, writes, dma, _, needs = ins[i]
            dma_before[i] = dict(dma_cnt) if False else None
            if dma is not None:
                dma_cnt[dma] = dma_cnt.get(dma, 0) + 16
                ticket[i] = (("d", dma), dma_cnt[dma])
                sem_names[("d", dma)] = True
            elif needs:
                c = eng_cnt.get(eng, 0) + 1
                eng_cnt[eng] = c
                ep = (c - 1) // self.EPOCH
                ticket[i] = (("e", eng, ep), c - ep * self.EPOCH)
                sem_names[("e", eng, ep)] = True
        sems = {}
        for k in sem_names:
            nm = "s_" + "_".join(str(x) for x in k)
            sems[k] = stack.enter_context(nc.semaphore(nm))
        dma_run = {}
        dma_seen_at = [None] * n
        for i in range(n):
            snap = {}
            for j in deps_all[i]:
                dj = ins[j][4]
                if dj is not None:
                    snap[dj] = dma_run.get(dj, 0)
            dma_seen_at[i] = snap
            if ins[i][4] is not None:
                dma_run[ins[i][4]] = dma_run.get(ins[i][4], 0) + 16
        final = {k: v for k, v in dma_run.items() if k.startswith(final_wait_prefix)}
        per_eng = {}
        for i in range(n):
            per_eng.setdefault(ins[i][0], []).append(i)

        def run_engine(ename, e):
            waited = {}
            for i in per_eng.get(ename, ()):
                _, fn, _, _, dma, _, needs = ins[i]
                wl = {}
                for j in deps_all[i]:
                    sk, val = ticket[j]
                    if sk[0] == "d":
                        val = dma_seen_at[i][sk[1]]
                    if val > wl.get(sk, 0):
                        wl[sk] = val
                for sk, val in wl.items():
                    if waited.get(sk, 0) < val:
                        e.wait_ge(sems[sk], val)
                        waited[sk] = val
                r = fn(e)
                if dma is not None:
                    r.then_inc(sems[("d", dma)], 16)
                elif needs:
                    r.then_inc(sems[ticket[i][0]], 1)
            if ename == "sp":
                for k, v in final.items():
                    e.wait_ge(sems[("d", k)], v)

        with nc.Block() as block:
            @block.tensor
            def _(e):
                run_engine("pe", e)

            @block.scalar
            def _(e):
                run_engine("act", e)

            @block.vector
            def _(e):
                run_engine("dve", e)

            @block.gpsimd
            def _(e):
                run_engine("pool", e)

            @block.sync
            def _(e):
                run_engine("sp", e)
        return {e: len(v) for e, v in per_eng.items()}


def build_nc(NSEQ, S, L, last_is_final=True):
    NT = S // 128
    NCH = S // QB
    NKB = NT
    NBA = max(NKB - 4, 1)
    NBC = 2 * NBA + 1
    nc = bass.Bass("TRN2", target_bir_lowering=False)
    dt_in = lambda name, shape: nc.dram_tensor(name, list(shape), F32, kind="ExternalInput").ap()
    x_d = dt_in("x", (NSEQ, S, D))
    win_d = dt_in("w_in", (L, D, DIN))
    wout_d = dt_in("w_out", (L, D, D))
    ng_d = dt_in("ng", (L, 128, 8))
    lq_d = dt_in("lq", (L, 128, 256))
    subg_d = dt_in("subg", (L, 128, 512))
    vng_d = dt_in("vng", (L, 128, 512))
    wsT_d = dt_in("wsT", (L, 128, 512))
    bs_d = dt_in("bs", (L, 128, 4))
    fg_d = dt_in("fg", (128, D))
    ident_d = dt_in("identf", (128, 128))
    mident_d = dt_in("midentf", (128, 512))
    bhi_d = dt_in("bhif", (128, 896))
    blo_d = dt_in("blof", (128, 896))
    bcol_d = dt_in("bcol", (128, NH * NBC))
    ftab_d = dt_in("ftab", (128, NH * 16))
    y_d = nc.dram_tensor("y", [NSEQ, S, D], F32, kind="ExternalOutput").ap()
    R_d = nc.dram_tensor("Rres", [NSEQ, S, D], F32).ap()
    hT_d = nc.dram_tensor("hTs", [NSEQ, NCH, 128, 8 * QB], BF16).ap()

    P = Prog()
    with ExitStack() as st:
        sb = lambda name, shape, dt: st.enter_context(nc.sbuf_tensor(name, list(shape), dt))
        KT = sb("KT", (128, NH, S), BF16)
        VP = sb("VP", (128, NKB, NH, 129), BF16)
        Wkv = sb("Wkv", (128, 8, 1024), BF16)
        Wq = sb("Wq", (128, 8, 512), BF16)
        Wg = sb("Wg", (128, 8, 2048), BF16)
        Wo = sb("Wo", (128, 8, 1024), BF16)
        xt = [sb("xt%d" % i, (128, D), F32) for i in range(2)]
        et = [sb("et%d" % i, (128, 1024), BF16) for i in range(2)]
        hT = sb("hT", (128, 8, QB), BF16)
        QT = sb("QT", (128, NH, QB), BF16)
        Oacc = sb("Oacc", (128, 4, 2, 129), F32)
        G = sb("G", (128, 4, 512), F32)
        cat = sb("cat", (128, 4, 1024), BF16)
        tA = sb("tA", (128, 512), F32)
        tB = sb("tB", (128, 512), F32)
        vn = sb("vn", (128, 512), BF16)
        otmp = sb("otmp", (128, 4, 128), F32)
        scr = sb("scr", (128, 1032), F32)
        ident = sb("ident", (128, 128), BF16)
        mident = sb("mident", (128, NH, 128), BF16)
        Bhi = sb("Bhi", (128, 896), BF16)
        Blo = sb("Blo", (128, 896), BF16)
        bcol = sb("bcol_s", (128, NH * NBC), F32)
        ftab = sb("ftab_s", (128, NH * 16), F32)
        ng = sb("ng_s", (128, 8), F32)
        sublnvec = sb("sublnvec", (128, 512), F32)
        vnormg = sb("vnormg", (128, 512), F32)
        wsT = sb("wsT_s", (128, 512), BF16)
        bsb = sb("bs_s", (128, 4), F32)
        fgs = sb("fg_s", (128, D), F32)
        lqs = tA[:, 0:256]
        lprod = tA[:, 256:384]
        lsum = sb("lsum", (128, 2), F32)
        lexp = sb("lexp", (128, 2), F32)
        neglam = sb("neglam", (128, 1), F32)
        mhalf = sb("mhalf", (128, 8), F32)
        epsc = sb("epsc", (128, 8), F32)
        stat = sb("stat", (128, 64), F32)
        ps = [st.enter_context(nc.psum_tensor("ps%d" % i, [128, 2, 512], F32)) for i in range(4)]

        def bank(b):
            return ps[b // 2][:, b % 2, :]

        def bkey(b):
            return "bank%d" % b

        cnt = [0]

        def load_const(dst_ap, src_ap, key, via=None):
            if via is None:
                P.op("sp", lambda e, d=dst_ap, s=src_ap: e.dma_start(out=d, in_=s), reads=(), writes=(key,), dma="ld_c")
            else:
                stg, stg_key, width = via
                P.op("sp", lambda e, d=stg[:, 0:width], s=src_ap: e.dma_start(out=d, in_=s), reads=(), writes=(stg_key,), dma="ld_" + stg_key)
                P.op("dve", lambda e, d=dst_ap, s=stg[:, 0:width]: e.tensor_copy(out=d, in_=s), reads=(stg_key,), writes=(key,))

        load_const(ident[:], ident_d, "ident", via=(xt[0], "xt0", 128))
        load_const(mident[:].rearrange("p h k -> p (h k)"), mident_d, "mident", via=(xt[1], "xt1", 512))
        load_const(Bhi[:], bhi_d, "Bhi", via=(xt[0], "xt0", 896))
        load_const(Blo[:], blo_d, "Blo", via=(xt[1], "xt1", 896))
        load_const(bcol[:], bcol_d, "bcol")
        load_const(ftab[:], ftab_d, "ftab")
        load_const(fgs[:], fg_d, "fgs")
        P.op("dve", lambda e: e.memset(mhalf[:], -0.5), writes=("mhalf",))
        P.op("dve", lambda e: e.memset(epsc[:], float(EPS)), writes=("epsc",))
        P.op("dve", lambda e: e.memset(VP[:].rearrange("p a b c -> p (a b) c")[:, :, 128:129], 1.0), writes=("VPones",))

        rr = [0]

        def next_bank(pool):
            b = pool[rr[0] % len(pool)]
            rr[0] += 1
            return b

        conv_rr = [0]

        def convert(dst_ap, src_ap, scal_ap, rkeys, wkey):
            eng = ("dve", "act")[conv_rr[0] % 2]
            conv_rr[0] += 1
            if scal_ap is None:
                if eng == "act":
                    P.op("act", lambda e: e.copy(out=dst_ap, in_=src_ap), reads=rkeys, writes=(wkey,))
                else:
                    P.op(eng, lambda e: e.tensor_copy(out=dst_ap, in_=src_ap), reads=rkeys, writes=(wkey,))
            else:
                if eng == "act":
                    P.op("act", lambda e: e.activation(out=dst_ap, in_=src_ap, func=AF.Copy, scale=scal_ap), reads=rkeys + ("ng",), writes=(wkey,))
                else:
                    P.op(eng, lambda e: e.tensor_scalar(out=dst_ap, in0=src_ap, scalar1=scal_ap, scalar2=None, op0=ALU.mult), reads=rkeys + ("ng",), writes=(wkey,))

        stage_rr = [0]

        def layer_setup(l):
            li = lam_init_fn(l)
            P.op("sp", lambda e: e.dma_start(out=ng[:], in_=ng_d[l]), writes=("ng",), dma="ld_c")
            P.op("sp", lambda e: e.dma_start(out=sublnvec[:], in_=subg_d[l]), writes=("sublnvec",), dma="ld_c")
            P.op("sp", lambda e: e.dma_start(out=vnormg[:], in_=vng_d[l]), writes=("vnormg",), dma="ld_c")
            P.op("sp", lambda e: e.dma_start(out=bsb[:], in_=bs_d[l]), writes=("bs",), dma="ld_c")
            P.op("sp", lambda e: e.dma_start(out=tA[:], in_=wsT_d[l]), writes=("tA",), dma="ld_c")
            P.op("dve", lambda e: e.tensor_copy(out=wsT[:], in_=tA[:]), reads=("tA",), writes=("wsT",))
            P.op("dve", lambda e: e.tensor_scalar(out=sublnvec[:], in0=sublnvec[:], scalar1=float((1.0 - li) * 0.5), scalar2=None, op0=ALU.mult),
                 reads=("sublnvec",), writes=("sublnvec",))
            P.op("sp", lambda e: e.dma_start(out=lqs, in_=lq_d[l]), writes=("tA",), dma="ld_c")
            lq4 = lqs.rearrange("p (a b d) -> p a b d", a=2, b=2)
            P.op("dve", lambda e: e.tensor_tensor(out=lprod.rearrange("p (a d) -> p a d", a=2), in0=lq4[:, :, 0, :], in1=lq4[:, :, 1, :], op=ALU.mult),
                 reads=("tA",), writes=("tA",))
            P.op("dve", lambda e: e.reduce_sum(out=lsum[:], in_=lprod.rearrange("p (a d) -> p a d", a=2), axis=AX.X), reads=("tA",), writes=("lsum",))
            P.op("act", lambda e: e.activation(out=lexp[:], in_=lsum[:], func=AF.Exp), reads=("lsum",), writes=("lexp",))
            P.op("dve", lambda e: e.tensor_tensor(out=neglam[:], in0=lexp[:, 1:2], in1=lexp[:, 0:1], op=ALU.subtract), reads=("lexp",), writes=("neglam",))
            P.op("dve", lambda e: e.tensor_scalar(out=neglam[:], in0=neglam[:], scalar1=float(-li), scalar2=None, op0=ALU.add), reads=("neglam",), writes=("neglam",))
            KTs = KT[:].rearrange("p h s -> p (h s)").bitcast(F32)
            NSL = min(8, (NH * S * 2) // 4096)
            skeys = ["KTs%d" % j for j in range(NSL)]
            P.op("dve", lambda e: e.memset(stat[:, 60:61], 0.0), writes=("KT", "fence") + tuple(skeys))
            pieces = [(Wq, 0, "Wq"), (Wkv, 0, "Wkv"), (Wkv, 512, "Wkv"), (Wg, 0, "Wg"), (Wg, 512, "Wg"), (Wg, 1024, "Wg"), (Wg, 1536, "Wg")]
            nq = [0]

            def stage_load(src_ap, wdt):
                j = stage_rr[0] % NSL
                stage_rr[0] += 1
                q = ("sp", "act")[nq[0] % 2]
                nq[0] += 1
                dst = KTs[:, j * 1024:j * 1024 + wdt]
                P.op(q, lambda e: e.dma_start(out=dst, in_=src_ap), writes=(skeys[j],), dma="ld_" + skeys[j])
                return KTs[:, j * 1024:(j + 1) * 1024], skeys[j]

            def conv(dst_ap, src_ap, scal_ap, skey, wkey):
                if scal_ap is None:
                    P.op("dve", lambda e: e.tensor_copy(out=dst_ap, in_=src_ap), reads=(skey,), writes=(wkey,))
                else:
                    P.op("dve", lambda e: e.tensor_scalar(out=dst_ap, in0=src_ap, scalar1=scal_ap, scalar2=None, op0=ALU.mult), reads=(skey, "ng"), writes=(wkey,))

            for k in range(8):
                for b in range(4):
                    c0 = b * 1024
                    wdt = min(1024, DIN - c0)
                    stg, skey = stage_load(win_d[l, k * 128:(k + 1) * 128, c0:c0 + wdt], wdt)
                    for hh in range(wdt // 512):
                        dstT, dcol, dkey = pieces[(c0 // 512) + hh]
                        conv(dstT[:, k, dcol:dcol + 512], stg[:, hh * 512:(hh + 1) * 512], ng[:, k:k + 1], skey, dkey)
                stg, skey = stage_load(wout_d[l, k * 128:(k + 1) * 128, :], 1024)
                conv(Wo[:, k, :], stg, None, skey, "Wo")
            P.op("dve", lambda e: e.memset(stat[:, 61:62], 0.0), reads=tuple(skeys), writes=("KT", "fence2"))

        ALLB = list(range(8))
        xslot = [0]

        def rstd_from_ss(ss_ap, n, inv_n, skey, okey, out_ap):
            P.op("dve", lambda e: e.tensor_scalar(out=out_ap, in0=ss_ap, scalar1=float(inv_n), scalar2=float(EPS), op0=ALU.mult, op1=ALU.add),
                 reads=(skey,), writes=(okey,))
            P.op("pool", lambda e: e.tensor_tensor(out=out_ap, in0=out_ap, in1=mhalf[:, 0:n], op=ALU.pow), reads=(okey, "mhalf"), writes=(okey,))

        def xload(s, t, src):
            sl = t % 2
            xk = "xt%d" % sl
            P.op("sp", lambda e: e.dma_start(out=xt[sl][:], in_=src[s, t * 128:(t + 1) * 128, :]),
                 reads=("R%d_%d" % (s, t),), writes=(xk,), dma="ld_" + xk)

        hT2 = G[:].rearrange("p a b -> p (a b)").bitcast(BF16).rearrange("p (k t) -> p k t", k=8)
        hTb = [hT[:], hT2]
        hTkeys = [("hT",), ("G0", "G1", "G2", "G3")]

        def pass1(s, l):
            src = x_d if l == 0 else R_d

            def build_tile(c, i):
                hbuf = hTb[c % 2]
                hkeys = hTkeys[c % 2]
                t = 4 * c + i
                sl = t % 2
                xk = "xt%d" % sl
                if t < 2:
                    xload(s, t, src)
                ssk = "ss%d" % (i % 2)
                ssa = stat[:, (i % 2):(i % 2) + 1]
                rsa = stat[:, 2 + (i % 2):3 + (i % 2)]
                rsk = "rs%d" % (i % 2)
                P.op("act", lambda e: e.activation(out=scr[:, 0:512], in_=xt[sl][:, 0:512], func=AF.Square, accum_out=ssa),
                     reads=(xk,), writes=("scr0", "scr1", ssk))
                ssa2 = stat[:, 4 + (i % 2):5 + (i % 2)]
                ssk2 = "ssb%d" % (i % 2)
                P.op("act", lambda e: e.activation(out=scr[:, 0:512], in_=xt[sl][:, 512:1024], func=AF.Square, accum_out=ssa2),
                     reads=(xk,), writes=("scr0", "scr1", ssk2))
                P.op("dve", lambda e: e.tensor_tensor(out=ssa, in0=ssa, in1=ssa2, op=ALU.add), reads=(ssk, ssk2), writes=(ssk,))
                rstd_from_ss(ssa, 1, 1.0 / D, ssk, rsk, rsa)
                hb = et[i % 2]
                hk = "et%d" % (i % 2)
                P.op("dve", lambda e: e.tensor_scalar(out=hb[:], in0=xt[sl][:], scalar1=rsa, scalar2=None, op0=ALU.mult),
                     reads=(xk, rsk), writes=(hk,))
                if t + 2 < NT:
                    xload(s, t + 2, src)

            def trans_tile(c, i):
                hbuf = hTb[c % 2]
                hkeys = hTkeys[c % 2]
                hb = et[i % 2]
                hk = "et%d" % (i % 2)
                b = next_bank(ALLB)
                pT = bank(b).bitcast(BF16)
                for k in range(8):
                    P.op("pe", lambda e, k=k: e.transpose(out=pT[:, k * 128:(k + 1) * 128], in_=hb[:, k * 128:(k + 1) * 128], identity=ident[:]),
                         reads=(hk, "ident"), writes=(bkey(b),))
                P.op("act", lambda e: e.copy(out=hbuf[:, :, i * 128:(i + 1) * 128], in_=pT.rearrange("p (k t) -> p k t", k=8)),
                     reads=(bkey(b),), writes=hkeys)

            def proj_group(c, g):
                hbuf = hTb[c % 2]
                hkeys = hTkeys[c % 2]
                b = next_bank(ALLB)
                if g < 4:
                    h = g
                    for k in range(8):
                        P.op("pe", lambda e, k=k: e.matmul(out=bank(b), lhsT=Wkv[:, k, h * 128:(h + 1) * 128], rhs=hbuf[:, k, :], start=(k == 0), stop=(k == 7)),
                             reads=("Wkv",) + hkeys, writes=(bkey(b),))
                    P.op("act", lambda e: e.copy(out=KT[:, h, c * QB:(c + 1) * QB], in_=bank(b)), reads=(bkey(b),), writes=("KT",))
                else:
                    i = g - 4
                    for k in range(8):
                        P.op("pe", lambda e, k=k: e.matmul(out=bank(b), lhsT=hbuf[:, k, i * 128:(i + 1) * 128], rhs=Wkv[:, k, 512:1024], start=(k == 0), stop=(k == 7)),
                             reads=("Wkv",) + hkeys, writes=(bkey(b),))
                    P.op("dve", lambda e: e.tensor_copy(out=VP[:, 4 * c + i, :, 0:128], in_=bank(b).rearrange("p (h d) -> p h d", h=NH)),
                         reads=(bkey(b),), writes=("VP",))

            def tile_ci(t):
                return (t // 4, t % 4)

            build_tile(0, 0)
            build_tile(0, 1)
            for t in range(4):
                trans_tile(0, t)
                if t + 2 < NT:
                    build_tile(*tile_ci(t + 2))
            for c in range(NCH):
                for i in range(4):
                    proj_group(c, 2 * i)
                    proj_group(c, 2 * i + 1)
                    if c + 1 < NCH:
                        t = 4 * (c + 1) + i
                        trans_tile(c + 1, i)
                        if t + 2 < NT:
                            build_tile(*tile_ci(t + 2))
                hbuf = hTb[c % 2]
                P.op("pool", lambda e, c=c, hbuf=hbuf: e.dma_start(out=hT_d[s, c], in_=hbuf.rearrange("p k t -> p (k t)")), reads=hTkeys[c % 2], writes=("hTd%d_%d" % (s, c),), dma="st_hT")

        MISC = [7, 0, 1, 2, 3]
        sbuf_tog = [0]
        et_tog = [0]
        et3_tog = [0]

        def pass2(s, l, is_last):
            src = x_d if l == 0 else R_d
            xload(s, 0, src)
            xload(s, 1, src)
            P.op("sp", lambda e: e.dma_start(out=hT[:].rearrange("p k t -> p (k t)"), in_=hT_d[s, 0]), reads=("hTd%d_%d" % (s, 0),), writes=("hT",), dma="ld_hT")
            def qproj():
                for h in range(NH):
                    b = next_bank(MISC)
                    for k in range(8):
                        P.op("pe", lambda e, b=b, k=k, h=h: e.matmul(out=bank(b), lhsT=Wq[:, k, h * 128:(h + 1) * 128], rhs=hT[:, k, :], start=(k == 0), stop=(k == 7)),
                             reads=("Wq", "hT"), writes=(bkey(b),))
                    P.op("act", lambda e, b=b, h=h: e.activation(out=QT[:, h, :], in_=bank(b), func=AF.Copy, scale=0.125), reads=(bkey(b),), writes=("QT",))

            qproj()
            for c in range(NCH):
                side_tasks = []

                def mm_actions(i, c0):
                    tok = slice(i * 128, (i + 1) * 128)
                    acts = []
                    for k in range(8):
                        acts.append(lambda k=k, tok=tok, c0=c0: P.op("pe", lambda e: e.matmul(out=bank(7), lhsT=hT[:, k, tok], rhs=Wg[:, k, c0:c0 + 512], start=(k == 0), stop=(k == 7)),
                                                                     reads=("Wg", "hT"), writes=(bkey(7),)))
                    return acts

                def task_gate(i):
                    a0 = lambda: P.op("act", lambda e: e.copy(out=tA[:], in_=bank(7)), reads=(bkey(7),), writes=("tA",))
                    a1 = lambda: P.op("act", lambda e: e.activation(out=G[:, i, :], in_=tA[:], func=AF.Tanh, scale=0.5), reads=("tA",), writes=("G%d" % i,))
                    d1 = lambda: P.op("dve", lambda e: e.scalar_tensor_tensor(out=G[:, i, :], in0=G[:, i, :], scalar=1.0, in1=tA[:], op0=ALU.add, op1=ALU.mult),
                                      reads=("tA", "G%d" % i), writes=("G%d" % i,))
                    d2 = lambda: P.op("pool", lambda e: e.tensor_tensor(out=G[:, i, :], in0=G[:, i, :], in1=sublnvec[:], op=ALU.mult), reads=("G%d" % i, "sublnvec"), writes=("G%d" % i,))
                    return (mm_actions(i, 0), [[a0, a1], [d1, d2]])

                def task_gm(i):
                    a0 = lambda: P.op("act", lambda e: e.copy(out=tB[:], in_=bank(7)), reads=(bkey(7),), writes=("tB",))
                    a1 = lambda: P.op("act", lambda e: e.activation(out=tA[:], in_=tB[:], func=AF.Tanh, scale=0.5), reads=("tB",), writes=("tA",))
                    d1 = lambda: P.op("dve", lambda e: e.scalar_tensor_tensor(out=tB[:], in0=tA[:], scalar=1.0, in1=tB[:], op0=ALU.add, op1=ALU.mult),
                                      reads=("tA", "tB"), writes=("tB",))
                    return (mm_actions(i, 1536), [[a0, a1], [d1]])

                def task_u(i):
                    a0 = lambda: P.op("act", lambda e: e.activation(out=tA[:], in_=bank(7), func=AF.Copy, scale=0.5), reads=(bkey(7),), writes=("tA",))
                    d1 = lambda: P.op("dve", lambda e: e.tensor_tensor(out=tB[:], in0=tB[:], in1=tA[:], op=ALU.mult), reads=("tA", "tB"), writes=("tB",))
                    return (mm_actions(i, 512), [[a0], [d1]])

                def task_vg(i):
                    ssv = stat[:, 8:9]
                    rsv = stat[:, 9:10]
                    a0 = lambda: P.op("act", lambda e: e.copy(out=tA[:], in_=bank(7)), reads=(bkey(7),), writes=("tA",))
                    a1 = lambda: P.op("act", lambda e: e.activation(out=vn[:], in_=tA[:], func=AF.Square, scale=float(512.0 ** -0.5), accum_out=ssv), reads=("tA",), writes=("vn", "ssv"))

                    def p1():
                        P.op("pool", lambda e: e.tensor_tensor(out=rsv, in0=ssv, in1=epsc[:, 0:1], op=ALU.add), reads=("ssv", "epsc"), writes=("rsv",))
                        P.op("pool", lambda e: e.tensor_tensor(out=rsv, in0=rsv, in1=mhalf[:, 0:1], op=ALU.pow), reads=("rsv", "mhalf"), writes=("rsv",))
                        P.op("pool", lambda e: e.tensor_tensor(out=tA[:], in0=tA[:], in1=rsv.to_broadcast([128, 512]), op=ALU.mult), reads=("tA", "rsv"), writes=("tA",))
                        P.op("pool", lambda e: e.tensor_tensor(out=vn[:], in0=tA[:], in1=vnormg[:], op=ALU.mult), reads=("tA", "vnormg"), writes=("vn",))
                    return (mm_actions(i, 1024), [[a0, a1], [p1], [], []])

                def task_sv(i):
                    mm = []
                    for g in range(4):
                        mm.append(lambda g=g: P.op("pe", lambda e: e.matmul(out=bank(7)[:, g * 128:(g + 1) * 128], lhsT=wsT[:, g * 128:(g + 1) * 128], rhs=vn[:, g * 128:(g + 1) * 128],
                                                                            start=True, stop=True, skip_group_check=True),
                                                   reads=("wsT", "vn"), writes=(bkey(7),)))
                    a0 = lambda: P.op("act", lambda e: e.copy(out=tA[:], in_=bank(7)), reads=(bkey(7),), writes=("tA",))
                    d1 = lambda: P.op("dve", lambda e: e.tensor_tensor(out=tA[:].rearrange("p (g d) -> p g d", g=4), in0=tA[:].rearrange("p (g d) -> p g d", g=4),
                                                                        in1=bsb[:, 0:4].unsqueeze(2).to_broadcast([128, 4, 128]), op=ALU.add),
                                      reads=("tA", "bs"), writes=("tA",))
                    d2 = lambda: P.op("dve", lambda e: e.tensor_tensor(out=cat[:, i, 512:1024], in0=tA[:], in1=tB[:], op=ALU.mult), reads=("tA", "tB"), writes=("cat%d" % i,))
                    return (mm, [[a0], [d1, d2]])

                def pgroup(i, c0):
                    b_ = next_bank(MISC)
                    tok = slice(i * 128, (i + 1) * 128)
                    for k in range(8):
                        P.op("pe", lambda e, k=k: e.matmul(out=bank(b_), lhsT=hT[:, k, tok], rhs=Wg[:, k, c0:c0 + 512], start=(k == 0), stop=(k == 7)),
                             reads=("Wg", "hT"), writes=(bkey(b_),))
                    return b_

                for i in range(4):
                    ssv = stat[:, 8 + 2 * (i % 2):9 + 2 * (i % 2)]
                    rsv = stat[:, 9 + 2 * (i % 2):10 + 2 * (i % 2)]
                    ssvk = "ssv%d" % (i % 2)
                    rsvk = "rsv%d" % (i % 2)
                    tG = tA if i % 2 == 0 else otmp[:].rearrange("p a b -> p (a b)")
                    tGk = "tA" if i % 2 == 0 else "otmp"
                    tGa = tA[:] if i % 2 == 0 else otmp[:].rearrange("p a b -> p (a b)")
                    bv = pgroup(i, 1024)
                    P.op("act", lambda e, bv=bv, ssv=ssv: e.activation(out=scr[:, 512:1024], in_=bank(bv), func=AF.Square, scale=float(512.0 ** -0.5), accum_out=ssv),
                         reads=(bkey(bv),), writes=("scr1", "scr2", ssvk))
                    P.op("pool", lambda e, ssv=ssv, rsv=rsv: e.tensor_tensor(out=rsv, in0=ssv, in1=epsc[:, 0:1], op=ALU.add), reads=(ssvk, "epsc"), writes=(rsvk,))
                    P.op("pool", lambda e, rsv=rsv: e.tensor_tensor(out=rsv, in0=rsv, in1=mhalf[:, 0:1], op=ALU.pow), reads=(rsvk, "mhalf"), writes=(rsvk,))
                    bg = pgroup(i, 0)
                    P.op("act", lambda e, bg=bg, tGa=tGa: e.activation(out=tGa, in_=bank(bg), func=AF.Tanh, scale=0.5), reads=(bkey(bg),), writes=(tGk,))
                    P.op("dve", lambda e, bv=bv, rsv=rsv: e.scalar_tensor_tensor(out=vn[:], in0=bank(bv), scalar=rsv, in1=vnormg[:], op0=ALU.mult, op1=ALU.mult),
                         reads=(bkey(bv), rsvk, "vnormg"), writes=("vn",))
                    P.op("dve", lambda e, bg=bg, i=i, tGa=tGa: e.scalar_tensor_tensor(out=G[:, i, :], in0=tGa, scalar=1.0, in1=bank(bg), op0=ALU.add, op1=ALU.mult),
                         reads=(tGk, bkey(bg)), writes=("G%d" % i,))
                    P.op("pool", lambda e, i=i: e.tensor_tensor(out=G[:, i, :], in0=G[:, i, :], in1=sublnvec[:], op=ALU.mult), reads=("G%d" % i, "sublnvec"), writes=("G%d" % i,))
                    bgm = pgroup(i, 1536)
                    P.op("act", lambda e, bgm=bgm: e.activation(out=tB[:], in_=bank(bgm), func=AF.Tanh, scale=0.5), reads=(bkey(bgm),), writes=("tB",))
                    P.op("dve", lambda e, bgm=bgm: e.scalar_tensor_tensor(out=tB[:], in0=tB[:], scalar=1.0, in1=bank(bgm), op0=ALU.add, op1=ALU.mult),
                         reads=("tB", bkey(bgm)), writes=("tB",))
                    bu = pgroup(i, 512)
                    P.op("dve", lambda e, bu=bu: e.scalar_tensor_tensor(out=tB[:], in0=tB[:], scalar=0.5, in1=bank(bu), op0=ALU.mult, op1=ALU.mult),
                         reads=("tB", bkey(bu)), writes=("tB",))
                    bsv = next_bank(MISC)
                    for g in range(4):
                        P.op("pe", lambda e, bsv=bsv, g=g: e.matmul(out=bank(bsv)[:, g * 128:(g + 1) * 128], lhsT=wsT[:, g * 128:(g + 1) * 128], rhs=vn[:, g * 128:(g + 1) * 128],
                                                                    start=True, stop=True, skip_group_check=True),
                             reads=("wsT", "vn"), writes=(bkey(bsv),))
                    for g in range(4):
                        P.op("dve", lambda e, bsv=bsv, g=g, i=i: e.scalar_tensor_tensor(out=cat[:, i, 512 + g * 128:512 + (g + 1) * 128], in0=bank(bsv)[:, g * 128:(g + 1) * 128],
                                                                                     scalar=bsb[:, g:g + 1], in1=tB[:, g * 128:(g + 1) * 128], op0=ALU.add, op1=ALU.mult),
                             reads=(bkey(bsv), "bs", "tB"), writes=("cat%d" % i,))
                if c + 1 < NCH:
                    P.op("sp", lambda e, c=c: e.dma_start(out=hT[:].rearrange("p k t -> p (k t)"), in_=hT_d[s, c + 1]), reads=("hTd%d_%d" % (s, c + 1),), writes=("hT",), dma="ld_hT")

                def acc_ap(idx):
                    bnk = 4 + idx // 3
                    col = (idx % 3) * 129
                    return bank(bnk)[:, col:col + 129], bnk

                steps = []
                for h in HEAD_ORDER:
                    lst = []
                    for kb in range(NKB):
                        if kb < 4 * c:
                            if 4 * c - kb <= KEEP[h]:
                                lst.append((kb, "b"))
                        elif kb < 4 * c + 4:
                            lst.append((kb, "i"))
                        else:
                            if kb - 4 * c - 3 <= KEEP[h]:
                                lst.append((kb, "a"))
                    lst = [x for x in lst if x[1] == "a"] + [x for x in lst if x[1] == "b"] + [x for x in lst if x[1] == "i"]
                    for n_, (kb, k) in enumerate(lst):
                        first = (n_ == 0) or (lst[n_ - 1][1] != k)
                        last = (n_ == len(lst) - 1) or (lst[n_ + 1][1] != k)
                        steps.append((h, kb, k, first, last, n_ == len(lst) - 1, first and n_ == 0))

                def emit_qk(h, kb, k):
                    X = sbuf_tog[0] % 2
                    sbuf_tog[0] += 1
                    inchunk = k == "i"
                    for m in range(2):
                        pr = slice(64 * m, 64 * m + 64)
                        P.op("pe", lambda e, X=X, m=m, pr=pr, h=h, kb=kb, inchunk=inchunk: e.matmul(out=ps[X][:, m, :], lhsT=KT[pr, h, kb * 128:(kb + 1) * 128], rhs=QT[pr, h, :],
                                                                                                  start=True, stop=(not inchunk), skip_group_check=True),
                             reads=("KT", "QT"), writes=(bkey(2 * X + m),))
                    if inchunk:
                        j = kb - 4 * c
                        off = 384 - 128 * j
                        c0, c1 = [(256, 512), (384, 512), (0, 128), (0, 256)][j]
                        for m in range(2):
                            P.op("pe", lambda e, X=X, m=m, h=h, off=off: e.matmul(out=ps[X][:, m, :], lhsT=mident[:, h, :], rhs=Bhi[:, off:off + 512], start=False, stop=False, skip_group_check=True),
                                 reads=("mident", "Bhi"), writes=(bkey(2 * X + m),))
                            P.op("pe", lambda e, X=X, m=m, h=h, off=off, c0=c0, c1=c1: e.matmul(out=ps[X][:, m, c0:c1], lhsT=mident[:, h, :], rhs=Blo[:, off + c0:off + c1], start=False, stop=True, skip_group_check=True),
                                 reads=("mident", "Blo"), writes=(bkey(2 * X + m),))
                    return X

                ETA = [et[0][:], et[1][:], tB[:].bitcast(BF16)]
                ETK = ["et0", "et1", "tB"]

                def emit_act(h, kb, k, X):
                    E = et3_tog[0] % 3
                    et3_tog[0] += 1
                    if k == "b":
                        idx = (4 * c - kb) - 1
                    elif k == "a":
                        idx = NBA + (kb - 4 * c - 4)
                    else:
                        idx = 2 * NBA
                    col = h * NBC + idx
                    P.op("act", lambda e, X=X, E=E, col=col: e.activation(out=ETA[E], in_=ps[X][:].rearrange("p a b -> p (a b)"), func=AF.Exp, bias=bcol[:, col:col + 1], scale=1.0),
                         reads=(bkey(2 * X), bkey(2 * X + 1), "bcol"), writes=(ETK[E],))
                    return E

                def emit_pv(h, kb, E, first):
                    for idx in range(8):
                        i, m = idx // 2, idx % 2
                        ap, bnk = acc_ap(idx)
                        stflag = first and (idx % 3 == 0)
                        P.op("pe", lambda e, ap=ap, E=E, m=m, i=i, kb=kb, h=h, stflag=stflag: e.matmul(out=ap, lhsT=ETA[E][:, m * 512 + i * 128:m * 512 + (i + 1) * 128], rhs=VP[:, kb, h, :],
                                                                                                     start=stflag, stop=False, skip_group_check=True),
                             reads=(ETK[E], "VP", "VPones"), writes=(bkey(bnk),))

                Oav = Oacc[:].rearrange("p i m e -> p (i m) e")
                scrv = scr[:, 0:1032].rearrange("p (a e) -> p a e", e=129)

                def emit_phase_evac(h, k, firstphase):
                    for g, (a0, a1) in enumerate([(0, 3), (3, 6), (6, 8)]):
                        bnk = 4 + g
                        n_ = a1 - a0
                        src_ = bank(bnk)[:, 0:n_ * 129].rearrange("p (a e) -> p a e", e=129)
                        dst = Oav[:, a0:a1, :]
                        if k == "i":
                            if firstphase:
                                P.op("dve", lambda e, src_=src_, dst=dst: e.tensor_copy(out=dst, in_=src_), reads=(bkey(bnk),), writes=("Oacc%d" % g,))
                            else:
                                P.op("dve", lambda e, src_=src_, dst=dst: e.tensor_tensor(out=dst, in0=src_, in1=dst, op=ALU.add), reads=(bkey(bnk), "Oacc%d" % g), writes=("Oacc%d" % g,))
                        else:
                            f0 = (h * 2 + (0 if k == "b" else 1)) * 8 + a0
                            fb = ftab[:, f0:f0 + n_].unsqueeze(2).to_broadcast([128, n_, 129])
                            if firstphase:
                                P.op("dve", lambda e, src_=src_, dst=dst, fb=fb: e.tensor_tensor(out=dst, in0=src_, in1=fb, op=ALU.mult),
                                     reads=(bkey(bnk), "ftab"), writes=("Oacc%d" % g,))
                            else:
                                P.op("dve", lambda e, src_=src_, a0=a0, a1=a1, fb=fb: e.tensor_tensor(out=scrv[:, a0:a1, :], in0=src_, in1=fb, op=ALU.mult),
                                     reads=(bkey(bnk), "ftab"), writes=("scr%d" % g,))
                    if k != "i" and not firstphase:
                        for g, (a0, a1) in enumerate([(0, 3), (3, 6), (6, 8)]):
                            P.op("dve", lambda e, a0=a0, a1=a1: e.tensor_tensor(out=Oav[:, a0:a1, :], in0=Oav[:, a0:a1, :], in1=scrv[:, a0:a1, :], op=ALU.add),
                                 reads=("Oacc%d" % g, "scr%d" % g), writes=("Oacc%d" % g,))

                rr_ = stat[:, 16:24].rearrange("p (i m) -> p i m", i=4)
                c1 = stat[:, 24:28]
                ssq = stat[:, 28:32]
                rso = stat[:, 32:36]
                scrA = tA[:].rearrange("p (a e) -> p a e", e=128)
                scrB = tA[:].rearrange("p (a e) -> p a e", e=128)

                def combine_stages(h):
                    def s0():
                        P.op("dve", lambda e: e.reciprocal(out=rr_, in_=Oacc[:, :, :, 128]), reads=("Oacc0", "Oacc1", "Oacc2",), writes=("rr",))
                        P.op("dve", lambda e: e.tensor_scalar(out=c1, in0=rr_[:, :, 1], scalar1=neglam[:, 0:1], scalar2=None, op0=ALU.mult), reads=("rr", "neglam"), writes=("c1",))
                        P.op("dve", lambda e: e.tensor_tensor(out=otmp[:], in0=Oacc[:, :, 0, 0:128], in1=rr_[:, :, 0:1].to_broadcast([128, 4, 128]), op=ALU.mult),
                             reads=("Oacc0", "Oacc1", "Oacc2", "rr"), writes=("otmp",))
                        P.op("dve", lambda e: e.tensor_tensor(out=scrB, in0=Oacc[:, :, 1, 0:128], in1=c1.unsqueeze(2).to_broadcast([128, 4, 128]), op=ALU.mult),
                             reads=("Oacc0", "Oacc1", "Oacc2", "c1"), writes=("tA",))

                    def s1():
                        P.op("dve", lambda e: e.tensor_tensor(out=otmp[:], in0=otmp[:], in1=scrB, op=ALU.add), reads=("otmp", "tA"), writes=("otmp",))
                        P.op("pool", lambda e: e.tensor_tensor(out=scrA, in0=otmp[:], in1=otmp[:], op=ALU.mult), reads=("otmp",), writes=("tA",))

                    def s2():
                        P.op("dve", lambda e: e.reduce_sum(out=ssq, in_=scrA, axis=AX.X), reads=("tA",), writes=("ssq",))
                        rstd_from_ss(ssq, 4, 1.0 / 128, "ssq", "rso", rso)

                    def s3():
                        P.op("dve", lambda e: e.tensor_tensor(out=otmp[:], in0=otmp[:], in1=rso.unsqueeze(2).to_broadcast([128, 4, 128]), op=ALU.mult),
                             reads=("otmp", "rso"), writes=("otmp",))
                        P.op("dve", lambda e: e.tensor_tensor(out=cat[:, :, h * 128:(h + 1) * 128], in0=otmp[:], in1=G[:, :, h * 128:(h + 1) * 128], op=ALU.mult),
                             reads=("otmp", "G0", "G1", "G2", "G3"), writes=("cat0", "cat1", "cat2", "cat3"))
                    return [s0, s1, None, s2, None, s3]

                nst = len(steps)

                def make_sched(mps):
                    sched = {}
                    pos = 0
                    prev_last = -1
                    for (mm, posts) in side_tasks:
                        if len(mm) <= 4:
                            pos = max(pos, prev_last + 1)
                        nmm = (len(mm) + mps - 1) // mps if len(mm) > 4 else 1
                        for q in range(nmm):
                            chunk_ = mm[q * mps:(q + 1) * mps] if len(mm) > 4 else mm
                            sched.setdefault(pos + q, []).extend(chunk_)
                        pstart = max(pos + nmm, prev_last)
                        for q, pa in enumerate(posts):
                            sched.setdefault(pstart + q, []).extend(pa)
                        prev_last = pstart + len(posts) - 1
                        pos = max(pos + nmm + 1, pstart + 1)
                    return sched, prev_last + 2

                for mps in (2, 3, 4, 8):
                    sched, need = make_sched(mps)
                    if need <= nst - 1:
                        break
                cq = []
                Xs = {}
                for q_ in range(min(2, nst)):
                    Xs[q_] = emit_qk(steps[q_][0], steps[q_][1], steps[q_][2])
                nphase_done = {}
                for si, (h, kb, k, first, last, lasthead, firsthead) in enumerate(steps):
                    X = Xs.pop(si)
                    E = emit_act(h, kb, k, X)
                    if si + 2 < nst:
                        Xs[si + 2] = emit_qk(steps[si + 2][0], steps[si + 2][1], steps[si + 2][2])
                    emit_pv(h, kb, E, first)
                    if last:
                        firstphase = h not in nphase_done
                        nphase_done[h] = True
                        if cq and not lasthead:
                            f_ = cq.pop(0)
                            if f_ is not None:
                                f_()
                        emit_phase_evac(h, k, firstphase)
                        if lasthead:
                            while cq:
                                f_ = cq.pop(0)
                                if f_ is not None:
                                    f_()
                            st_ = combine_stages(h)
                            st_[0]()
                            cq = st_[1:]
                    elif cq:
                        f_ = cq.pop(0)
                        if f_ is not None:
                            f_()
                    for fn_ in sched.pop(si - 1, ()):
                        fn_()
                for key_ in sorted(sched.keys()):
                    for fn_ in sched[key_]:
                        fn_()
                while cq:
                    f_ = cq.pop(0)
                    if f_ is not None:
                        f_()
                if c + 1 < NCH:
                    qproj()
                tinfo = {}

                def tail_T(i):
                    b = next_bank(MISC)
                    pT = bank(b).bitcast(BF16)
                    E = et_tog[0] % 2
                    et_tog[0] += 1
                    ek = "et%d" % E
                    for k in range(8):
                        P.op("pe", lambda e, pT=pT, i=i, k=k: e.transpose(out=pT[:, k * 128:(k + 1) * 128], in_=cat[:, i, k * 128:(k + 1) * 128], identity=ident[:]),
                             reads=("cat%d" % i, "ident"), writes=(bkey(b),))
                    P.op("dve", lambda e, pT=pT, E=E: e.tensor_copy(out=et[E][:], in_=pT), reads=(bkey(b),), writes=(ek,))
                    tinfo[i] = (E, ek)

                def tail_M(i):
                    E, ek = tinfo[i]
                    t = 4 * c + i
                    sl = t % 2
                    xk = "xt%d" % sl
                    for half in range(2):
                        bo = next_bank(MISC)
                        for k in range(8):
                            P.op("pe", lambda e, bo=bo, E=E, k=k, half=half: e.matmul(out=bank(bo), lhsT=et[E][:, k * 128:(k + 1) * 128], rhs=Wo[:, k, half * 512:(half + 1) * 512],
                                                                                    start=(k == 0), stop=(k == 7)),
                                 reads=(ek, "Wo"), writes=(bkey(bo),))
                        P.op("dve", lambda e, bo=bo, sl=sl, half=half: e.tensor_tensor(out=xt[sl][:, half * 512:(half + 1) * 512], in0=bank(bo), in1=xt[sl][:, half * 512:(half + 1) * 512], op=ALU.add),
                             reads=(bkey(bo), xk), writes=(xk,))
                    if is_last:
                        ssf = stat[:, 40:41]
                        ssf2 = stat[:, 41:42]
                        rsf = stat[:, 42:43]
                        P.op("act", lambda e, sl=sl: e.activation(out=tA[:], in_=xt[sl][:, 0:512], func=AF.Square, accum_out=ssf), reads=(xk,), writes=("tA", "ssf"))
                        P.op("act", lambda e, sl=sl: e.activation(out=tA[:], in_=xt[sl][:, 512:1024], func=AF.Square, accum_out=ssf2), reads=(xk,), writes=("tA", "ssf2"))
                        P.op("dve", lambda e: e.tensor_tensor(out=ssf, in0=ssf, in1=ssf2, op=ALU.add), reads=("ssf", "ssf2"), writes=("ssf",))
                        rstd_from_ss(ssf, 1, 1.0 / D, "ssf", "rsf", rsf)
                        P.op("dve", lambda e, sl=sl: e.scalar_tensor_tensor(out=xt[sl][:], in0=xt[sl][:], scalar=rsf, in1=fgs[:], op0=ALU.mult, op1=ALU.mult),
                             reads=(xk, "rsf", "fgs"), writes=(xk,))
                        P.op("pool", lambda e, sl=sl, t=t: e.dma_start(out=y_d[s, t * 128:(t + 1) * 128, :], in_=xt[sl][:]), reads=(xk,), writes=("y%d_%d" % (s, t),), dma="st_" + xk)
                    else:
                        P.op("pool", lambda e, sl=sl, t=t: e.dma_start(out=R_d[s, t * 128:(t + 1) * 128, :], in_=xt[sl][:]), reads=(xk,), writes=("R%d_%d" % (s, t),), dma="st_" + xk)
                    if t + 2 < NT:
                        xload(s, t + 2, src)

                tail_T(0)
                tail_T(1)
                tail_M(0)
                tail_T(2)
                tail_M(1)
                tail_T(3)
                tail_M(2)
                tail_M(3)

        for l in range(L):
            layer_setup(l)
            for s in range(NSEQ):
                pass1(s, l)
                pass2(s, l, (l == L - 1) and last_is_final)
        counts = P.build(nc, st)
    return nc, counts


def make_consts(S):
    NKB = S // 128
    NBA = max(NKB - 4, 1)
    NBC = 2 * NBA + 1
    p = np.arange(128, dtype=np.float64)
    ident = np.eye(128, dtype=np.float32)
    mident = np.zeros((128, NH, 128), np.float32)
    for h in range(NH):
        mident[:, h, :] = np.eye(128) * SLOPES[h]
    y = np.arange(896, dtype=np.float64)
    d = np.abs(y[None, :] - 384.0 - p[:, None])
    bhi = (-np.minimum(d, 256.0)).astype(np.float32)
    blo = (-np.maximum(d - 256.0, 0.0)).astype(np.float32)
    bcol = np.zeros((128, NH, NBC), np.float64)
    for h in range(NH):
        m = SLOPES[h]
        for dl in range(1, NBA + 1):
            bcol[:, h, dl - 1] = -m * (128.0 * dl - p)
        for dl in range(0, NBA):
            bcol[:, h, NBA + dl] = -m * (128.0 * dl + p + 1.0)
    ftab = np.zeros((128, NH, 2, 8), np.float64)
    for h in range(NH):
        m = SLOPES[h]
        for i in range(4):
            for mm in range(2):
                ftab[:, h, 0, 2 * i + mm] = np.exp(-m * (128.0 * i + p))
                ftab[:, h, 1, 2 * i + mm] = np.exp(-m * (511.0 - 128.0 * i - p))
    return dict(identf=ident, midentf=mident.reshape(128, NH * 128), bhif=bhi, blof=blo,
                bcol=bcol.reshape(128, NH * NBC).astype(np.float32), ftab=ftab.reshape(128, NH * 16).astype(np.float32))


def make_params(norm_g, w_in, lambda_qk, subln_g, vnorm_g, w_s, b_s, w_out, final_g):
    L = w_in.shape[0]
    f = np.float32
    c = np.ascontiguousarray
    d = {}
    d["w_in"] = c(w_in, dtype=f)
    d["w_out"] = c(w_out, dtype=f)
    d["ng"] = c(np.asarray(norm_g, f).reshape(L, 8, 128).transpose(0, 2, 1))
    d["lq"] = c(np.broadcast_to(np.asarray(lambda_qk, f).reshape(L, 1, 256), (L, 128, 256)))
    d["subg"] = c(np.broadcast_to(np.tile(np.asarray(subln_g, f), (1, 4)).reshape(L, 1, 512), (L, 128, 512)))
    d["vng"] = c(np.broadcast_to(np.asarray(vnorm_g, f).reshape(L, 1, 512), (L, 128, 512)))
    d["wsT"] = c(np.asarray(w_s, f).transpose(0, 3, 1, 2).reshape(L, 128, 512))
    d["bs"] = c(np.asarray(b_s, f).transpose(0, 2, 1))
    d["fg"] = c(np.broadcast_to(np.asarray(final_g, f).reshape(1, D), (128, D)))
    return d


_CACHE = {}


def run(xs_per_core, params, S, L, n_cores, trace=False):
    NSEQ = xs_per_core[0].shape[0]
    key = (NSEQ, S, L)
    if key not in _CACHE:
        _CACHE[key] = build_nc(NSEQ, S, L)
    nc, counts = _CACHE[key]
    consts = make_consts(S)
    in_maps = []
    for ci in range(n_cores):
        m = {"x": np.ascontiguousarray(xs_per_core[ci], dtype=np.float32)}
        m.update(params)
        m.update(consts)
        in_maps.append(m)
    res = run_bass_kernel_spmd(nc, in_maps, core_ids=list(range(n_cores)))
    return [r["y"] for r in res.results]


def kernel(x_prompt, x_sample, norm_g, w_in, lambda_qk, subln_g, vnorm_g, w_s, b_s, w_out, final_g):
    x_prompt = np.asarray(x_prompt, np.float32)
    x_sample = np.asarray(x_sample, np.float32)
    S = x_prompt.shape[1]
    L = np.asarray(w_in).shape[0]
    allx = np.concatenate([x_prompt, x_sample], axis=0)
    n_cores = 8
    per = allx.shape[0] // n_cores
    xs = [allx[i * per:(i + 1) * per] for i in range(n_cores)]
    params = make_params(np.asarray(norm_g), np.asarray(w_in), np.asarray(lambda_qk), np.asarray(subln_g), np.asarray(vnorm_g),
                         np.asarray(w_s), np.asarray(b_s), np.asarray(w_out), np.asarray(final_g))
    ys = run(xs, params, S, L, n_cores)
    ally = np.concatenate(ys, axis=0)
    nb = x_prompt.shape[0]
    return (np.ascontiguousarray(ally[:nb]), np.ascontiguousarray(ally[nb:]))
```

```python
import math
from contextlib import ExitStack

import numpy as np
import concourse.bass as bass
import concourse.mybir as mybir
from concourse.bass_utils import run_bass_kernel_spmd

F32 = mybir.dt.float32
BF16 = mybir.dt.bfloat16
ALU = mybir.AluOpType
AF = mybir.ActivationFunctionType
AX = mybir.AxisListType

D = 1024
DIN = 3584
NH = 4
EPS = 1e-6
QB = 512
SLOPES = [2.0 ** (-2.0 * (i + 1)) for i in range(NH)]
THRESH = 60.0
KEEP = [int(math.floor((THRESH / m - 1.0) / 128.0 - 1e-9)) + 1 for m in SLOPES]
HEAD_ORDER = [3, 2, 1, 0]


def lam_init_fn(l):
    return 0.8 - 0.6 * math.exp(-0.3 * l)


class Prog:
    EPOCH = 12000

    def __init__(self):
        self.ins = []

    def op(self, eng, fn, reads=(), writes=(), dma=None):
        self.ins.append([eng, fn, tuple(reads), tuple(writes), dma, None, False])

    def build(self, nc, stack, final_wait_prefix=("st_",)):
        ins = self.ins
        n = len(ins)
        last_w = {}
        rd_eng = {}
        rd_dma = {}
        deps_all = [None] * n
        for i in range(n):
            eng, fn, reads, writes, dma, _, _ = ins[i]
            deps = {}
            for k in reads:
                j = last_w.get(k)
                if j is not None:
                    deps[j] = True
            for k in writes:
                j = last_w.get(k)
                if j is not None and j not in deps:
                    deps[j] = True
                for j in rd_eng.get(k, {}).values():
                    if j != i and j not in deps:
                        deps[j] = False
                for j in rd_dma.get(k, ()):
                    if j not in deps:
                        deps[j] = False
            for k in reads:
                if dma is not None:
                    rd_dma.setdefault(k, []).append(i)
                else:
                    rd_eng.setdefault(k, {})[eng] = i
            for k in writes:
                last_w[k] = i
                rd_eng[k] = {}
                rd_dma[k] = []
            need = []
            for j, raw in deps.items():
                ej, dj = ins[j][0], ins[j][4]
                if dj is not None:
                    need.append(j)
                elif ej == eng and dma is None:
                    if raw and eng != "pe":
                        need.append(j)
                elif ej == eng and dma is not None:
                    need.append(j)
                else:
                    need.append(j)
            deps_all[i] = need
            for j in need:
                ins[j][6] = True
        eng_cnt = {}
        dma_cnt = {}
        ticket = [None] * n
        dma_before = [None] * n
        sem_names = {}
        for i in range(n):
            eng, fn, reads, writes, dma, _, needs = ins[i]
            dma_before[i] = dict(dma_cnt) if False else None
            if dma is not None:
                dma_cnt[dma] = dma_cnt.get(dma, 0) + 16
                ticket[i] = (("d", dma), dma_cnt[dma])
                sem_names[("d", dma)] = True
            elif needs:
                c = eng_cnt.get(eng, 0) + 1
                eng_cnt[eng] = c
                ep = (c - 1) // self.EPOCH
                ticket[i] = (("e", eng, ep), c - ep * self.EPOCH)
                sem_names[("e", eng, ep)] = True
        sems = {}
        for k in sem_names:
            nm = "s_" + "_".join(str(x) for x in k)
            sems[k] = stack.enter_context(nc.semaphore(nm))
        dma_run = {}
        dma_seen_at = [None] * n
        for i in range(n):
            snap = {}
            for j in deps_all[i]:
                dj = ins[j][4]
                if dj is not None:
                    snap[dj] = dma_run.get(dj, 0)
            dma_seen_at[i] = snap
            if ins[i][4] is not None:
                dma_run[ins[i][4]] = dma_run.get(ins[i][4], 0) + 16
        final = {k: v for k, v in dma_run.items() if k.startswith(final_wait_prefix)}
        per_eng = {}
        for i in range(n):
            per_eng.setdefault(ins[i][0], []).append(i)

        def run_engine(ename, e):
            waited = {}
            for i in per_eng.get(ename, ()):
                _, fn, _, _, dma, _, needs = ins[i]
                wl = {}
                for j in deps_all[i]:
                    sk, val = ticket[j]
                    if sk[0] == "d":
                        val = dma_seen_at[i][sk[1]]
                    if val > wl.get(sk, 0):
                        wl[sk] = val
                for sk, val in wl.items():
                    if waited.get(sk, 0) < val:
                        e.wait_ge(sems[sk], val)
                        waited[sk] = val
                r = fn(e)
                if dma is not None:
                    r.then_inc(sems[("d", dma)], 16)
                elif needs:
                    r.then_inc(sems[ticket[i][0]], 1)
            if ename == "sp":
                for k, v in final.items():
                    e.wait_ge(sems[("d", k)], v)

        with nc.Block() as block:
            @block.tensor
            def _(e):
                run_engine("pe", e)

            @block.scalar
            def _(e):
                run_engine("act", e)

            @block.vector
            def _(e):
                run_engine("dve", e)

            @block.gpsimd
            def _(e):
                run_engine("pool", e)

            @block.sync
            def _(e):
                run_engine("sp", e)
        return {e: len(v) for e, v in per_eng.items()}


def build_nc(NSEQ, S, L, last_is_final=True):
    NT = S // 128
    NCH = S // QB
    NKB = NT
    NBA = max(NKB - 4, 1)
    NBC = 2 * NBA + 1
    nc = bass.Bass("TRN2", target_bir_lowering=False)
    dt_in = lambda name, shape: nc.dram_tensor(name, list(shape), F32, kind="ExternalInput").ap()
    x_d = dt_in("x", (NSEQ, S, D))
    win_d = dt_in("w_in", (L, D, DIN))
    wout_d = dt_in("w_out", (L, D, D))
    ng_d = dt_in("ng", (L, 128, 8))
    lq_d = dt_in("lq", (L, 128, 256))
    subg_d = dt_in("subg", (L, 128, 512))
    vng_d = dt_in("vng", (L, 128, 512))
    wsT_d = dt_in("wsT", (L, 128, 512))
    bs_d = dt_in("bs", (L, 128, 4))
    fg_d = dt_in("fg", (128, D))
    ident_d = dt_in("identf", (128, 128))
    mident_d = dt_in("midentf", (128, 512))
    bhi_d = dt_in("bhif", (128, 896))
    blo_d = dt_in("blof", (128, 896))
    bcol_d = dt_in("bcol", (128, NH * NBC))
    ftab_d = dt_in("ftab", (128, NH * 16))
    y_d = nc.dram_tensor("y", [NSEQ, S, D], F32, kind="ExternalOutput").ap()
    R_d = nc.dram_tensor("Rres", [NSEQ, S, D], F32).ap()
    hT_d = nc.dram_tensor("hTs", [NSEQ, NCH, 128, 8 * QB], BF16).ap()

    P = Prog()
    with ExitStack() as st:
        sb = lambda name, shape, dt: st.enter_context(nc.sbuf_tensor(name, list(shape), dt))
        KT = sb("KT", (128, NH, S), BF16)
        VP = sb("VP", (128, NKB, NH, 129), BF16)
        Wkv = sb("Wkv", (128, 8, 1024), BF16)
        Wq = sb("Wq", (128, 8, 512), BF16)
        Wg = sb("Wg", (128, 8, 2048), BF16)
        Wo = sb("Wo", (128, 8, 1024), BF16)
        xt = [sb("xt%d" % i, (128, D), F32) for i in range(2)]
        et = [sb("et%d" % i, (128, 1024), BF16) for i in range(2)]
        hT = sb("hT", (128, 8, QB), BF16)
        QT = sb("QT", (128, NH, QB), BF16)
        Oacc = sb("Oacc", (128, 4, 2, 129), F32)
        G = sb("G", (128, 4, 512), F32)
        cat = sb("cat", (128, 4, 1024), BF16)
        tA = sb("tA", (128, 512), F32)
        tB = sb("tB", (128, 512), F32)
        vn = sb("vn", (128, 512), BF16)
        otmp = sb("otmp", (128, 4, 128), F32)
        scr = sb("scr", (128, 1032), F32)
        ident = sb("ident", (128, 128), BF16)
        mident = sb("mident", (128, NH, 128), BF16)
        Bhi = sb("Bhi", (128, 896), BF16)
        Blo = sb("Blo", (128, 896), BF16)
        bcol = sb("bcol_s", (128, NH * NBC), F32)
        ftab = sb("ftab_s", (128, NH * 16), F32)
        ng = sb("ng_s", (128, 8), F32)
        sublnvec = sb("sublnvec", (128, 512), F32)
        vnormg = sb("vnormg", (128, 512), F32)
        wsT = sb("wsT_s", (128, 512), BF16)
        bsb = sb("bs_s", (128, 4), F32)
        fgs = sb("fg_s", (128, D), F32)
        lqs = tA[:, 0:256]
        lprod = tA[:, 256:384]
        lsum = sb("lsum", (128, 2), F32)
        lexp = sb("lexp", (128, 2), F32)
        neglam = sb("neglam", (128, 1), F32)
        mhalf = sb("mhalf", (128, 8), F32)
        epsc = sb("epsc", (128, 8), F32)
        stat = sb("stat", (128, 64), F32)
        ps = [st.enter_context(nc.psum_tensor("ps%d" % i, [128, 2, 512], F32)) for i in range(4)]

        def bank(b):
            return ps[b // 2][:, b % 2, :]

        def bkey(b):
            return "bank%d" % b

        cnt = [0]

        def load_const(dst_ap, src_ap, key, via=None):
            if via is None:
                P.op("sp", lambda e, d=dst_ap, s=src_ap: e.dma_start(out=d, in_=s), reads=(), writes=(key,), dma="ld_c")
            else:
                stg, stg_key, width = via
                P.op("sp", lambda e, d=stg[:, 0:width], s=src_ap: e.dma_start(out=d, in_=s), reads=(), writes=(stg_key,), dma="ld_" + stg_key)
                P.op("dve", lambda e, d=dst_ap, s=stg[:, 0:width]: e.tensor_copy(out=d, in_=s), reads=(stg_key,), writes=(key,))

        load_const(ident[:], ident_d, "ident", via=(xt[0], "xt0", 128))
        load_const(mident[:].rearrange("p h k -> p (h k)"), mident_d, "mident", via=(xt[1], "xt1", 512))
        load_const(Bhi[:], bhi_d, "Bhi", via=(xt[0], "xt0", 896))
        load_const(Blo[:], blo_d, "Blo", via=(xt[1], "xt1", 896))
        load_const(bcol[:], bcol_d, "bcol")
        load_const(ftab[:], ftab_d, "ftab")
        load_const(fgs[:], fg_d, "fgs")
        P.op("dve", lambda e: e.memset(mhalf[:], -0.5), writes=("mhalf",))
        P.op("dve", lambda e: e.memset(epsc[:], float(EPS)), writes=("epsc",))
        P.op("dve", lambda e: e.memset(VP[:].rearrange("p a b c -> p (a b) c")[:, :, 128:129], 1.0), writes=("VPones",))

        rr = [0]

        def next_bank(pool):
            b = pool[rr[0] % len(pool)]
            rr[0] += 1
            return b

        conv_rr = [0]

        def convert(dst_ap, src_ap, scal_ap, rkeys, wkey):
            eng = ("dve", "act")[conv_rr[0] % 2]
            conv_rr[0] += 1
            if scal_ap is None:
                if eng == "act":
                    P.op("act", lambda e: e.copy(out=dst_ap, in_=src_ap), reads=rkeys, writes=(wkey,))
                else:
                    P.op(eng, lambda e: e.tensor_copy(out=dst_ap, in_=src_ap), reads=rkeys, writes=(wkey,))
            else:
                if eng == "act":
                    P.op("act", lambda e: e.activation(out=dst_ap, in_=src_ap, func=AF.Copy, scale=scal_ap), reads=rkeys + ("ng",), writes=(wkey,))
                else:
                    P.op(eng, lambda e: e.tensor_scalar(out=dst_ap, in0=src_ap, scalar1=scal_ap, scalar2=None, op0=ALU.mult), reads=rkeys + ("ng",), writes=(wkey,))

        stage_rr = [0]

        def layer_setup(l):
            li = lam_init_fn(l)
            P.op("sp", lambda e: e.dma_start(out=ng[:], in_=ng_d[l]), writes=("ng",), dma="ld_c")
            P.op("sp", lambda e: e.dma_start(out=sublnvec[:], in_=subg_d[l]), writes=("sublnvec",), dma="ld_c")
            P.op("sp", lambda e: e.dma_start(out=vnormg[:], in_=vng_d[l]), writes=("vnormg",), dma="ld_c")
            P.op("sp", lambda e: e.dma_start(out=bsb[:], in_=bs_d[l]), writes=("bs",), dma="ld_c")
            P.op("sp", lambda e: e.dma_start(out=tA[:], in_=wsT_d[l]), writes=("tA",), dma="ld_c")
            P.op("dve", lambda e: e.tensor_copy(out=wsT[:], in_=tA[:]), reads=("tA",), writes=("wsT",))
            P.op("dve", lambda e: e.tensor_scalar(out=sublnvec[:], in0=sublnvec[:], scalar1=float((1.0 - li) * 0.5), scalar2=None, op0=ALU.mult),
                 reads=("sublnvec",), writes=("sublnvec",))
            P.op("sp", lambda e: e.dma_start(out=lqs, in_=lq_d[l]), writes=("tA",), dma="ld_c")
            lq4 = lqs.rearrange("p (a b d) -> p a b d", a=2, b=2)
            P.op("dve", lambda e: e.tensor_tensor(out=lprod.rearrange("p (a d) -> p a d", a=2), in0=lq4[:, :, 0, :], in1=lq4[:, :, 1, :], op=ALU.mult),
                 reads=("tA",), writes=("tA",))
            P.op("dve", lambda e: e.reduce_sum(out=lsum[:], in_=lprod.rearrange("p (a d) -> p a d", a=2), axis=AX.X), reads=("tA",), writes=("lsum",))
            P.op("act", lambda e: e.activation(out=lexp[:], in_=lsum[:], func=AF.Exp), reads=("lsum",), writes=("lexp",))
            P.op("dve", lambda e: e.tensor_tensor(out=neglam[:], in0=lexp[:, 1:2], in1=lexp[:, 0:1], op=ALU.subtract), reads=("lexp",), writes=("neglam",))
            P.op("dve", lambda e: e.tensor_scalar(out=neglam[:], in0=neglam[:], scalar1=float(-li), scalar2=None, op0=ALU.add), reads=("neglam",), writes=("neglam",))
            KTs = KT[:].rearrange("p h s -> p (h s)").bitcast(F32)
            NSL = min(8, (NH * S * 2) // 4096)
            skeys = ["KTs%d" % j for j in range(NSL)]
            P.op("dve", lambda e: e.memset(stat[:, 60:61], 0.0), writes=("KT", "fence") + tuple(skeys))
            pieces = [(Wq, 0, "Wq"), (Wkv, 0, "Wkv"), (Wkv, 512, "Wkv"), (Wg, 0, "Wg"), (Wg, 512, "Wg"), (Wg, 1024, "Wg"), (Wg, 1536, "Wg")]
            nq = [0]

            def stage_load(src_ap, wdt):
                j = stage_rr[0] % NSL
                stage_rr[0] += 1
                q = ("sp", "act")[nq[0] % 2]
                nq[0] += 1
                dst = KTs[:, j * 1024:j * 1024 + wdt]
                P.op(q, lambda e: e.dma_start(out=dst, in_=src_ap), writes=(skeys[j],), dma="ld_" + skeys[j])
                return KTs[:, j * 1024:(j + 1) * 1024], skeys[j]

            def conv(dst_ap, src_ap, scal_ap, skey, wkey):
                if scal_ap is None:
                    P.op("dve", lambda e: e.tensor_copy(out=dst_ap, in_=src_ap), reads=(skey,), writes=(wkey,))
                else:
                    P.op("dve", lambda e: e.tensor_scalar(out=dst_ap, in0=src_ap, scalar1=scal_ap, scalar2=None, op0=ALU.mult), reads=(skey, "ng"), writes=(wkey,))

            for k in range(8):
                for b in range(4):
                    c0 = b * 1024
                    wdt = min(1024, DIN - c0)
                    stg, skey = stage_load(win_d[l, k * 128:(k + 1) * 128, c0:c0 + wdt], wdt)
                    for hh in range(wdt // 512):
                        dstT, dcol, dkey = pieces[(c0 // 512) + hh]
                        conv(dstT[:, k, dcol:dcol + 512], stg[:, hh * 512:(hh + 1) * 512], ng[:, k:k + 1], skey, dkey)
                stg, skey = stage_load(wout_d[l, k * 128:(k + 1) * 128, :], 1024)
                conv(Wo[:, k, :], stg, None, skey, "Wo")
            P.op("dve", lambda e: e.memset(stat[:, 61:62], 0.0), reads=tuple(skeys), writes=("KT", "fence2"))

        ALLB = list(range(8))
        xslot = [0]

        def rstd_from_ss(ss_ap, n, inv_n, skey, okey, out_ap):
            P.op("dve", lambda e: e.tensor_scalar(out=out_ap, in0=ss_ap, scalar1=float(inv_n), scalar2=float(EPS), op0=ALU.mult, op1=ALU.add),
                 reads=(skey,), writes=(okey,))
            P.op("pool", lambda e: e.tensor_tensor(out=out_ap, in0=out_ap, in1=mhalf[:, 0:n], op=ALU.pow), reads=(okey, "mhalf"), writes=(okey,))

        def xload(s, t, src):
            sl = t % 2
            xk = "xt%d" % sl
            P.op("sp", lambda e: e.dma_start(out=xt[sl][:], in_=src[s, t * 128:(t + 1) * 128, :]),
                 reads=("R%d_%d" % (s, t),), writes=(xk,), dma="ld_" + xk)

        hT2 = G[:].rearrange("p a b -> p (a b)").bitcast(BF16).rearrange("p (k t) -> p k t", k=8)
        hTb = [hT[:], hT2]
        hTkeys = [("hT",), ("G0", "G1", "G2", "G3")]

        def pass1(s, l):
            src = x_d if l == 0 else R_d

            def build_tile(c, i):
                hbuf = hTb[c % 2]
                hkeys = hTkeys[c % 2]
                t = 4 * c + i
                sl = t % 2
                xk = "xt%d" % sl
                if t < 2:
                    xload(s, t, src)
                ssk = "ss%d" % (i % 2)
                ssa = stat[:, (i % 2):(i % 2) + 1]
                rsa = stat[:, 2 + (i % 2):3 + (i % 2)]
                rsk = "rs%d" % (i % 2)
                P.op("act", lambda e: e.activation(out=scr[:, 0:512], in_=xt[sl][:, 0:512], func=AF.Square, accum_out=ssa),
                     reads=(xk,), writes=("scr0", "scr1", ssk))
                ssa2 = stat[:, 4 + (i % 2):5 + (i % 2)]
                ssk2 = "ssb%d" % (i % 2)
                P.op("act", lambda e: e.activation(out=scr[:, 0:512], in_=xt[sl][:, 512:1024], func=AF.Square, accum_out=ssa2),
                     reads=(xk,), writes=("scr0", "scr1", ssk2))
                P.op("dve", lambda e: e.tensor_tensor(out=ssa, in0=ssa, in1=ssa2, op=ALU.add), reads=(ssk, ssk2), writes=(ssk,))
                rstd_from_ss(ssa, 1, 1.0 / D, ssk, rsk, rsa)
                hb = et[i % 2]
                hk = "et%d" % (i % 2)
                P.op("dve", lambda e: e.tensor_scalar(out=hb[:], in0=xt[sl][:], scalar1=rsa, scalar2=None, op0=ALU.mult),
                     reads=(xk, rsk), writes=(hk,))
                if t + 2 < NT:
                    xload(s, t + 2, src)

            def trans_tile(c, i):
                hbuf = hTb[c % 2]
                hkeys = hTkeys[c % 2]
                hb = et[i % 2]
                hk = "et%d" % (i % 2)
                b = next_bank(ALLB)
                pT = bank(b).bitcast(BF16)
                for k in range(8):
                    P.op("pe", lambda e, k=k: e.transpose(out=pT[:, k * 128:(k + 1) * 128], in_=hb[:, k * 128:(k + 1) * 128], identity=ident[:]),
                         reads=(hk, "ident"), writes=(bkey(b),))
                P.op("act", lambda e: e.copy(out=hbuf[:, :, i * 128:(i + 1) * 128], in_=pT.rearrange("p (k t) -> p k t", k=8)),
                     reads=(bkey(b),), writes=hkeys)

            def proj_group(c, g):
                hbuf = hTb[c % 2]
                hkeys = hTkeys[c % 2]
                b = next_bank(ALLB)
                if g < 4:
                    h = g
                    for k in range(8):
                        P.op("pe", lambda e, k=k: e.matmul(out=bank(b), lhsT=Wkv[:, k, h * 128:(h + 1) * 128], rhs=hbuf[:, k, :], start=(k == 0), stop=(k == 7)),
                             reads=("Wkv",) + hkeys, writes=(bkey(b),))
                    P.op("act", lambda e: e.copy(out=KT[:, h, c * QB:(c + 1) * QB], in_=bank(b)), reads=(bkey(b),), writes=("KT",))
                else:
                    i = g - 4
                    for k in range(8):
                        P.op("pe", lambda e, k=k: e.matmul(out=bank(b), lhsT=hbuf[:, k, i * 128:(i + 1) * 128], rhs=Wkv[:, k, 512:1024], start=(k == 0), stop=(k == 7)),
                             reads=("Wkv",) + hkeys, writes=(bkey(b),))
                    P.op("dve", lambda e: e.tensor_copy(out=VP[:, 4 * c + i, :, 0:128], in_=bank(b).rearrange("p (h d) -> p h d", h=NH)),
                         reads=(bkey(b),), writes=("VP",))

            def tile_ci(t):
                return (t // 4, t % 4)

            build_tile(0, 0)
            build_tile(0, 1)
            for t in range(4):
                trans_tile(0, t)
                if t + 2 < NT:
                    build_tile(*tile_ci(t + 2))
            for c in range(NCH):
                for i in range(4):
                    proj_group(c, 2 * i)
                    proj_group(c, 2 * i + 1)
                    if c + 1 < NCH:
                        t = 4 * (c + 1) + i
                        trans_tile(c + 1, i)
                        if t + 2 < NT:
                            build_tile(*tile_ci(t + 2))
                hbuf = hTb[c % 2]
                P.op("pool", lambda e, c=c, hbuf=hbuf: e.dma_start(out=hT_d[s, c], in_=hbuf.rearrange("p k t -> p (k t)")), reads=hTkeys[c % 2], writes=("hTd%d_%d" % (s, c),), dma="st_hT")

        MISC = [7, 0, 1, 2, 3]
        sbuf_tog = [0]
        et_tog = [0]
        et3_tog = [0]
        ph_par = [0]

        def pass2(s, l, is_last):
            src = x_d if l == 0 else R_d
            xload(s, 0, src)
            xload(s, 1, src)
            P.op("sp", lambda e: e.dma_start(out=hT[:].rearrange("p k t -> p (k t)"), in_=hT_d[s, 0]), reads=("hTd%d_%d" % (s, 0),), writes=("hT",), dma="ld_hT")
            def qproj():
                for h in range(NH):
                    b = next_bank(MISC)
                    for k in range(8):
                        P.op("pe", lambda e, b=b, k=k, h=h: e.matmul(out=bank(b), lhsT=Wq[:, k, h * 128:(h + 1) * 128], rhs=hT[:, k, :], start=(k == 0), stop=(k == 7)),
                             reads=("Wq", "hT"), writes=(bkey(b),))
                    P.op("act", lambda e, b=b, h=h: e.activation(out=QT[:, h, :], in_=bank(b), func=AF.Copy, scale=0.125), reads=(bkey(b),), writes=("QT",))

            qproj()
            for c in range(NCH):
                side_tasks = []

                def mm_actions(i, c0):
                    tok = slice(i * 128, (i + 1) * 128)
                    acts = []
                    for k in range(8):
                        acts.append(lambda k=k, tok=tok, c0=c0: P.op("pe", lambda e: e.matmul(out=bank(7), lhsT=hT[:, k, tok], rhs=Wg[:, k, c0:c0 + 512], start=(k == 0), stop=(k == 7)),
                                                                     reads=("Wg", "hT"), writes=(bkey(7),)))
                    return acts

                def task_gate(i):
                    a0 = lambda: P.op("act", lambda e: e.copy(out=tA[:], in_=bank(7)), reads=(bkey(7),), writes=("tA",))
                    a1 = lambda: P.op("act", lambda e: e.activation(out=G[:, i, :], in_=tA[:], func=AF.Tanh, scale=0.5), reads=("tA",), writes=("G%d" % i,))
                    d1 = lambda: P.op("dve", lambda e: e.scalar_tensor_tensor(out=G[:, i, :], in0=G[:, i, :], scalar=1.0, in1=tA[:], op0=ALU.add, op1=ALU.mult),
                                      reads=("tA", "G%d" % i), writes=("G%d" % i,))
                    d2 = lambda: P.op("pool", lambda e: e.tensor_tensor(out=G[:, i, :], in0=G[:, i, :], in1=sublnvec[:], op=ALU.mult), reads=("G%d" % i, "sublnvec"), writes=("G%d" % i,))
                    return (mm_actions(i, 0), [[a0, a1], [d1, d2]])

                def task_gm(i):
                    a0 = lambda: P.op("act", lambda e: e.copy(out=tB[:], in_=bank(7)), reads=(bkey(7),), writes=("tB",))
                    a1 = lambda: P.op("act", lambda e: e.activation(out=tA[:], in_=tB[:], func=AF.Tanh, scale=0.5), reads=("tB",), writes=("tA",))
                    d1 = lambda: P.op("dve", lambda e: e.scalar_tensor_tensor(out=tB[:], in0=tA[:], scalar=1.0, in1=tB[:], op0=ALU.add, op1=ALU.mult),
                                      reads=("tA", "tB"), writes=("tB",))
                    return (mm_actions(i, 1536), [[a0, a1], [d1]])

                def task_u(i):
                    a0 = lambda: P.op("act", lambda e: e.activation(out=tA[:], in_=bank(7), func=AF.Copy, scale=0.5), reads=(bkey(7),), writes=("tA",))
                    d1 = lambda: P.op("dve", lambda e: e.tensor_tensor(out=tB[:], in0=tB[:], in1=tA[:], op=ALU.mult), reads=("tA", "tB"), writes=("tB",))
                    return (mm_actions(i, 512), [[a0], [d1]])

                def task_vg(i):
                    ssv = stat[:, 8:9]
                    rsv = stat[:, 9:10]
                    a0 = lambda: P.op("act", lambda e: e.copy(out=tA[:], in_=bank(7)), reads=(bkey(7),), writes=("tA",))
                    a1 = lambda: P.op("act", lambda e: e.activation(out=vn[:], in_=tA[:], func=AF.Square, scale=float(512.0 ** -0.5), accum_out=ssv), reads=("tA",), writes=("vn", "ssv"))

                    def p1():
                        P.op("pool", lambda e: e.tensor_tensor(out=rsv, in0=ssv, in1=epsc[:, 0:1], op=ALU.add), reads=("ssv", "epsc"), writes=("rsv",))
                        P.op("pool", lambda e: e.tensor_tensor(out=rsv, in0=rsv, in1=mhalf[:, 0:1], op=ALU.pow), reads=("rsv", "mhalf"), writes=("rsv",))
                        P.op("pool", lambda e: e.tensor_tensor(out=tA[:], in0=tA[:], in1=rsv.to_broadcast([128, 512]), op=ALU.mult), reads=("tA", "rsv"), writes=("tA",))
                        P.op("pool", lambda e: e.tensor_tensor(out=vn[:], in0=tA[:], in1=vnormg[:], op=ALU.mult), reads=("tA", "vnormg"), writes=("vn",))
                    return (mm_actions(i, 1024), [[a0, a1], [p1], [], []])

                def task_sv(i):
                    mm = []
                    for g in range(4):
                        mm.append(lambda g=g: P.op("pe", lambda e: e.matmul(out=bank(7)[:, g * 128:(g + 1) * 128], lhsT=wsT[:, g * 128:(g + 1) * 128], rhs=vn[:, g * 128:(g + 1) * 128],
                                                                            start=True, stop=True, skip_group_check=True),
                                                   reads=("wsT", "vn"), writes=(bkey(7),)))
                    a0 = lambda: P.op("act", lambda e: e.copy(out=tA[:], in_=bank(7)), reads=(bkey(7),), writes=("tA",))
                    d1 = lambda: P.op("dve", lambda e: e.tensor_tensor(out=tA[:].rearrange("p (g d) -> p g d", g=4), in0=tA[:].rearrange("p (g d) -> p g d", g=4),
                                                                        in1=bsb[:, 0:4].unsqueeze(2).to_broadcast([128, 4, 128]), op=ALU.add),
                                      reads=("tA", "bs"), writes=("tA",))
                    d2 = lambda: P.op("dve", lambda e: e.tensor_tensor(out=cat[:, i, 512:1024], in0=tA[:], in1=tB[:], op=ALU.mult), reads=("tA", "tB"), writes=("cat%d" % i,))
                    return (mm, [[a0], [d1, d2]])

                def pgroup(i, c0):
                    b_ = next_bank(MISC)
                    tok = slice(i * 128, (i + 1) * 128)
                    for k in range(8):
                        P.op("pe", lambda e, k=k: e.matmul(out=bank(b_), lhsT=hT[:, k, tok], rhs=Wg[:, k, c0:c0 + 512], start=(k == 0), stop=(k == 7)),
                             reads=("Wg", "hT"), writes=(bkey(b_),))
                    return b_

                for i in range(4):
                    ssv = stat[:, 8 + 2 * (i % 2):9 + 2 * (i % 2)]
                    rsv = stat[:, 9 + 2 * (i % 2):10 + 2 * (i % 2)]
                    ssvk = "ssv%d" % (i % 2)
                    rsvk = "rsv%d" % (i % 2)
                    tG = tA if i % 2 == 0 else otmp[:].rearrange("p a b -> p (a b)")
                    tGk = "tA" if i % 2 == 0 else "otmp"
                    tGa = tA[:] if i % 2 == 0 else otmp[:].rearrange("p a b -> p (a b)")
                    bv = pgroup(i, 1024)
                    P.op("act", lambda e, bv=bv, ssv=ssv: e.activation(out=scr[:, 512:1024], in_=bank(bv), func=AF.Square, scale=float(512.0 ** -0.5), accum_out=ssv),
                         reads=(bkey(bv),), writes=("scr1", "scr2", ssvk))
                    P.op("pool", lambda e, ssv=ssv, rsv=rsv: e.tensor_tensor(out=rsv, in0=ssv, in1=epsc[:, 0:1], op=ALU.add), reads=(ssvk, "epsc"), writes=(rsvk,))
                    P.op("pool", lambda e, rsv=rsv: e.tensor_tensor(out=rsv, in0=rsv, in1=mhalf[:, 0:1], op=ALU.pow), reads=(rsvk, "mhalf"), writes=(rsvk,))
                    bg = pgroup(i, 0)
                    P.op("act", lambda e, bg=bg, tGa=tGa: e.activation(out=tGa, in_=bank(bg), func=AF.Tanh, scale=0.5), reads=(bkey(bg),), writes=(tGk,))
                    P.op("dve", lambda e, bv=bv, rsv=rsv: e.scalar_tensor_tensor(out=vn[:], in0=bank(bv), scalar=rsv, in1=vnormg[:], op0=ALU.mult, op1=ALU.mult),
                         reads=(bkey(bv), rsvk, "vnormg"), writes=("vn",))
                    P.op("dve", lambda e, bg=bg, i=i, tGa=tGa: e.scalar_tensor_tensor(out=G[:, i, :], in0=tGa, scalar=1.0, in1=bank(bg), op0=ALU.add, op1=ALU.mult),
                         reads=(tGk, bkey(bg)), writes=("G%d" % i,))
                    P.op("pool", lambda e, i=i: e.tensor_tensor(out=G[:, i, :], in0=G[:, i, :], in1=sublnvec[:], op=ALU.mult), reads=("G%d" % i, "sublnvec"), writes=("G%d" % i,))
                    bgm = pgroup(i, 1536)
                    P.op("act", lambda e, bgm=bgm: e.activation(out=tB[:], in_=bank(bgm), func=AF.Tanh, scale=0.5), reads=(bkey(bgm),), writes=("tB",))
                    P.op("dve", lambda e, bgm=bgm: e.scalar_tensor_tensor(out=tB[:], in0=tB[:], scalar=1.0, in1=bank(bgm), op0=ALU.add, op1=ALU.mult),
                         reads=("tB", bkey(bgm)), writes=("tB",))
                    bu = pgroup(i, 512)
                    P.op("dve", lambda e, bu=bu: e.scalar_tensor_tensor(out=tB[:], in0=tB[:], scalar=0.5, in1=bank(bu), op0=ALU.mult, op1=ALU.mult),
                         reads=("tB", bkey(bu)), writes=("tB",))
                    bsv = next_bank(MISC)
                    for g in range(4):
                        P.op("pe", lambda e, bsv=bsv, g=g: e.matmul(out=bank(bsv)[:, g * 128:(g + 1) * 128], lhsT=wsT[:, g * 128:(g + 1) * 128], rhs=vn[:, g * 128:(g + 1) * 128],
                                                                    start=True, stop=True, skip_group_check=True),
                             reads=("wsT", "vn"), writes=(bkey(bsv),))
                    for g in range(4):
                        P.op("dve", lambda e, bsv=bsv, g=g, i=i: e.scalar_tensor_tensor(out=cat[:, i, 512 + g * 128:512 + (g + 1) * 128], in0=bank(bsv)[:, g * 128:(g + 1) * 128],
                                                                                     scalar=bsb[:, g:g + 1], in1=tB[:, g * 128:(g + 1) * 128], op0=ALU.add, op1=ALU.mult),
                             reads=(bkey(bsv), "bs", "tB"), writes=("cat%d" % i,))
                if c + 1 < NCH:
                    P.op("sp", lambda e, c=c: e.dma_start(out=hT[:].rearrange("p k t -> p (k t)"), in_=hT_d[s, c + 1]), reads=("hTd%d_%d" % (s, c + 1),), writes=("hT",), dma="ld_hT")

                def gbank(g, par):
                    return (4 if par == 0 else 7) if g == 0 else 4 + g

                def acc_ap(idx, par):
                    bnk = gbank(idx // 3, par)
                    col = (idx % 3) * 129
                    return bank(bnk)[:, col:col + 129], bnk

                steps = []
                for h in HEAD_ORDER:
                    lst = []
                    for kb in range(NKB):
                        if kb < 4 * c:
                            if 4 * c - kb <= KEEP[h]:
                                lst.append((kb, "b"))
                        elif kb < 4 * c + 4:
                            lst.append((kb, "i"))
                        else:
                            if kb - 4 * c - 3 <= KEEP[h]:
                                lst.append((kb, "a"))
                    lst = [x for x in lst if x[1] == "a"] + [x for x in lst if x[1] == "b"] + [x for x in lst if x[1] == "i"]
                    for n_, (kb, k) in enumerate(lst):
                        first = (n_ == 0) or (lst[n_ - 1][1] != k)
                        last = (n_ == len(lst) - 1) or (lst[n_ + 1][1] != k)
                        steps.append((h, kb, k, first, last, n_ == len(lst) - 1, first and n_ == 0))

                def emit_qk(h, kb, k):
                    X = sbuf_tog[0] % 2
                    sbuf_tog[0] += 1
                    inchunk = k == "i"
                    for m in range(2):
                        pr = slice(64 * m, 64 * m + 64)
                        P.op("pe", lambda e, X=X, m=m, pr=pr, h=h, kb=kb, inchunk=inchunk: e.matmul(out=ps[X][:, m, :], lhsT=KT[pr, h, kb * 128:(kb + 1) * 128], rhs=QT[pr, h, :],
                                                                                                  start=True, stop=(not inchunk), skip_group_check=True),
                             reads=("KT", "QT"), writes=(bkey(2 * X + m),))
                    if inchunk:
                        j = kb - 4 * c
                        off = 384 - 128 * j
                        c0, c1 = [(256, 512), (384, 512), (0, 128), (0, 256)][j]
                        for m in range(2):
                            P.op("pe", lambda e, X=X, m=m, h=h, off=off: e.matmul(out=ps[X][:, m, :], lhsT=mident[:, h, :], rhs=Bhi[:, off:off + 512], start=False, stop=False, skip_group_check=True),
                                 reads=("mident", "Bhi"), writes=(bkey(2 * X + m),))
                            P.op("pe", lambda e, X=X, m=m, h=h, off=off, c0=c0, c1=c1: e.matmul(out=ps[X][:, m, c0:c1], lhsT=mident[:, h, :], rhs=Blo[:, off + c0:off + c1], start=False, stop=True, skip_group_check=True),
                                 reads=("mident", "Blo"), writes=(bkey(2 * X + m),))
                    return X

                ETA = [et[0][:], et[1][:], tB[:].bitcast(BF16)]
                ETK = ["et0", "et1", "tB"]

                def emit_act(h, kb, k, X):
                    E = et3_tog[0] % 3
                    et3_tog[0] += 1
                    if k == "b":
                        idx = (4 * c - kb) - 1
                    elif k == "a":
                        idx = NBA + (kb - 4 * c - 4)
                    else:
                        idx = 2 * NBA
                    col = h * NBC + idx
                    P.op("act", lambda e, X=X, E=E, col=col: e.activation(out=ETA[E], in_=ps[X][:].rearrange("p a b -> p (a b)"), func=AF.Exp, bias=bcol[:, col:col + 1], scale=1.0),
                         reads=(bkey(2 * X), bkey(2 * X + 1), "bcol"), writes=(ETK[E],))
                    return E

                def emit_pv(h, kb, E, first, par):
                    for idx in range(8):
                        i, m = idx // 2, idx % 2
                        ap, bnk = acc_ap(idx, par)
                        stflag = first and (idx % 3 == 0)
                        P.op("pe", lambda e, ap=ap, E=E, m=m, i=i, kb=kb, h=h, stflag=stflag: e.matmul(out=ap, lhsT=ETA[E][:, m * 512 + i * 128:m * 512 + (i + 1) * 128], rhs=VP[:, kb, h, :],
                                                                                                     start=stflag, stop=False, skip_group_check=True),
                             reads=(ETK[E], "VP", "VPones"), writes=(bkey(bnk),))

                Oav = Oacc[:].rearrange("p i m e -> p (i m) e")
                scrv = scr[:, 0:1032].rearrange("p (a e) -> p a e", e=129)

                def emit_phase_evac(h, k, firstphase, par):
                    for g, (a0, a1) in ((1, (3, 6)), (2, (6, 8)), (0, (0, 3))):
                        bnk = gbank(g, par)
                        n_ = a1 - a0
                        src_ = bank(bnk)[:, 0:n_ * 129].rearrange("p (a e) -> p a e", e=129)
                        dst = Oav[:, a0:a1, :]
                        if k == "i":
                            if firstphase:
                                P.op("dve", lambda e, src_=src_, dst=dst: e.tensor_copy(out=dst, in_=src_), reads=(bkey(bnk),), writes=("Oacc%d" % g,))
                            else:
                                P.op("dve", lambda e, src_=src_, dst=dst: e.tensor_tensor(out=dst, in0=src_, in1=dst, op=ALU.add), reads=(bkey(bnk), "Oacc%d" % g), writes=("Oacc%d" % g,))
                        else:
                            f0 = (h * 2 + (0 if k == "b" else 1)) * 8 + a0
                            fb = ftab[:, f0:f0 + n_].unsqueeze(2).to_broadcast([128, n_, 129])
                            if firstphase:
                                P.op("dve", lambda e, src_=src_, dst=dst, fb=fb: e.tensor_tensor(out=dst, in0=src_, in1=fb, op=ALU.mult),
                                     reads=(bkey(bnk), "ftab"), writes=("Oacc%d" % g,))
                            else:
                                P.op("dve", lambda e, src_=src_, a0=a0, a1=a1, fb=fb: e.tensor_tensor(out=scrv[:, a0:a1, :], in0=src_, in1=fb, op=ALU.mult),
                                     reads=(bkey(bnk), "ftab"), writes=("scr%d" % g,))
                    if k != "i" and not firstphase:
                        for g, (a0, a1) in enumerate([(0, 3), (3, 6), (6, 8)]):
                            P.op("dve", lambda e, a0=a0, a1=a1: e.tensor_tensor(out=Oav[:, a0:a1, :], in0=Oav[:, a0:a1, :], in1=scrv[:, a0:a1, :], op=ALU.add),
                                 reads=("Oacc%d" % g, "scr%d" % g), writes=("Oacc%d" % g,))

                rr_ = stat[:, 16:24].rearrange("p (i m) -> p i m", i=4)
                c1 = stat[:, 24:28]
                ssq = stat[:, 28:32]
                rso = stat[:, 32:36]
                scrA = tA[:].rearrange("p (a e) -> p a e", e=128)
                scrB = tA[:].rearrange("p (a e) -> p a e", e=128)

                def combine_stages(h):
                    def s0():
                        P.op("dve", lambda e: e.reciprocal(out=rr_, in_=Oacc[:, :, :, 128]), reads=("Oacc0", "Oacc1", "Oacc2",), writes=("rr",))
                        P.op("dve", lambda e: e.tensor_scalar(out=c1, in0=rr_[:, :, 1], scalar1=neglam[:, 0:1], scalar2=None, op0=ALU.mult), reads=("rr", "neglam"), writes=("c1",))
                        P.op("dve", lambda e: e.tensor_tensor(out=otmp[:], in0=Oacc[:, :, 0, 0:128], in1=rr_[:, :, 0:1].to_broadcast([128, 4, 128]), op=ALU.mult),
                             reads=("Oacc0", "Oacc1", "Oacc2", "rr"), writes=("otmp",))
                        P.op("dve", lambda e: e.tensor_tensor(out=scrB, in0=Oacc[:, :, 1, 0:128], in1=c1.unsqueeze(2).to_broadcast([128, 4, 128]), op=ALU.mult),
                             reads=("Oacc0", "Oacc1", "Oacc2", "c1"), writes=("tA",))

                    def s1():
                        P.op("dve", lambda e: e.tensor_tensor(out=otmp[:], in0=otmp[:], in1=scrB, op=ALU.add), reads=("otmp", "tA"), writes=("otmp",))
                        P.op("pool", lambda e: e.tensor_tensor(out=scrA, in0=otmp[:], in1=otmp[:], op=ALU.mult), reads=("otmp",), writes=("tA",))

                    def s2():
                        P.op("dve", lambda e: e.reduce_sum(out=ssq, in_=scrA, axis=AX.X), reads=("tA",), writes=("ssq",))
                        rstd_from_ss(ssq, 4, 1.0 / 128, "ssq", "rso", rso)

                    def s3():
                        P.op("dve", lambda e: e.tensor_tensor(out=otmp[:], in0=otmp[:], in1=rso.unsqueeze(2).to_broadcast([128, 4, 128]), op=ALU.mult),
                             reads=("otmp", "rso"), writes=("otmp",))
                        P.op("dve", lambda e: e.tensor_tensor(out=cat[:, :, h * 128:(h + 1) * 128], in0=otmp[:], in1=G[:, :, h * 128:(h + 1) * 128], op=ALU.mult),
                             reads=("otmp", "G0", "G1", "G2", "G3"), writes=("cat0", "cat1", "cat2", "cat3"))
                    return [s0, s1, None, s2, None, s3]

                nst = len(steps)

                def make_sched(mps):
                    sched = {}
                    pos = 0
                    prev_last = -1
                    for (mm, posts) in side_tasks:
                        if len(mm) <= 4:
                            pos = max(pos, prev_last + 1)
                        nmm = (len(mm) + mps - 1) // mps if len(mm) > 4 else 1
                        for q in range(nmm):
                            chunk_ = mm[q * mps:(q + 1) * mps] if len(mm) > 4 else mm
                            sched.setdefault(pos + q, []).extend(chunk_)
                        pstart = max(pos + nmm, prev_last)
                        for q, pa in enumerate(posts):
                            sched.setdefault(pstart + q, []).extend(pa)
                        prev_last = pstart + len(posts) - 1
                        pos = max(pos + nmm + 1, pstart + 1)
                    return sched, prev_last + 2

                for mps in (2, 3, 4, 8):
                    sched, need = make_sched(mps)
                    if need <= nst - 1:
                        break
                cq = []
                Xs = {}
                for q_ in range(min(2, nst)):
                    Xs[q_] = emit_qk(steps[q_][0], steps[q_][1], steps[q_][2])
                nphase_done = {}
                for si, (h, kb, k, first, last, lasthead, firsthead) in enumerate(steps):
                    X = Xs.pop(si)
                    E = emit_act(h, kb, k, X)
                    if si + 2 < nst:
                        Xs[si + 2] = emit_qk(steps[si + 2][0], steps[si + 2][1], steps[si + 2][2])
                    if first:
                        ph_par[0] ^= 1
                    emit_pv(h, kb, E, first, ph_par[0])
                    if last:
                        firstphase = h not in nphase_done
                        nphase_done[h] = True
                        if cq and not lasthead:
                            f_ = cq.pop(0)
                            if f_ is not None:
                                f_()
                        emit_phase_evac(h, k, firstphase, ph_par[0])
                        if lasthead:
                            while cq:
                                f_ = cq.pop(0)
                                if f_ is not None:
                                    f_()
                            st_ = combine_stages(h)
                            st_[0]()
                            cq = st_[1:]
                    elif cq:
                        f_ = cq.pop(0)
                        if f_ is not None:
                            f_()
                    for fn_ in sched.pop(si - 1, ()):
                        fn_()
                for key_ in sorted(sched.keys()):
                    for fn_ in sched[key_]:
                        fn_()
                while cq:
                    f_ = cq.pop(0)
                    if f_ is not None:
                        f_()
                if c + 1 < NCH:
                    qproj()
                tinfo = {}

                def tail_T(i):
                    b = next_bank(MISC)
                    pT = bank(b).bitcast(BF16)
                    E = et_tog[0] % 2
                    et_tog[0] += 1
                    ek = "et%d" % E
                    for k in range(8):
                        P.op("pe", lambda e, pT=pT, i=i, k=k: e.transpose(out=pT[:, k * 128:(k + 1) * 128], in_=cat[:, i, k * 128:(k + 1) * 128], identity=ident[:]),
                             reads=("cat%d" % i, "ident"), writes=(bkey(b),))
                    P.op("dve", lambda e, pT=pT, E=E: e.tensor_copy(out=et[E][:], in_=pT), reads=(bkey(b),), writes=(ek,))
                    tinfo[i] = (E, ek)

                def tail_M(i):
                    E, ek = tinfo[i]
                    t = 4 * c + i
                    sl = t % 2
                    xk = "xt%d" % sl
                    for half in range(2):
                        bo = next_bank(MISC)
                        for k in range(8):
                            P.op("pe", lambda e, bo=bo, E=E, k=k, half=half: e.matmul(out=bank(bo), lhsT=et[E][:, k * 128:(k + 1) * 128], rhs=Wo[:, k, half * 512:(half + 1) * 512],
                                                                                    start=(k == 0), stop=(k == 7)),
                                 reads=(ek, "Wo"), writes=(bkey(bo),))
                        P.op("dve", lambda e, bo=bo, sl=sl, half=half: e.tensor_tensor(out=xt[sl][:, half * 512:(half + 1) * 512], in0=bank(bo), in1=xt[sl][:, half * 512:(half + 1) * 512], op=ALU.add),
                             reads=(bkey(bo), xk), writes=(xk,))
                    if is_last:
                        ssf = stat[:, 40:41]
                        ssf2 = stat[:, 41:42]
                        rsf = stat[:, 42:43]
                        P.op("act", lambda e, sl=sl: e.activation(out=tA[:], in_=xt[sl][:, 0:512], func=AF.Square, accum_out=ssf), reads=(xk,), writes=("tA", "ssf"))
                        P.op("act", lambda e, sl=sl: e.activation(out=tA[:], in_=xt[sl][:, 512:1024], func=AF.Square, accum_out=ssf2), reads=(xk,), writes=("tA", "ssf2"))
                        P.op("dve", lambda e: e.tensor_tensor(out=ssf, in0=ssf, in1=ssf2, op=ALU.add), reads=("ssf", "ssf2"), writes=("ssf",))
                        rstd_from_ss(ssf, 1, 1.0 / D, "ssf", "rsf", rsf)
                        P.op("dve", lambda e, sl=sl: e.scalar_tensor_tensor(out=xt[sl][:], in0=xt[sl][:], scalar=rsf, in1=fgs[:], op0=ALU.mult, op1=ALU.mult),
                             reads=(xk, "rsf", "fgs"), writes=(xk,))
                        P.op("pool", lambda e, sl=sl, t=t: e.dma_start(out=y_d[s, t * 128:(t + 1) * 128, :], in_=xt[sl][:]), reads=(xk,), writes=("y%d_%d" % (s, t),), dma="st_" + xk)
                    else:
                        P.op("pool", lambda e, sl=sl, t=t: e.dma_start(out=R_d[s, t * 128:(t + 1) * 128, :], in_=xt[sl][:]), reads=(xk,), writes=("R%d_%d" % (s, t),), dma="st_" + xk)
                    if t + 2 < NT:
                        xload(s, t + 2, src)

                tail_T(0)
                tail_T(1)
                tail_M(0)
                tail_T(2)
                tail_M(1)
                tail_T(3)
                tail_M(2)
                tail_M(3)

        for l in range(L):
            layer_setup(l)
            for s in range(NSEQ):
                pass1(s, l)
                pass2(s, l, (l == L - 1) and last_is_final)
        counts = P.build(nc, st)
    return nc, counts


def make_consts(S):
    NKB = S // 128
    NBA = max(NKB - 4, 1)
    NBC = 2 * NBA + 1
    p = np.arange(128, dtype=np.float64)
    ident = np.eye(128, dtype=np.float32)
    mident = np.zeros((128, NH, 128), np.float32)
    for h in range(NH):
        mident[:, h, :] = np.eye(128) * SLOPES[h]
    y = np.arange(896, dtype=np.float64)
    d = np.abs(y[None, :] - 384.0 - p[:, None])
    bhi = (-np.minimum(d, 256.0)).astype(np.float32)
    blo = (-np.maximum(d - 256.0, 0.0)).astype(np.float32)
    bcol = np.zeros((128, NH, NBC), np.float64)
    for h in range(NH):
        m = SLOPES[h]
        for dl in range(1, NBA + 1):
            bcol[:, h, dl - 1] = -m * (128.0 * dl - p)
        for dl in range(0, NBA):
            bcol[:, h, NBA + dl] = -m * (128.0 * dl + p + 1.0)
    ftab = np.zeros((128, NH, 2, 8), np.float64)
    for h in range(NH):
        m = SLOPES[h]
        for i in range(4):
            for mm in range(2):
                ftab[:, h, 0, 2 * i + mm] = np.exp(-m * (128.0 * i + p))
                ftab[:, h, 1, 2 * i + mm] = np.exp(-m * (511.0 - 128.0 * i - p))
    return dict(identf=ident, midentf=mident.reshape(128, NH * 128), bhif=bhi, blof=blo,
                bcol=bcol.reshape(128, NH * NBC).astype(np.float32), ftab=ftab.reshape(128, NH * 16).astype(np.float32))


def make_params(norm_g, w_in, lambda_qk, subln_g, vnorm_g, w_s, b_s, w_out, final_g):
    L = w_in.shape[0]
    f = np.float32
    c = np.ascontiguousarray
    d = {}
    d["w_in"] = c(w_in, dtype=f)
    d["w_out"] = c(w_out, dtype=f)
    d["ng"] = c(np.asarray(norm_g, f).reshape(L, 8, 128).transpose(0, 2, 1))
    d["lq"] = c(np.broadcast_to(np.asarray(lambda_qk, f).reshape(L, 1, 256), (L, 128, 256)))
    d["subg"] = c(np.broadcast_to(np.tile(np.asarray(subln_g, f), (1, 4)).reshape(L, 1, 512), (L, 128, 512)))
    d["vng"] = c(np.broadcast_to(np.asarray(vnorm_g, f).reshape(L, 1, 512), (L, 128, 512)))
    d["wsT"] = c(np.asarray(w_s, f).transpose(0, 3, 1, 2).reshape(L, 128, 512))
    d["bs"] = c(np.asarray(b_s, f).transpose(0, 2, 1))
    d["fg"] = c(np.broadcast_to(np.asarray(final_g, f).reshape(1, D), (128, D)))
    return d


_CACHE = {}


def run(xs_per_core, params, S, L, n_cores, trace=False):
    NSEQ = xs_per_core[0].shape[0]
    key = (NSEQ, S, L)
    if key not in _CACHE:
        _CACHE[key] = build_nc(NSEQ, S, L)
    nc, counts = _CACHE[key]
    consts = make_consts(S)
    in_maps = []
    for ci in range(n_cores):
        m = {"x": np.ascontiguousarray(xs_per_core[ci], dtype=np.float32)}
        m.update(params)
        m.update(consts)
        in_maps.append(m)
    res = run_bass_kernel_spmd(nc, in_maps, core_ids=list(range(n_cores)))
    return [r["y"] for r in res.results]


def kernel(x_prompt, x_sample, norm_g, w_in, lambda_qk, subln_g, vnorm_g, w_s, b_s, w_out, final_g):
    x_prompt = np.asarray(x_prompt, np.float32)
    x_sample = np.asarray(x_sample, np.float32)
    S = x_prompt.shape[1]
    L = np.asarray(w_in).shape[0]
    allx = np.concatenate([x_prompt, x_sample], axis=0)
    n_cores = 8
    per = allx.shape[0] // n_cores
    xs = [allx[i * per:(i + 1) * per] for i in range(n_cores)]
    params = make_params(np.asarray(norm_g), np.asarray(w_in), np.asarray(lambda_qk), np.asarray(subln_g), np.asarray(vnorm_g),
                         np.asarray(w_s), np.asarray(b_s), np.asarray(w_out), np.asarray(final_g))
    ys = run(xs, params, S, L, n_cores)
    ally = np.concatenate(ys, axis=0)
    nb = x_prompt.shape[0]
    return (np.ascontiguousarray(ally[:nb]), np.ascontiguousarray(ally[nb:]))
```

```python
import math
from contextlib import ExitStack

import numpy as np
import concourse.bass as bass
import concourse.mybir as mybir
from concourse.bass_utils import run_bass_kernel_spmd

F32 = mybir.dt.float32
BF16 = mybir.dt.bfloat16
ALU = mybir.AluOpType
AF = mybir.ActivationFunctionType
AX = mybir.AxisListType

D = 1024
DIN = 3584
NH = 4
EPS = 1e-6
QB = 512
SLOPES = [2.0 ** (-2.0 * (i + 1)) for i in range(NH)]
THRESH = 60.0
KEEP = [int(math.floor((THRESH / m - 1.0) / 128.0 - 1e-9)) + 1 for m in SLOPES]
HEAD_ORDER = [3, 2, 1, 0]


def lam_init_fn(l):
    return 0.8 - 0.6 * math.exp(-0.3 * l)


class Prog:
    EPOCH = 12000

    def __init__(self):
        self.ins = []

    def op(self, eng, fn, reads=(), writes=(), dma=None):
        self.ins.append([eng, fn, tuple(reads), tuple(writes), dma, None, False])

    def build(self, nc, stack, final_wait_prefix=("st_",)):
        ins = self.ins
        n = len(ins)
        last_w = {}
        rd_eng = {}
        rd_dma = {}
        deps_all = [None] * n
        for i in range(n):
            eng, fn, reads, writes, dma, _, _ = ins[i]
            deps = {}
            for k in reads:
                j = last_w.get(k)
                if j is not None:
                    deps[j] = True
            for k in writes:
                j = last_w.get(k)
                if j is not None and j not in deps:
                    deps[j] = True
                for j in rd_eng.get(k, {}).values():
                    if j != i and j not in deps:
                        deps[j] = False
                for j in rd_dma.get(k, ()):
                    if j not in deps:
                        deps[j] = False
            for k in reads:
                if dma is not None:
                    rd_dma.setdefault(k, []).append(i)
                else:
                    rd_eng.setdefault(k, {})[eng] = i
            for k in writes:
                last_w[k] = i
                rd_eng[k] = {}
                rd_dma[k] = []
            need = []
            for j, raw in deps.items():
                ej, dj = ins[j][0], ins[j][4]
                if dj is not None:
                    need.append(j)
                elif ej == eng and dma is None:
                    if raw and eng != "pe":
                        need.append(j)
                elif ej == eng and dma is not None:
                    need.append(j)
                else:
                    need.append(j)
            deps_all[i] = need
            for j in need:
                ins[j][6] = True
        eng_cnt = {}
        dma_cnt = {}
        ticket = [None] * n
        dma_before = [None] * n
        sem_names = {}
        for i in range(n):
            eng, fn, reads, writes, dma, _, needs = ins[i]
            dma_before[i] = dict(dma_cnt) if False else None
            if dma is not None:
                dma_cnt[dma] = dma_cnt.get(dma, 0) + 16
                ticket[i] = (("d", dma), dma_cnt[dma])
                sem_names[("d", dma)] = True
            elif needs:
                c = eng_cnt.get(eng, 0) + 1
                eng_cnt[eng] = c
                ep = (c - 1) // self.EPOCH
                ticket[i] = (("e", eng, ep), c - ep * self.EPOCH)
                sem_names[("e", eng, ep)] = True
        sems = {}
        for k in sem_names:
            nm = "s_" + "_".join(str(x) for x in k)
            sems[k] = stack.enter_context(nc.semaphore(nm))
        dma_run = {}
        dma_seen_at = [None] * n
        for i in range(n):
            snap = {}
            for j in deps_all[i]:
                dj = ins[j][4]
                if dj is not None:
                    snap[dj] = dma_run.get(dj, 0)
            dma_seen_at[i] = snap
            if ins[i][4] is not None:
                dma_run[ins[i][4]] = dma_run.get(ins[i][4], 0) + 16
        final = {k: v for k, v in dma_run.items() if k.startswith(final_wait_prefix)}
        per_eng = {}
        for i in range(n):
            per_eng.setdefault(ins[i][0], []).append(i)

        def run_engine(ename, e):
            waited = {}
            for i in per_eng.get(ename, ()):
                _, fn, _, _, dma, _, needs = ins[i]
                wl = {}
                for j in deps_all[i]:
                    sk, val = ticket[j]
                    if sk[0] == "d":
                        val = dma_seen_at[i][sk[1]]
                    if val > wl.get(sk, 0):
                        wl[sk] = val
                for sk, val in wl.items():
                    if waited.get(sk, 0) < val:
                        e.wait_ge(sems[sk], val)
                        waited[sk] = val
                r = fn(e)
                if dma is not None:
                    r.then_inc(sems[("d", dma)], 16)
                elif needs:
                    r.then_inc(sems[ticket[i][0]], 1)
            if ename == "sp":
                for k, v in final.items():
                    e.wait_ge(sems[("d", k)], v)

        with nc.Block() as block:
            @block.tensor
            def _(e):
                run_engine("pe", e)

            @block.scalar
            def _(e):
                run_engine("act", e)

            @block.vector
            def _(e):
                run_engine("dve", e)

            @block.gpsimd
            def _(e):
                run_engine("pool", e)

            @block.sync
            def _(e):
                run_engine("sp", e)
        return {e: len(v) for e, v in per_eng.items()}


def build_nc(NSEQ, S, L, last_is_final=True):
    NT = S // 128
    NCH = S // QB
    NKB = NT
    NBA = max(NKB - 4, 1)
    NBC = 2 * NBA + 1
    nc = bass.Bass("TRN2", target_bir_lowering=False)
    dt_in = lambda name, shape: nc.dram_tensor(name, list(shape), F32, kind="ExternalInput").ap()
    x_d = dt_in("x", (NSEQ, S, D))
    win_d = dt_in("w_in", (L, D, DIN))
    wout_d = dt_in("w_out", (L, D, D))
    ng_d = dt_in("ng", (L, 128, 8))
    lq_d = dt_in("lq", (L, 128, 256))
    subg_d = dt_in("subg", (L, 128, 512))
    vng_d = dt_in("vng", (L, 128, 512))
    wsT_d = dt_in("wsT", (L, 128, 512))
    bs_d = dt_in("bs", (L, 128, 4))
    fg_d = dt_in("fg", (128, D))
    ident_d = dt_in("identf", (128, 128))
    mident_d = dt_in("midentf", (128, 512))
    bhi_d = dt_in("bhif", (128, 896))
    blo_d = dt_in("blof", (128, 896))
    bcol_d = dt_in("bcol", (128, NH * NBC))
    ftab_d = dt_in("ftab", (128, NH * 16))
    y_d = nc.dram_tensor("y", [NSEQ, S, D], F32, kind="ExternalOutput").ap()
    R_d = nc.dram_tensor("Rres", [NSEQ, S, D], F32).ap()
    hT_d = nc.dram_tensor("hTs", [NSEQ, NCH, 128, 8 * QB], BF16).ap()

    P = Prog()
    with ExitStack() as st:
        sb = lambda name, shape, dt: st.enter_context(nc.sbuf_tensor(name, list(shape), dt))
        KT = sb("KT", (128, NH, S), BF16)
        VP = sb("VP", (128, NKB, NH, 129), BF16)
        Wkv = sb("Wkv", (128, 8, 1024), BF16)
        Wq = sb("Wq", (128, 8, 512), BF16)
        Wg = sb("Wg", (128, 8, 2048), BF16)
        Wo = sb("Wo", (128, 8, 1024), BF16)
        xt = [sb("xt%d" % i, (128, D), F32) for i in range(2)]
        et = [sb("et%d" % i, (128, 1024), BF16) for i in range(2)]
        hT = sb("hT", (128, 8, QB), BF16)
        QT = sb("QT", (128, NH, QB), BF16)
        Oacc = sb("Oacc", (128, 4, 2, 129), F32)
        G = sb("G", (128, 4, 512), F32)
        cat = sb("cat", (128, 4, 1024), BF16)
        tA = sb("tA", (128, 512), F32)
        tB = sb("tB", (128, 512), F32)
        vn = sb("vn", (128, 512), BF16)
        otmp = sb("otmp", (128, 4, 128), F32)
        scr = sb("scr", (128, 1032), F32)
        ident = sb("ident", (128, 128), BF16)
        mident = sb("mident", (128, NH, 128), BF16)
        Bhi = sb("Bhi", (128, 896), BF16)
        Blo = sb("Blo", (128, 896), BF16)
        bcol = sb("bcol_s", (128, NH * NBC), F32)
        ftab = sb("ftab_s", (128, NH * 16), F32)
        ng = sb("ng_s", (128, 8), F32)
        sublnvec = sb("sublnvec", (128, 512), F32)
        vnormg = sb("vnormg", (128, 512), F32)
        wsT = sb("wsT_s", (128, 512), BF16)
        bsb = sb("bs_s", (128, 4), F32)
        fgs = sb("fg_s", (128, D), F32)
        lqs = tA[:, 0:256]
        lprod = tA[:, 256:384]
        lsum = sb("lsum", (128, 2), F32)
        lexp = sb("lexp", (128, 2), F32)
        neglam = sb("neglam", (128, 1), F32)
        mhalf = sb("mhalf", (128, 8), F32)
        epsc = sb("epsc", (128, 8), F32)
        stat = sb("stat", (128, 64), F32)
        ps = [st.enter_context(nc.psum_tensor("ps%d" % i, [128, 2, 512], F32)) for i in range(4)]

        def bank(b):
            return ps[b // 2][:, b % 2, :]

        def bkey(b):
            return "bank%d" % b

        cnt = [0]

        def load_const(dst_ap, src_ap, key, via=None):
            if via is None:
                P.op("sp", lambda e, d=dst_ap, s=src_ap: e.dma_start(out=d, in_=s), reads=(), writes=(key,), dma="ld_c")
            else:
                stg, stg_key, width = via
                P.op("sp", lambda e, d=stg[:, 0:width], s=src_ap: e.dma_start(out=d, in_=s), reads=(), writes=(stg_key,), dma="ld_" + stg_key)
                P.op("dve", lambda e, d=dst_ap, s=stg[:, 0:width]: e.tensor_copy(out=d, in_=s), reads=(stg_key,), writes=(key,))

        load_const(ident[:], ident_d, "ident", via=(xt[0], "xt0", 128))
        load_const(mident[:].rearrange("p h k -> p (h k)"), mident_d, "mident", via=(xt[1], "xt1", 512))
        load_const(Bhi[:], bhi_d, "Bhi", via=(xt[0], "xt0", 896))
        load_const(Blo[:], blo_d, "Blo", via=(xt[1], "xt1", 896))
        load_const(bcol[:], bcol_d, "bcol")
        load_const(ftab[:], ftab_d, "ftab")
        load_const(fgs[:], fg_d, "fgs")
        P.op("dve", lambda e: e.memset(mhalf[:], -0.5), writes=("mhalf",))
        P.op("dve", lambda e: e.memset(epsc[:], float(EPS)), writes=("epsc",))
        P.op("dve", lambda e: e.memset(VP[:].rearrange("p a b c -> p (a b) c")[:, :, 128:129], 1.0), writes=("VPones",))

        rr = [0]

        def next_bank(pool):
            b = pool[rr[0] % len(pool)]
            rr[0] += 1
            return b

        conv_rr = [0]

        def convert(dst_ap, src_ap, scal_ap, rkeys, wkey):
            eng = ("dve", "act")[conv_rr[0] % 2]
            conv_rr[0] += 1
            if scal_ap is None:
                if eng == "act":
                    P.op("act", lambda e: e.copy(out=dst_ap, in_=src_ap), reads=rkeys, writes=(wkey,))
                else:
                    P.op(eng, lambda e: e.tensor_copy(out=dst_ap, in_=src_ap), reads=rkeys, writes=(wkey,))
            else:
                if eng == "act":
                    P.op("act", lambda e: e.activation(out=dst_ap, in_=src_ap, func=AF.Copy, scale=scal_ap), reads=rkeys + ("ng",), writes=(wkey,))
                else:
                    P.op(eng, lambda e: e.tensor_scalar(out=dst_ap, in0=src_ap, scalar1=scal_ap, scalar2=None, op0=ALU.mult), reads=rkeys + ("ng",), writes=(wkey,))

        stage_rr = [0]

        def layer_setup(l):
            li = lam_init_fn(l)
            P.op("sp", lambda e: e.dma_start(out=ng[:], in_=ng_d[l]), writes=("ng",), dma="ld_c")
            P.op("sp", lambda e: e.dma_start(out=sublnvec[:], in_=subg_d[l]), writes=("sublnvec",), dma="ld_c")
            P.op("sp", lambda e: e.dma_start(out=vnormg[:], in_=vng_d[l]), writes=("vnormg",), dma="ld_c")
            P.op("sp", lambda e: e.dma_start(out=bsb[:], in_=bs_d[l]), writes=("bs",), dma="ld_c")
            P.op("sp", lambda e: e.dma_start(out=tA[:], in_=wsT_d[l]), writes=("tA",), dma="ld_c")
            P.op("dve", lambda e: e.tensor_copy(out=wsT[:], in_=tA[:]), reads=("tA",), writes=("wsT",))
            P.op("dve", lambda e: e.tensor_scalar(out=sublnvec[:], in0=sublnvec[:], scalar1=float((1.0 - li) * 0.5), scalar2=None, op0=ALU.mult),
                 reads=("sublnvec",), writes=("sublnvec",))
            P.op("sp", lambda e: e.dma_start(out=lqs, in_=lq_d[l]), writes=("tA",), dma="ld_c")
            lq4 = lqs.rearrange("p (a b d) -> p a b d", a=2, b=2)
            P.op("dve", lambda e: e.tensor_tensor(out=lprod.rearrange("p (a d) -> p a d", a=2), in0=lq4[:, :, 0, :], in1=lq4[:, :, 1, :], op=ALU.mult),
                 reads=("tA",), writes=("tA",))
            P.op("dve", lambda e: e.reduce_sum(out=lsum[:], in_=lprod.rearrange("p (a d) -> p a d", a=2), axis=AX.X), reads=("tA",), writes=("lsum",))
            P.op("act", lambda e: e.activation(out=lexp[:], in_=lsum[:], func=AF.Exp), reads=("lsum",), writes=("lexp",))
            P.op("dve", lambda e: e.tensor_tensor(out=neglam[:], in0=lexp[:, 1:2], in1=lexp[:, 0:1], op=ALU.subtract), reads=("lexp",), writes=("neglam",))
            P.op("dve", lambda e: e.tensor_scalar(out=neglam[:], in0=neglam[:], scalar1=float(-li), scalar2=None, op0=ALU.add), reads=("neglam",), writes=("neglam",))
            KTs = KT[:].rearrange("p h s -> p (h s)").bitcast(F32)
            NSL = min(8, (NH * S * 2) // 4096)
            skeys = ["KTs%d" % j for j in range(NSL)]
            P.op("dve", lambda e: e.memset(stat[:, 60:61], 0.0), writes=("KT", "fence") + tuple(skeys))
            pieces = [(Wq, 0, "Wq"), (Wkv, 0, "Wkv"), (Wkv, 512, "Wkv"), (Wg, 0, "Wg"), (Wg, 512, "Wg"), (Wg, 1024, "Wg"), (Wg, 1536, "Wg")]
            nq = [0]

            def stage_load(src_ap, wdt):
                j = stage_rr[0] % NSL
                stage_rr[0] += 1
                q = ("sp", "act")[nq[0] % 2]
                nq[0] += 1
                dst = KTs[:, j * 1024:j * 1024 + wdt]
                P.op(q, lambda e: e.dma_start(out=dst, in_=src_ap), writes=(skeys[j],), dma="ld_" + skeys[j])
                return KTs[:, j * 1024:(j + 1) * 1024], skeys[j]

            def conv(dst_ap, src_ap, scal_ap, skey, wkey):
                if scal_ap is None:
                    P.op("dve", lambda e: e.tensor_copy(out=dst_ap, in_=src_ap), reads=(skey,), writes=(wkey,))
                else:
                    P.op("dve", lambda e: e.tensor_scalar(out=dst_ap, in0=src_ap, scalar1=scal_ap, scalar2=None, op0=ALU.mult), reads=(skey, "ng"), writes=(wkey,))

            for k in range(8):
                for b in range(4):
                    c0 = b * 1024
                    wdt = min(1024, DIN - c0)
                    stg, skey = stage_load(win_d[l, k * 128:(k + 1) * 128, c0:c0 + wdt], wdt)
                    for hh in range(wdt // 512):
                        dstT, dcol, dkey = pieces[(c0 // 512) + hh]
                        conv(dstT[:, k, dcol:dcol + 512], stg[:, hh * 512:(hh + 1) * 512], ng[:, k:k + 1], skey, dkey)
                stg, skey = stage_load(wout_d[l, k * 128:(k + 1) * 128, :], 1024)
                conv(Wo[:, k, :], stg, None, skey, "Wo")
            P.op("dve", lambda e: e.memset(stat[:, 61:62], 0.0), reads=tuple(skeys), writes=("KT", "fence2"))

        ALLB = list(range(8))
        xslot = [0]

        def rstd_from_ss(ss_ap, n, inv_n, skey, okey, out_ap):
            P.op("dve", lambda e: e.tensor_scalar(out=out_ap, in0=ss_ap, scalar1=float(inv_n), scalar2=float(EPS), op0=ALU.mult, op1=ALU.add),
                 reads=(skey,), writes=(okey,))
            P.op("pool", lambda e: e.tensor_tensor(out=out_ap, in0=out_ap, in1=mhalf[:, 0:n], op=ALU.pow), reads=(okey, "mhalf"), writes=(okey,))

        def xload(s, t, src):
            sl = t % 2
            xk = "xt%d" % sl
            P.op("sp", lambda e: e.dma_start(out=xt[sl][:], in_=src[s, t * 128:(t + 1) * 128, :]),
                 reads=("R%d_%d" % (s, t),), writes=(xk,), dma="ld_" + xk)

        hT2 = G[:].rearrange("p a b -> p (a b)").bitcast(BF16).rearrange("p (k t) -> p k t", k=8)
        hTb = [hT[:], hT2]
        hTkeys = [("hT",), ("G0", "G1", "G2", "G3")]

        def pass1(s, l):
            src = x_d if l == 0 else R_d

            def build_tile(c, i):
                hbuf = hTb[c % 2]
                hkeys = hTkeys[c % 2]
                t = 4 * c + i
                sl = t % 2
                xk = "xt%d" % sl
                if t < 2:
                    xload(s, t, src)
                ssk = "ss%d" % (i % 2)
                ssa = stat[:, (i % 2):(i % 2) + 1]
                rsa = stat[:, 2 + (i % 2):3 + (i % 2)]
                rsk = "rs%d" % (i % 2)
                P.op("act", lambda e: e.activation(out=scr[:, 0:512], in_=xt[sl][:, 0:512], func=AF.Square, accum_out=ssa),
                     reads=(xk,), writes=("scr0", "scr1", ssk))
                ssa2 = stat[:, 4 + (i % 2):5 + (i % 2)]
                ssk2 = "ssb%d" % (i % 2)
                P.op("act", lambda e: e.activation(out=scr[:, 0:512], in_=xt[sl][:, 512:1024], func=AF.Square, accum_out=ssa2),
                     reads=(xk,), writes=("scr0", "scr1", ssk2))
                P.op("dve", lambda e: e.tensor_tensor(out=ssa, in0=ssa, in1=ssa2, op=ALU.add), reads=(ssk, ssk2), writes=(ssk,))
                rstd_from_ss(ssa, 1, 1.0 / D, ssk, rsk, rsa)

            def hop_tile(c, i):
                t = 4 * c + i
                sl = t % 2
                xk = "xt%d" % sl
                rsa = stat[:, 2 + (i % 2):3 + (i % 2)]
                rsk = "rs%d" % (i % 2)
                hb = et[i % 2]
                hk = "et%d" % (i % 2)
                P.op("dve", lambda e: e.tensor_scalar(out=hb[:], in0=xt[sl][:], scalar1=rsa, scalar2=None, op0=ALU.mult),
                     reads=(xk, rsk), writes=(hk,))
                if t + 2 < NT:
                    xload(s, t + 2, src)

            def trans_tile(c, i):
                hbuf = hTb[c % 2]
                hkeys = hTkeys[c % 2]
                hb = et[i % 2]
                hk = "et%d" % (i % 2)
                b = next_bank(ALLB)
                pT = bank(b).bitcast(BF16)
                for k in range(8):
                    P.op("pe", lambda e, k=k: e.transpose(out=pT[:, k * 128:(k + 1) * 128], in_=hb[:, k * 128:(k + 1) * 128], identity=ident[:]),
                         reads=(hk, "ident"), writes=(bkey(b),))
                P.op("act", lambda e: e.copy(out=hbuf[:, :, i * 128:(i + 1) * 128], in_=pT.rearrange("p (k t) -> p k t", k=8)),
                     reads=(bkey(b),), writes=hkeys)

            def proj_group(c, g):
                hbuf = hTb[c % 2]
                hkeys = hTkeys[c % 2]
                b = next_bank(ALLB)
                if g < 4:
                    h = g
                    for k in range(8):
                        P.op("pe", lambda e, k=k: e.matmul(out=bank(b), lhsT=Wkv[:, k, h * 128:(h + 1) * 128], rhs=hbuf[:, k, :], start=(k == 0), stop=(k == 7)),
                             reads=("Wkv",) + hkeys, writes=(bkey(b),))
                    P.op("act", lambda e: e.copy(out=KT[:, h, c * QB:(c + 1) * QB], in_=bank(b)), reads=(bkey(b),), writes=("KT",))
                else:
                    i = g - 4
                    for k in range(8):
                        P.op("pe", lambda e, k=k: e.matmul(out=bank(b), lhsT=hbuf[:, k, i * 128:(i + 1) * 128], rhs=Wkv[:, k, 512:1024], start=(k == 0), stop=(k == 7)),
                             reads=("Wkv",) + hkeys, writes=(bkey(b),))
                    P.op("dve", lambda e: e.tensor_copy(out=VP[:, 4 * c + i, :, 0:128], in_=bank(b).rearrange("p (h d) -> p h d", h=NH)),
                         reads=(bkey(b),), writes=("VP",))

            def tile_ci(t):
                return (t // 4, t % 4)

            build_tile(0, 0)
            hop_tile(0, 0)
            build_tile(0, 1)
            hop_tile(0, 1)
            for t in range(4):
                if t + 2 < NT:
                    build_tile(*tile_ci(t + 2))
                trans_tile(0, t)
                if t + 2 < NT:
                    hop_tile(*tile_ci(t + 2))
            for c in range(NCH):
                for i in range(4):
                    proj_group(c, 2 * i)
                    proj_group(c, 2 * i + 1)
                    if c + 1 < NCH:
                        t = 4 * (c + 1) + i
                        if t + 2 < NT:
                            build_tile(*tile_ci(t + 2))
                        trans_tile(c + 1, i)
                        if t + 2 < NT:
                            hop_tile(*tile_ci(t + 2))
                hbuf = hTb[c % 2]
                P.op("pool", lambda e, c=c, hbuf=hbuf: e.dma_start(out=hT_d[s, c], in_=hbuf.rearrange("p k t -> p (k t)")), reads=hTkeys[c % 2], writes=("hTd%d_%d" % (s, c),), dma="st_hT")

        MISC = [7, 0, 1, 2, 3]
        sbuf_tog = [0]
        et_tog = [0]
        et3_tog = [0]
        ph_par = [0]

        def pass2(s, l, is_last):
            src = x_d if l == 0 else R_d
            xload(s, 0, src)
            xload(s, 1, src)
            P.op("sp", lambda e: e.dma_start(out=hT[:].rearrange("p k t -> p (k t)"), in_=hT_d[s, 0]), reads=("hTd%d_%d" % (s, 0),), writes=("hT",), dma="ld_hT")
            def qproj():
                for h in range(NH):
                    b = next_bank(MISC)
                    for k in range(8):
                        P.op("pe", lambda e, b=b, k=k, h=h: e.matmul(out=bank(b), lhsT=Wq[:, k, h * 128:(h + 1) * 128], rhs=hT[:, k, :], start=(k == 0), stop=(k == 7)),
                             reads=("Wq", "hT"), writes=(bkey(b),))
                    P.op("act", lambda e, b=b, h=h: e.activation(out=QT[:, h, :], in_=bank(b), func=AF.Copy, scale=0.125), reads=(bkey(b),), writes=("QT",))

            qproj()
            for c in range(NCH):
                side_tasks = []

                def mm_actions(i, c0):
                    tok = slice(i * 128, (i + 1) * 128)
                    acts = []
                    for k in range(8):
                        acts.append(lambda k=k, tok=tok, c0=c0: P.op("pe", lambda e: e.matmul(out=bank(7), lhsT=hT[:, k, tok], rhs=Wg[:, k, c0:c0 + 512], start=(k == 0), stop=(k == 7)),
                                                                     reads=("Wg", "hT"), writes=(bkey(7),)))
                    return acts

                def task_gate(i):
                    a0 = lambda: P.op("act", lambda e: e.copy(out=tA[:], in_=bank(7)), reads=(bkey(7),), writes=("tA",))
                    a1 = lambda: P.op("act", lambda e: e.activation(out=G[:, i, :], in_=tA[:], func=AF.Tanh, scale=0.5), reads=("tA",), writes=("G%d" % i,))
                    d1 = lambda: P.op("dve", lambda e: e.scalar_tensor_tensor(out=G[:, i, :], in0=G[:, i, :], scalar=1.0, in1=tA[:], op0=ALU.add, op1=ALU.mult),
                                      reads=("tA", "G%d" % i), writes=("G%d" % i,))
                    d2 = lambda: P.op("pool", lambda e: e.tensor_tensor(out=G[:, i, :], in0=G[:, i, :], in1=sublnvec[:], op=ALU.mult), reads=("G%d" % i, "sublnvec"), writes=("G%d" % i,))
                    return (mm_actions(i, 0), [[a0, a1], [d1, d2]])

                def task_gm(i):
                    a0 = lambda: P.op("act", lambda e: e.copy(out=tB[:], in_=bank(7)), reads=(bkey(7),), writes=("tB",))
                    a1 = lambda: P.op("act", lambda e: e.activation(out=tA[:], in_=tB[:], func=AF.Tanh, scale=0.5), reads=("tB",), writes=("tA",))
                    d1 = lambda: P.op("dve", lambda e: e.scalar_tensor_tensor(out=tB[:], in0=tA[:], scalar=1.0, in1=tB[:], op0=ALU.add, op1=ALU.mult),
                                      reads=("tA", "tB"), writes=("tB",))
                    return (mm_actions(i, 1536), [[a0, a1], [d1]])

                def task_u(i):
                    a0 = lambda: P.op("act", lambda e: e.activation(out=tA[:], in_=bank(7), func=AF.Copy, scale=0.5), reads=(bkey(7),), writes=("tA",))
                    d1 = lambda: P.op("dve", lambda e: e.tensor_tensor(out=tB[:], in0=tB[:], in1=tA[:], op=ALU.mult), reads=("tA", "tB"), writes=("tB",))
                    return (mm_actions(i, 512), [[a0], [d1]])

                def task_vg(i):
                    ssv = stat[:, 8:9]
                    rsv = stat[:, 9:10]
                    a0 = lambda: P.op("act", lambda e: e.copy(out=tA[:], in_=bank(7)), reads=(bkey(7),), writes=("tA",))
                    a1 = lambda: P.op("act", lambda e: e.activation(out=vn[:], in_=tA[:], func=AF.Square, scale=float(512.0 ** -0.5), accum_out=ssv), reads=("tA",), writes=("vn", "ssv"))

                    def p1():
                        P.op("pool", lambda e: e.tensor_tensor(out=rsv, in0=ssv, in1=epsc[:, 0:1], op=ALU.add), reads=("ssv", "epsc"), writes=("rsv",))
                        P.op("pool", lambda e: e.tensor_tensor(out=rsv, in0=rsv, in1=mhalf[:, 0:1], op=ALU.pow), reads=("rsv", "mhalf"), writes=("rsv",))
                        P.op("pool", lambda e: e.tensor_tensor(out=tA[:], in0=tA[:], in1=rsv.to_broadcast([128, 512]), op=ALU.mult), reads=("tA", "rsv"), writes=("tA",))
                        P.op("pool", lambda e: e.tensor_tensor(out=vn[:], in0=tA[:], in1=vnormg[:], op=ALU.mult), reads=("tA", "vnormg"), writes=("vn",))
                    return (mm_actions(i, 1024), [[a0, a1], [p1], [], []])

                def task_sv(i):
                    mm = []
                    for g in range(4):
                        mm.append(lambda g=g: P.op("pe", lambda e: e.matmul(out=bank(7)[:, g * 128:(g + 1) * 128], lhsT=wsT[:, g * 128:(g + 1) * 128], rhs=vn[:, g * 128:(g + 1) * 128],
                                                                            start=True, stop=True, skip_group_check=True),
                                                   reads=("wsT", "vn"), writes=(bkey(7),)))
                    a0 = lambda: P.op("act", lambda e: e.copy(out=tA[:], in_=bank(7)), reads=(bkey(7),), writes=("tA",))
                    d1 = lambda: P.op("dve", lambda e: e.tensor_tensor(out=tA[:].rearrange("p (g d) -> p g d", g=4), in0=tA[:].rearrange("p (g d) -> p g d", g=4),
                                                                        in1=bsb[:, 0:4].unsqueeze(2).to_broadcast([128, 4, 128]), op=ALU.add),
                                      reads=("tA", "bs"), writes=("tA",))
                    d2 = lambda: P.op("dve", lambda e: e.tensor_tensor(out=cat[:, i, 512:1024], in0=tA[:], in1=tB[:], op=ALU.mult), reads=("tA", "tB"), writes=("cat%d" % i,))
                    return (mm, [[a0], [d1, d2]])

                def pgroup(i, c0):
                    b_ = next_bank(MISC)
                    tok = slice(i * 128, (i + 1) * 128)
                    for k in range(8):
                        P.op("pe", lambda e, k=k: e.matmul(out=bank(b_), lhsT=hT[:, k, tok], rhs=Wg[:, k, c0:c0 + 512], start=(k == 0), stop=(k == 7)),
                             reads=("Wg", "hT"), writes=(bkey(b_),))
                    return b_

                for i in range(4):
                    ssv = stat[:, 8 + 2 * (i % 2):9 + 2 * (i % 2)]
                    rsv = stat[:, 9 + 2 * (i % 2):10 + 2 * (i % 2)]
                    ssvk = "ssv%d" % (i % 2)
                    rsvk = "rsv%d" % (i % 2)
                    tG = tA if i % 2 == 0 else otmp[:].rearrange("p a b -> p (a b)")
                    tGk = "tA" if i % 2 == 0 else "otmp"
                    tGa = tA[:] if i % 2 == 0 else otmp[:].rearrange("p a b -> p (a b)")
                    bv = pgroup(i, 1024)
                    P.op("act", lambda e, bv=bv, ssv=ssv: e.activation(out=scr[:, 512:1024], in_=bank(bv), func=AF.Square, scale=float(512.0 ** -0.5), accum_out=ssv),
                         reads=(bkey(bv),), writes=("scr1", "scr2", ssvk))
                    P.op("pool", lambda e, ssv=ssv, rsv=rsv: e.tensor_tensor(out=rsv, in0=ssv, in1=epsc[:, 0:1], op=ALU.add), reads=(ssvk, "epsc"), writes=(rsvk,))
                    P.op("pool", lambda e, rsv=rsv: e.tensor_tensor(out=rsv, in0=rsv, in1=mhalf[:, 0:1], op=ALU.pow), reads=(rsvk, "mhalf"), writes=(rsvk,))
                    bg = pgroup(i, 0)
                    P.op("act", lambda e, bg=bg, tGa=tGa: e.activation(out=tGa, in_=bank(bg), func=AF.Tanh, scale=0.5), reads=(bkey(bg),), writes=(tGk,))
                    P.op("dve", lambda e, bv=bv, rsv=rsv: e.scalar_tensor_tensor(out=vn[:], in0=bank(bv), scalar=rsv, in1=vnormg[:], op0=ALU.mult, op1=ALU.mult),
                         reads=(bkey(bv), rsvk, "vnormg"), writes=("vn",))
                    P.op("dve", lambda e, bg=bg, i=i, tGa=tGa: e.scalar_tensor_tensor(out=G[:, i, :], in0=tGa, scalar=1.0, in1=bank(bg), op0=ALU.add, op1=ALU.mult),
                         reads=(tGk, bkey(bg)), writes=("G%d" % i,))
                    P.op("pool", lambda e, i=i: e.tensor_tensor(out=G[:, i, :], in0=G[:, i, :], in1=sublnvec[:], op=ALU.mult), reads=("G%d" % i, "sublnvec"), writes=("G%d" % i,))
                    bgm = pgroup(i, 1536)
                    P.op("act", lambda e, bgm=bgm: e.activation(out=tB[:], in_=bank(bgm), func=AF.Tanh, scale=0.5), reads=(bkey(bgm),), writes=("tB",))
                    P.op("dve", lambda e, bgm=bgm: e.scalar_tensor_tensor(out=tB[:], in0=tB[:], scalar=1.0, in1=bank(bgm), op0=ALU.add, op1=ALU.mult),
                         reads=("tB", bkey(bgm)), writes=("tB",))
                    bu = pgroup(i, 512)
                    P.op("dve", lambda e, bu=bu: e.scalar_tensor_tensor(out=tB[:], in0=tB[:], scalar=0.5, in1=bank(bu), op0=ALU.mult, op1=ALU.mult),
                         reads=("tB", bkey(bu)), writes=("tB",))
                    bsv = next_bank(MISC)
                    for g in range(4):
                        P.op("pe", lambda e, bsv=bsv, g=g: e.matmul(out=bank(bsv)[:, g * 128:(g + 1) * 128], lhsT=wsT[:, g * 128:(g + 1) * 128], rhs=vn[:, g * 128:(g + 1) * 128],
                                                                    start=True, stop=True, skip_group_check=True),
                             reads=("wsT", "vn"), writes=(bkey(bsv),))
                    for g in range(4):
                        P.op("dve", lambda e, bsv=bsv, g=g, i=i: e.scalar_tensor_tensor(out=cat[:, i, 512 + g * 128:512 + (g + 1) * 128], in0=bank(bsv)[:, g * 128:(g + 1) * 128],
                                                                                     scalar=bsb[:, g:g + 1], in1=tB[:, g * 128:(g + 1) * 128], op0=ALU.add, op1=ALU.mult),
                             reads=(bkey(bsv), "bs", "tB"), writes=("cat%d" % i,))
                if c + 1 < NCH:
                    P.op("sp", lambda e, c=c: e.dma_start(out=hT[:].rearrange("p k t -> p (k t)"), in_=hT_d[s, c + 1]), reads=("hTd%d_%d" % (s, c + 1),), writes=("hT",), dma="ld_hT")

                def gbank(g, par):
                    return (4 if par == 0 else 7) if g == 0 else 4 + g

                def acc_ap(idx, par):
                    bnk = gbank(idx // 3, par)
                    col = (idx % 3) * 129
                    return bank(bnk)[:, col:col + 129], bnk

                steps = []
                for h in HEAD_ORDER:
                    lst = []
                    for kb in range(NKB):
                        if kb < 4 * c:
                            if 4 * c - kb <= KEEP[h]:
                                lst.append((kb, "b"))
                        elif kb < 4 * c + 4:
                            lst.append((kb, "i"))
                        else:
                            if kb - 4 * c - 3 <= KEEP[h]:
                                lst.append((kb, "a"))
                    lst = [x for x in lst if x[1] == "a"] + [x for x in lst if x[1] == "b"] + [x for x in lst if x[1] == "i"]
                    for n_, (kb, k) in enumerate(lst):
                        first = (n_ == 0) or (lst[n_ - 1][1] != k)
                        last = (n_ == len(lst) - 1) or (lst[n_ + 1][1] != k)
                        steps.append((h, kb, k, first, last, n_ == len(lst) - 1, first and n_ == 0))

                def emit_qk(h, kb, k):
                    X = sbuf_tog[0] % 2
                    sbuf_tog[0] += 1
                    inchunk = k == "i"
                    for m in range(2):
                        pr = slice(64 * m, 64 * m + 64)
                        P.op("pe", lambda e, X=X, m=m, pr=pr, h=h, kb=kb, inchunk=inchunk: e.matmul(out=ps[X][:, m, :], lhsT=KT[pr, h, kb * 128:(kb + 1) * 128], rhs=QT[pr, h, :],
                                                                                                  start=True, stop=(not inchunk), skip_group_check=True),
                             reads=("KT", "QT"), writes=(bkey(2 * X + m),))
                    if inchunk:
                        j = kb - 4 * c
                        off = 384 - 128 * j
                        c0, c1 = [(256, 512), (384, 512), (0, 128), (0, 256)][j]
                        for m in range(2):
                            P.op("pe", lambda e, X=X, m=m, h=h, off=off: e.matmul(out=ps[X][:, m, :], lhsT=mident[:, h, :], rhs=Bhi[:, off:off + 512], start=False, stop=False, skip_group_check=True),
                                 reads=("mident", "Bhi"), writes=(bkey(2 * X + m),))
                            P.op("pe", lambda e, X=X, m=m, h=h, off=off, c0=c0, c1=c1: e.matmul(out=ps[X][:, m, c0:c1], lhsT=mident[:, h, :], rhs=Blo[:, off + c0:off + c1], start=False, stop=True, skip_group_check=True),
                                 reads=("mident", "Blo"), writes=(bkey(2 * X + m),))
                    return X

                ETA = [et[0][:], et[1][:], tB[:].bitcast(BF16)]
                ETK = ["et0", "et1", "tB"]

                def emit_act(h, kb, k, X):
                    E = et3_tog[0] % 3
                    et3_tog[0] += 1
                    if k == "b":
                        idx = (4 * c - kb) - 1
                    elif k == "a":
                        idx = NBA + (kb - 4 * c - 4)
                    else:
                        idx = 2 * NBA
                    col = h * NBC + idx
                    P.op("act", lambda e, X=X, E=E, col=col: e.activation(out=ETA[E], in_=ps[X][:].rearrange("p a b -> p (a b)"), func=AF.Exp, bias=bcol[:, col:col + 1], scale=1.0),
                         reads=(bkey(2 * X), bkey(2 * X + 1), "bcol"), writes=(ETK[E],))
                    return E

                def emit_pv(h, kb, E, first, par):
                    for idx in range(8):
                        i, m = idx // 2, idx % 2
                        ap, bnk = acc_ap(idx, par)
                        stflag = first and (idx % 3 == 0)
                        P.op("pe", lambda e, ap=ap, E=E, m=m, i=i, kb=kb, h=h, stflag=stflag: e.matmul(out=ap, lhsT=ETA[E][:, m * 512 + i * 128:m * 512 + (i + 1) * 128], rhs=VP[:, kb, h, :],
                                                                                                     start=stflag, stop=False, skip_group_check=True),
                             reads=(ETK[E], "VP", "VPones"), writes=(bkey(bnk),))

                Oav = Oacc[:].rearrange("p i m e -> p (i m) e")
                scrv = scr[:, 0:1032].rearrange("p (a e) -> p a e", e=129)

                def emit_phase_evac(h, k, firstphase, par):
                    for g, (a0, a1) in ((1, (3, 6)), (2, (6, 8)), (0, (0, 3))):
                        bnk = gbank(g, par)
                        n_ = a1 - a0
                        src_ = bank(bnk)[:, 0:n_ * 129].rearrange("p (a e) -> p a e", e=129)
                        dst = Oav[:, a0:a1, :]
                        if k == "i":
                            if firstphase:
                                P.op("dve", lambda e, src_=src_, dst=dst: e.tensor_copy(out=dst, in_=src_), reads=(bkey(bnk),), writes=("Oacc%d" % g,))
                            else:
                                P.op("dve", lambda e, src_=src_, dst=dst: e.tensor_tensor(out=dst, in0=src_, in1=dst, op=ALU.add), reads=(bkey(bnk), "Oacc%d" % g), writes=("Oacc%d" % g,))
                        else:
                            f0 = (h * 2 + (0 if k == "b" else 1)) * 8 + a0
                            fb = ftab[:, f0:f0 + n_].unsqueeze(2).to_broadcast([128, n_, 129])
                            if firstphase:
                                P.op("dve", lambda e, src_=src_, dst=dst, fb=fb: e.tensor_tensor(out=dst, in0=src_, in1=fb, op=ALU.mult),
                                     reads=(bkey(bnk), "ftab"), writes=("Oacc%d" % g,))
                            else:
                                P.op("dve", lambda e, src_=src_, a0=a0, a1=a1, fb=fb: e.tensor_tensor(out=scrv[:, a0:a1, :], in0=src_, in1=fb, op=ALU.mult),
                                     reads=(bkey(bnk), "ftab"), writes=("scr%d" % g,))
                    if k != "i" and not firstphase:
                        for g, (a0, a1) in enumerate([(0, 3), (3, 6), (6, 8)]):
                            P.op("dve", lambda e, a0=a0, a1=a1: e.tensor_tensor(out=Oav[:, a0:a1, :], in0=Oav[:, a0:a1, :], in1=scrv[:, a0:a1, :], op=ALU.add),
                                 reads=("Oacc%d" % g, "scr%d" % g), writes=("Oacc%d" % g,))

                rr_ = stat[:, 16:24].rearrange("p (i m) -> p i m", i=4)
                c1 = stat[:, 24:28]
                ssq = stat[:, 28:32]
                rso = stat[:, 32:36]
                scrA = tA[:].rearrange("p (a e) -> p a e", e=128)
                scrB = tA[:].rearrange("p (a e) -> p a e", e=128)

                def combine_stages(h):
                    def s0():
                        P.op("dve", lambda e: e.reciprocal(out=rr_, in_=Oacc[:, :, :, 128]), reads=("Oacc0", "Oacc1", "Oacc2",), writes=("rr",))
                        P.op("dve", lambda e: e.tensor_scalar(out=c1, in0=rr_[:, :, 1], scalar1=neglam[:, 0:1], scalar2=None, op0=ALU.mult), reads=("rr", "neglam"), writes=("c1",))
                        P.op("dve", lambda e: e.tensor_tensor(out=otmp[:], in0=Oacc[:, :, 0, 0:128], in1=rr_[:, :, 0:1].to_broadcast([128, 4, 128]), op=ALU.mult),
                             reads=("Oacc0", "Oacc1", "Oacc2", "rr"), writes=("otmp",))
                        P.op("dve", lambda e: e.tensor_tensor(out=scrB, in0=Oacc[:, :, 1, 0:128], in1=c1.unsqueeze(2).to_broadcast([128, 4, 128]), op=ALU.mult),
                             reads=("Oacc0", "Oacc1", "Oacc2", "c1"), writes=("tA",))

                    def s1():
                        P.op("dve", lambda e: e.tensor_tensor(out=otmp[:], in0=otmp[:], in1=scrB, op=ALU.add), reads=("otmp", "tA"), writes=("otmp",))
                        P.op("pool", lambda e: e.tensor_tensor(out=scrA, in0=otmp[:], in1=otmp[:], op=ALU.mult), reads=("otmp",), writes=("tA",))

                    def s2():
                        P.op("dve", lambda e: e.reduce_sum(out=ssq, in_=scrA, axis=AX.X), reads=("tA",), writes=("ssq",))
                        rstd_from_ss(ssq, 4, 1.0 / 128, "ssq", "rso", rso)

                    def s3():
                        P.op("dve", lambda e: e.tensor_tensor(out=otmp[:], in0=otmp[:], in1=rso.unsqueeze(2).to_broadcast([128, 4, 128]), op=ALU.mult),
                             reads=("otmp", "rso"), writes=("otmp",))
                        P.op("dve", lambda e: e.tensor_tensor(out=cat[:, :, h * 128:(h + 1) * 128], in0=otmp[:], in1=G[:, :, h * 128:(h + 1) * 128], op=ALU.mult),
                             reads=("otmp", "G0", "G1", "G2", "G3"), writes=("cat0", "cat1", "cat2", "cat3"))
                    return [s0, s1, None, s2, None, s3]

                nst = len(steps)

                def make_sched(mps):
                    sched = {}
                    pos = 0
                    prev_last = -1
                    for (mm, posts) in side_tasks:
                        if len(mm) <= 4:
                            pos = max(pos, prev_last + 1)
                        nmm = (len(mm) + mps - 1) // mps if len(mm) > 4 else 1
                        for q in range(nmm):
                            chunk_ = mm[q * mps:(q + 1) * mps] if len(mm) > 4 else mm
                            sched.setdefault(pos + q, []).extend(chunk_)
                        pstart = max(pos + nmm, prev_last)
                        for q, pa in enumerate(posts):
                            sched.setdefault(pstart + q, []).extend(pa)
                        prev_last = pstart + len(posts) - 1
                        pos = max(pos + nmm + 1, pstart + 1)
                    return sched, prev_last + 2

                for mps in (2, 3, 4, 8):
                    sched, need = make_sched(mps)
                    if need <= nst - 1:
                        break
                cq = []
                Xs = {}
                for q_ in range(min(2, nst)):
                    Xs[q_] = emit_qk(steps[q_][0], steps[q_][1], steps[q_][2])
                nphase_done = {}
                for si, (h, kb, k, first, last, lasthead, firsthead) in enumerate(steps):
                    X = Xs.pop(si)
                    E = emit_act(h, kb, k, X)
                    if si + 2 < nst:
                        Xs[si + 2] = emit_qk(steps[si + 2][0], steps[si + 2][1], steps[si + 2][2])
                    if first:
                        ph_par[0] ^= 1
                    emit_pv(h, kb, E, first, ph_par[0])
                    if last:
                        firstphase = h not in nphase_done
                        nphase_done[h] = True
                        if cq and not lasthead:
                            f_ = cq.pop(0)
                            if f_ is not None:
                                f_()
                        emit_phase_evac(h, k, firstphase, ph_par[0])
                        if lasthead:
                            while cq:
                                f_ = cq.pop(0)
                                if f_ is not None:
                                    f_()
                            st_ = combine_stages(h)
                            st_[0]()
                            cq = st_[1:]
                    elif cq:
                        f_ = cq.pop(0)
                        if f_ is not None:
                            f_()
                    for fn_ in sched.pop(si - 1, ()):
                        fn_()
                for key_ in sorted(sched.keys()):
                    for fn_ in sched[key_]:
                        fn_()
                while cq:
                    f_ = cq.pop(0)
                    if f_ is not None:
                        f_()
                if c + 1 < NCH:
                    qproj()
                tinfo = {}

                def tail_T(i):
                    b = next_bank(MISC)
                    pT = bank(b).bitcast(BF16)
                    E = et_tog[0] % 2
                    et_tog[0] += 1
                    ek = "et%d" % E
                    for k in range(8):
                        P.op("pe", lambda e, pT=pT, i=i, k=k: e.transpose(out=pT[:, k * 128:(k + 1) * 128], in_=cat[:, i, k * 128:(k + 1) * 128], identity=ident[:]),
                             reads=("cat%d" % i, "ident"), writes=(bkey(b),))
                    P.op("dve", lambda e, pT=pT, E=E: e.tensor_copy(out=et[E][:], in_=pT), reads=(bkey(b),), writes=(ek,))
                    tinfo[i] = (E, ek)

                def tail_M(i):
                    E, ek = tinfo[i]
                    t = 4 * c + i
                    sl = t % 2
                    xk = "xt%d" % sl
                    for half in range(2):
                        bo = next_bank(MISC)
                        for k in range(8):
                            P.op("pe", lambda e, bo=bo, E=E, k=k, half=half: e.matmul(out=bank(bo), lhsT=et[E][:, k * 128:(k + 1) * 128], rhs=Wo[:, k, half * 512:(half + 1) * 512],
                                                                                    start=(k == 0), stop=(k == 7)),
                                 reads=(ek, "Wo"), writes=(bkey(bo),))
                        P.op("dve", lambda e, bo=bo, sl=sl, half=half: e.tensor_tensor(out=xt[sl][:, half * 512:(half + 1) * 512], in0=bank(bo), in1=xt[sl][:, half * 512:(half + 1) * 512], op=ALU.add),
                             reads=(bkey(bo), xk), writes=(xk,))
                    if is_last:
                        ssf = stat[:, 40:41]
                        ssf2 = stat[:, 41:42]
                        rsf = stat[:, 42:43]
                        P.op("act", lambda e, sl=sl: e.activation(out=tA[:], in_=xt[sl][:, 0:512], func=AF.Square, accum_out=ssf), reads=(xk,), writes=("tA", "ssf"))
                        P.op("act", lambda e, sl=sl: e.activation(out=tA[:], in_=xt[sl][:, 512:1024], func=AF.Square, accum_out=ssf2), reads=(xk,), writes=("tA", "ssf2"))
                        P.op("dve", lambda e: e.tensor_tensor(out=ssf, in0=ssf, in1=ssf2, op=ALU.add), reads=("ssf", "ssf2"), writes=("ssf",))
                        rstd_from_ss(ssf, 1, 1.0 / D, "ssf", "rsf", rsf)
                        P.op("dve", lambda e, sl=sl: e.scalar_tensor_tensor(out=xt[sl][:], in0=xt[sl][:], scalar=rsf, in1=fgs[:], op0=ALU.mult, op1=ALU.mult),
                             reads=(xk, "rsf", "fgs"), writes=(xk,))
                        P.op("pool", lambda e, sl=sl, t=t: e.dma_start(out=y_d[s, t * 128:(t + 1) * 128, :], in_=xt[sl][:]), reads=(xk,), writes=("y%d_%d" % (s, t),), dma="st_" + xk)
                    else:
                        P.op("pool", lambda e, sl=sl, t=t: e.dma_start(out=R_d[s, t * 128:(t + 1) * 128, :], in_=xt[sl][:]), reads=(xk,), writes=("R%d_%d" % (s, t),), dma="st_" + xk)
                    if t + 2 < NT:
                        xload(s, t + 2, src)

                tail_T(0)
                tail_T(1)
                tail_M(0)
                tail_T(2)
                tail_M(1)
                tail_T(3)
                tail_M(2)
                tail_M(3)

        for l in range(L):
            layer_setup(l)
            for s in range(NSEQ):
                pass1(s, l)
                pass2(s, l, (l == L - 1) and last_is_final)
        counts = P.build(nc, st)
    return nc, counts


def make_consts(S):
    NKB = S // 128
    NBA = max(NKB - 4, 1)
    NBC = 2 * NBA + 1
    p = np.arange(128, dtype=np.float64)
    ident = np.eye(128, dtype=np.float32)
    mident = np.zeros((128, NH, 128), np.float32)
    for h in range(NH):
        mident[:, h, :] = np.eye(128) * SLOPES[h]
    y = np.arange(896, dtype=np.float64)
    d = np.abs(y[None, :] - 384.0 - p[:, None])
    bhi = (-np.minimum(d, 256.0)).astype(np.float32)
    blo = (-np.maximum(d - 256.0, 0.0)).astype(np.float32)
    bcol = np.zeros((128, NH, NBC), np.float64)
    for h in range(NH):
        m = SLOPES[h]
        for dl in range(1, NBA + 1):
            bcol[:, h, dl - 1] = -m * (128.0 * dl - p)
        for dl in range(0, NBA):
            bcol[:, h, NBA + dl] = -m * (128.0 * dl + p + 1.0)
    ftab = np.zeros((128, NH, 2, 8), np.float64)
    for h in range(NH):
        m = SLOPES[h]
        for i in range(4):
            for mm in range(2):
                ftab[:, h, 0, 2 * i + mm] = np.exp(-m * (128.0 * i + p))
                ftab[:, h, 1, 2 * i + mm] = np.exp(-m * (511.0 - 128.0 * i - p))
    return dict(identf=ident, midentf=mident.reshape(128, NH * 128), bhif=bhi, blof=blo,
                bcol=bcol.reshape(128, NH * NBC).astype(np.float32), ftab=ftab.reshape(128, NH * 16).astype(np.float32))


def make_params(norm_g, w_in, lambda_qk, subln_g, vnorm_g, w_s, b_s, w_out, final_g):
    L = w_in.shape[0]
    f = np.float32
    c = np.ascontiguousarray
    d = {}
    d["w_in"] = c(w_in, dtype=f)
    d["w_out"] = c(w_out, dtype=f)
    d["ng"] = c(np.asarray(norm_g, f).reshape(L, 8, 128).transpose(0, 2, 1))
    d["lq"] = c(np.broadcast_to(np.asarray(lambda_qk, f).reshape(L, 1, 256), (L, 128, 256)))
    d["subg"] = c(np.broadcast_to(np.tile(np.asarray(subln_g, f), (1, 4)).reshape(L, 1, 512), (L, 128, 512)))
    d["vng"] = c(np.broadcast_to(np.asarray(vnorm_g, f).reshape(L, 1, 512), (L, 128, 512)))
    d["wsT"] = c(np.asarray(w_s, f).transpose(0, 3, 1, 2).reshape(L, 128, 512))
    d["bs"] = c(np.asarray(b_s, f).transpose(0, 2, 1))
    d["fg"] = c(np.broadcast_to(np.asarray(final_g, f).reshape(1, D), (128, D)))
    return d


_CACHE = {}


def run(xs_per_core, params, S, L, n_cores, trace=False):
    NSEQ = xs_per_core[0].shape[0]
    key = (NSEQ, S, L)
    if key not in _CACHE:
        _CACHE[key] = build_nc(NSEQ, S, L)
    nc, counts = _CACHE[key]
    consts = make_consts(S)
    in_maps = []
    for ci in range(n_cores):
        m = {"x": np.ascontiguousarray(xs_per_core[ci], dtype=np.float32)}
        m.update(params)
        m.update(consts)
        in_maps.append(m)
    res = run_bass_kernel_spmd(nc, in_maps, core_ids=list(range(n_cores)))
    return [r["y"] for r in res.results]


def kernel(x_prompt, x_sample, norm_g, w_in, lambda_qk, subln_g, vnorm_g, w_s, b_s, w_out, final_g):
    x_prompt = np.asarray(x_prompt, np.float32)
    x_sample = np.asarray(x_sample, np.float32)
    S = x_prompt.shape[1]
    L = np.asarray(w_in).shape[0]
    allx = np.concatenate([x_prompt, x_sample], axis=0)
    n_cores = 8
    per = allx.shape[0] // n_cores
    xs = [allx[i * per:(i + 1) * per] for i in range(n_cores)]
    params = make_params(np.asarray(norm_g), np.asarray(w_in), np.asarray(lambda_qk), np.asarray(subln_g), np.asarray(vnorm_g),
                         np.asarray(w_s), np.asarray(b_s), np.asarray(w_out), np.asarray(final_g))
    ys = run(xs, params, S, L, n_cores)
    ally = np.concatenate(ys, axis=0)
    nb = x_prompt.shape[0]
    return (np.ascontiguousarray(ally[:nb]), np.ascontiguousarray(ally[nb:]))
```
